# Optimizing a Trainium2 kernel written in Bass

```python
import math
import jax, jax.numpy as jnp
from jax import lax
import numpy as np

D_MODEL = 1024
BATCH = 8
SEQ = 2048
DEPTH = 1

W_M = D_MODEL // 2
H_M = 4
DH_M = W_M // H_M
CONV_QK = 4
CHUNK = 64
W_D = D_MODEL - W_M
H_D = 4
DV_D = W_D // H_D
DQK_D = DV_D // 2
W_MIX = W_M + W_D
D_FF = ((8 * D_MODEL) // 3 + 127) // 128 * 128
CONV_FFN = 3
ROPE_THETA = 10000.0
Q_BLOCK = 128
EPS = 1e-6
IN_SIZES = (W_M, W_M, W_M, W_M, H_M, H_M, 2 * W_D // 2 * 1, 2 * W_D // 2 * 1, W_D)
IN_COLS = sum(IN_SIZES)

kernel_name = 'hybrid_mlstm_diffattn_convffn_adaln'


def rmsnorm(x, g):
    xf = x.astype(jnp.float32)
    y = xf * lax.rsqrt(jnp.mean(xf * xf, axis=-1, keepdims=True) + EPS)
    return (y * g.astype(jnp.float32)).astype(x.dtype)


def modulate(h, shift, scale):
    return h * (1 + scale[:, None, :]) + shift[:, None, :]


def causal_dwconv(x, w, b):
    K = w.shape[0]
    y = lax.conv_general_dilated(x, w[:, None, :].astype(x.dtype), window_strides=(1,),
                                 padding=[(K - 1, 0)], dimension_numbers=('NWC', 'WIO', 'NWC'),
                                 feature_group_count=x.shape[-1])
    return y + b.astype(x.dtype)


def rope(x, cos, sin):
    half = x.shape[-1] // 2
    xf = x.astype(jnp.float32)
    x1, x2 = xf[..., :half], xf[..., half:]
    return jnp.concatenate([x1 * cos - x2 * sin, x2 * cos + x1 * sin], axis=-1).astype(x.dtype)


def mlstm_chunkwise(q, k, v, i_pre, f_pre):
    B, S, H, D = q.shape
    nc = S // CHUNK
    def to_chunks(t):
        return t.astype(jnp.float32).transpose(0, 2, 1, 3).reshape(B, H, nc, CHUNK, D).transpose(2, 0, 1, 3, 4)
    def gate_chunks(g):
        return g.astype(jnp.float32).transpose(0, 2, 1).reshape(B, H, nc, CHUNK).transpose(2, 0, 1, 3)
    qc, kc, vc = to_chunks(q), to_chunks(k * (D ** -0.5)), to_chunks(v)
    ic = gate_chunks(i_pre)
    fc = gate_chunks(jax.nn.log_sigmoid(f_pre.astype(jnp.float32)))
    tri = jnp.tril(jnp.ones((CHUNK, CHUNK), dtype=bool))

    def step(carry, inp):
        C, n, m = carry
        qb, kb, vb, ig, lf = inp
        b = jnp.cumsum(lf, axis=-1)
        dmat = jnp.where(tri, b[..., :, None] - b[..., None, :] + ig[..., None, :], -jnp.inf)
        inter = b + m[..., None]
        m_t = jnp.maximum(inter, jnp.max(dmat, axis=-1))
        w_intra = jnp.exp(dmat - m_t[..., None]) * jnp.einsum('bhtd,bhsd->bhts', qb, kb)
        w_inter = jnp.exp(inter - m_t)
        num = (w_inter[..., None] * jnp.einsum('bhvk,bhtk->bhtv', C, qb)
               + jnp.einsum('bhts,bhsv->bhtv', w_intra, vb))
        den = w_inter * jnp.einsum('bhk,bhtk->bht', n, qb) + jnp.sum(w_intra, axis=-1)
        h = num / jnp.maximum(jnp.abs(den), jnp.exp(-m_t))[..., None]
        b_last = b[..., -1]
        log_s = b_last[..., None] - b + ig
        m_new = jnp.maximum(b_last + m, jnp.max(log_s, axis=-1))
        w_s = jnp.exp(log_s - m_new[..., None])
        decay = jnp.exp(b_last + m - m_new)
        C = decay[..., None, None] * C + jnp.einsum('bhs,bhsv,bhsk->bhvk', w_s, vb, kb)
        n = decay[..., None] * n + jnp.einsum('bhs,bhsk->bhk', w_s, kb)
        return (C, n, m_new), h

    init = (jnp.zeros((B, H, D, D), jnp.float32), jnp.zeros((B, H, D), jnp.float32),
            jnp.zeros((B, H), jnp.float32))
    _, hs = lax.scan(step, init, (qc, kc, vc, ic, fc))
    return hs.transpose(1, 0, 3, 2, 4).reshape(B, S, H, D).astype(q.dtype)


def diff_attention(q, k, v, lam):
    B, S, H, _, d = q.shape
    dv = v.shape[-1]
    nq = S // Q_BLOCK
    kt = k.transpose(0, 2, 3, 1, 4)
    vt = v.transpose(0, 2, 1, 3)
    qb = q.transpose(0, 2, 3, 1, 4).reshape(B, H, 2, nq, Q_BLOCK, d).transpose(3, 0, 1, 2, 4, 5)
    key_pos = jnp.arange(S)
    scale = d ** -0.5

    def block(args):
        qi, idx = args
        s = jnp.einsum('bhcqd,bhckd->bhcqk', qi, kt).astype(jnp.float32) * scale
        q_pos = idx * Q_BLOCK + jnp.arange(Q_BLOCK)
        mask = key_pos[None, :] <= q_pos[:, None]
        p = jax.nn.softmax(jnp.where(mask, s, -jnp.inf), axis=-1)
        a = p[:, :, 0] - lam * p[:, :, 1]
        return jnp.einsum('bhqk,bhkv->bhqv', a.astype(vt.dtype), vt)

    out = lax.map(block, (qb, jnp.arange(nq)))
    return out.transpose(1, 0, 3, 2, 4).reshape(B, S, H, dv)


def setup_inputs(seed: int = 0) -> dict:
    key = jax.random.key(seed)
    ks = jax.random.split(key, 24)
    nrm = lambda k, shape, s: jax.random.normal(k, shape, jnp.float32) * s
    x = nrm(ks[0], (BATCH, SEQ, D_MODEL), 1.0)
    c = nrm(ks[1], (BATCH, D_MODEL), 1.0)
    start = jax.random.randint(ks[2], (BATCH, 1), 0, 4096, dtype=jnp.int32)
    positions = (start + jnp.arange(SEQ, dtype=jnp.int32)[None, :]).astype(jnp.int32)
    b_i = nrm(ks[8], (DEPTH, H_M), 0.1)
    b_f = jnp.linspace(3.0, 6.0, H_M, dtype=jnp.float32)[None, :] + nrm(ks[9], (DEPTH, H_M), 0.1)
    return {
        'x': x,
        'c': c,
        'positions': positions,
        'w_ada': nrm(ks[3], (DEPTH, D_MODEL, 6 * D_MODEL), D_MODEL ** -0.5),
        'b_ada': nrm(ks[4], (DEPTH, 6 * D_MODEL), 0.02),
        'g_mix': 1.0 + nrm(ks[5], (DEPTH, D_MODEL), 0.05),
        'w_in': nrm(ks[6], (DEPTH, D_MODEL, IN_COLS), D_MODEL ** -0.5),
        'conv_qk_w': nrm(ks[7], (DEPTH, CONV_QK, 2 * W_M), CONV_QK ** -0.5),
        'conv_qk_b': nrm(ks[10], (DEPTH, 2 * W_M), 0.02),
        'b_if': jnp.concatenate([b_i, b_f], axis=-1),
        'g_mlstm': 1.0 + nrm(ks[11], (DEPTH, W_M), 0.05),
        'lam_q1': nrm(ks[12], (DEPTH, DQK_D), 0.1),
        'lam_k1': nrm(ks[13], (DEPTH, DQK_D), 0.1),
        'lam_q2': nrm(ks[14], (DEPTH, DQK_D), 0.1),
        'lam_k2': nrm(ks[15], (DEPTH, DQK_D), 0.1),
        'g_diff': 1.0 + nrm(ks[16], (DEPTH, W_D), 0.05),
        'w_out': nrm(ks[17], (DEPTH, W_MIX, D_MODEL), W_MIX ** -0.5),
        'g_ffn': 1.0 + nrm(ks[18], (DEPTH, D_MODEL), 0.05),
        'w_up': nrm(ks[19], (DEPTH, D_MODEL, 2 * D_FF), D_MODEL ** -0.5),
        'conv_ffn_w': nrm(ks[20], (DEPTH, CONV_FFN, 2 * D_FF), CONV_FFN ** -0.5),
        'conv_ffn_b': nrm(ks[21], (DEPTH, 2 * D_FF), 0.02),
        'w_down': nrm(ks[22], (DEPTH, D_FF, D_MODEL), D_FF ** -0.5),
        'g_final': 1.0 + nrm(ks[23], (D_MODEL,), 0.05),
    }


def reference(x, c, positions, w_ada, b_ada, g_mix, w_in, conv_qk_w, conv_qk_b, b_if, g_mlstm,
              lam_q1, lam_k1, lam_q2, lam_k2, g_diff, w_out, g_ffn, w_up, conv_ffn_w, conv_ffn_b,
              w_down, g_final):
    B, S, _ = x.shape
    half = DQK_D // 2
    inv_freq = ROPE_THETA ** (-jnp.arange(half, dtype=jnp.float32) / half)
    ang = positions.astype(jnp.float32)[..., None] * inv_freq
    cos = jnp.cos(ang)[:, :, None, None, :]
    sin = jnp.sin(ang)[:, :, None, None, :]
    c_act = jax.nn.silu(c)
    splits = list(np.cumsum(IN_SIZES)[:-1])

    for l in range(DEPTH):
        mod = c_act @ w_ada[l] + b_ada[l]
        sh_a, sc_a, gt_a, sh_f, sc_f, gt_f = jnp.split(mod, 6, axis=-1)

        h = modulate(rmsnorm(x, g_mix[l]), sh_a, sc_a)
        proj = h @ w_in[l]
        q_m, k_m, v_m, o_m, i_m, f_m, q_d, k_d, v_d = jnp.split(proj, splits, axis=-1)

        qk = jax.nn.silu(causal_dwconv(jnp.concatenate([q_m, k_m], axis=-1), conv_qk_w[l], conv_qk_b[l]))
        q_m, k_m = jnp.split(qk, 2, axis=-1)
        i_m = i_m + b_if[l, :H_M]
        f_m = f_m + b_if[l, H_M:]
        hm = mlstm_chunkwise(q_m.reshape(B, S, H_M, DH_M), k_m.reshape(B, S, H_M, DH_M),
                             v_m.reshape(B, S, H_M, DH_M), i_m, f_m)
        hm = rmsnorm(hm, g_mlstm[l].reshape(H_M, DH_M)) * jax.nn.sigmoid(o_m).reshape(B, S, H_M, DH_M)

        lam_init = 0.8 - 0.6 * math.exp(-0.3 * l)
        lam = (jnp.exp(jnp.sum(lam_q1[l].astype(jnp.float32) * lam_k1[l].astype(jnp.float32)))
               - jnp.exp(jnp.sum(lam_q2[l].astype(jnp.float32) * lam_k2[l].astype(jnp.float32))) + lam_init)
        qd = rope(q_d.reshape(B, S, H_D, 2, DQK_D), cos, sin)
        kd = rope(k_d.reshape(B, S, H_D, 2, DQK_D), cos, sin)
        hd = diff_attention(qd, kd, v_d.reshape(B, S, H_D, DV_D), lam)
        hd = rmsnorm(hd, g_diff[l].reshape(H_D, DV_D)) * (1.0 - lam_init)

        mix = jnp.concatenate([hm.reshape(B, S, W_M), hd.reshape(B, S, W_D)], axis=-1)
        x = x + gt_a[:, None, :] * (mix @ w_out[l])

        h = modulate(rmsnorm(x, g_ffn[l]), sh_f, sc_f)
        u = causal_dwconv(h @ w_up[l], conv_ffn_w[l], conv_ffn_b[l])
        a, g = jnp.split(u, 2, axis=-1)
        x = x + gt_f[:, None, :] * ((jax.nn.silu(g) * a) @ w_down[l])

    return rmsnorm(x, g_final)
```

```python
import os
import math
import numpy as np
from contextlib import ExitStack
import concourse.bass as bass
import concourse.mybir as mybir
from concourse.bass_utils import run_bass_kernel_spmd

F32 = mybir.dt.float32
BF16 = mybir.dt.bfloat16
I32 = mybir.dt.int32
AF = mybir.ActivationFunctionType
ALU = mybir.AluOpType
AX = mybir.AxisListType

S = 2048
D = 1024
NT = 16
NB = 4
EPS = 1e-6
LAM_INIT = 0.8 - 0.6 * math.exp(-0.3 * 0)
PI = math.pi
TWO_PI = 2.0 * math.pi
CW1 = 6.28125
CW2 = TWO_PI - 6.28125
NSLOT = 4
LOOKAHEAD = 3
DEBUG = os.environ.get("MK_DEBUG", "")


class T:
    __slots__ = ("name", "w", "r", "excl", "wl")

    def __init__(self, name="", excl=False):
        self.name = name
        self.w = None
        self.r = {}
        self.wl = []
        self.excl = excl


class Builder:
    ENG = ("pe", "dve", "act", "pool", "sp")

    def __init__(self, nc, es):
        self.nc = nc
        self.sem = {k: es.enter_context(nc.semaphore("sem_" + k)) for k in self.ENG}
        self.cnt = {k: 0 for k in self.ENG}
        self.ops = {k: [] for k in self.ENG}
        self.known = {k: {} for k in self.ENG}
        self.NRING = 24
        self.ring = [es.enter_context(nc.semaphore("dma%d" % i)) for i in range(self.NRING)]
        self.ring_cnt = [0] * self.NRING
        self.ring_i = 0
        self.out_tokens = []
        self.swsem = []
        self._es = es

    def _semobj(self, key):
        if isinstance(key, tuple):
            return self.swsem[key[1]]
        return self.sem[key] if isinstance(key, str) else self.ring[key]

    def _need(self, eng, tok, waits, raw):
        if tok is None:
            return
        key, val = tok
        if key == eng and not raw and eng != "pool":
            return
        if self.known[eng].get(key, 0) >= val:
            return
        self.known[eng][key] = val
        waits.append((key, val))

    def _deps(self, eng, R, W, soft=False):
        waits = []
        for t in R:
            self._need(eng, t.w, waits, True)
            for tk in t.wl:
                self._need(eng, tk, waits, True)
            if t.excl:
                for k, v in t.r.items():
                    self._need(eng, (k, v), waits, False)
        for t in W:
            if not soft:
                self._need(eng, t.w, waits, False)
                for tk in t.wl:
                    self._need(eng, tk, waits, False)
            for k, v in t.r.items():
                self._need(eng, (k, v), waits, False)
        return waits

    def _commit(self, tok, R, W, soft=False):
        for t in W:
            if soft:
                t.wl.append(tok)
            else:
                t.w = tok
                t.wl = []
            t.r = {}
        k, v = tok
        for t in R:
            if t.r.get(k, 0) < v:
                t.r[k] = v

    def op(self, eng, fn, R=(), W=(), inc=True, soft=False):
        waits = self._deps(eng, R, W, soft)
        tok = (eng, self.cnt[eng] + 1)
        if inc:
            self.cnt[eng] += 1
        self._commit(tok, R, W, soft)
        self.ops[eng].append((waits, fn, (eng, 1) if inc else None))
        return tok

    def dma(self, q, out, in_, R=(), W=(), is_out=False, soft=False, **kw):
        waits = self._deps(q, R, W, soft)
        if q == "pool":
            n = len(self.swsem)
            self.swsem.append(self._es.enter_context(self.nc.semaphore("swdma%d" % n)))
            tok = (("sw", n), 16)
            self._commit(tok, R, W, soft)
            self.ops[q].append((waits, lambda e, out=out, in_=in_, kw=kw: e.dma_start(out=out, in_=in_, **kw), (tok[0], 16)))
            if is_out:
                self.out_tokens.append(tok)
            return tok
        i = self.ring_i
        self.ring_i = (self.ring_i + 1) % self.NRING
        if self.ring_cnt[i] > 0:
            self._need(q, (i, self.ring_cnt[i]), waits, True)
        self.ring_cnt[i] += 16
        tok = (i, self.ring_cnt[i])
        self._commit(tok, R, W, soft)
        self.ops[q].append((waits, lambda e, out=out, in_=in_, kw=kw: e.dma_start(out=out, in_=in_, **kw), (i, 16)))
        if is_out:
            self.out_tokens.append(tok)
        return tok

    def finish(self):
        waits = []
        for tok in self.out_tokens:
            self._need("sp", tok, waits, True)
        self.ops["sp"].append((waits, None, None))

    def emit(self, block):
        def run(eng_key):
            def body(e):
                for waits, fn, inc in self.ops[eng_key]:
                    for key, val in waits:
                        e.wait_ge(self._semobj(key), val)
                    if fn is None:
                        continue
                    ins = fn(e)
                    if inc is not None:
                        ins.then_inc(self._semobj(inc[0]), inc[1])
            return body
        block.tensor(run("pe"))
        block.vector(run("dve"))
        block.scalar(run("act"))
        block.gpsimd(run("pool"))
        block.sync(run("sp"))


def build_program(stop_after=None):
    nc = bass.Bass("TRN2", target_bir_lowering=False)
    dt_in = lambda name, shape, dt=F32: nc.dram_tensor(name, list(shape), dt, kind="ExternalInput").ap()
    xT_d = dt_in("xT", [128, 8, S])
    c_d = dt_in("c", [128, 8])
    pos_d = dt_in("pos", [128, S], I32)
    wada_d = dt_in("w_ada", [12, 128, 4096])
    bada_d = dt_in("b_ada", [128, 48])
    gcols_d = dt_in("gcols", [128, 24])
    win_d = dt_in("w_in", [7, 128, 4096])
    wgate_d = dt_in("w_gate", [128, 64])
    cqk_d = dt_in("cqk", [128, 8, 5])
    bif_d = dt_in("bif", [128, 16, 8])
    gmB_d = dt_in("gmB", [128, 512])
    gdB_d = dt_in("gdB", [128, 512])
    lam_d = dt_in("lam", [128, 4, 64])
    wout_d = dt_in("w_out", [2, 128, 4096])
    wup_d = dt_in("w_up", [11, 128, 4096])
    cffn_d = dt_in("cffn", [128, 44, 4])
    wdown_d = dt_in("w_down", [8, 128, 2816])
    cmask_d = dt_in("cmask", [128, 4, 128])
    consts_d = dt_in("consts", [128, 5, 128])
    out_d = nc.dram_tensor("outT", [128, 8, S], F32, kind="ExternalOutput").ap()
    dbg_d = nc.dram_tensor("dbg", [128, 8, S], F32, kind="ExternalOutput").ap() if DEBUG else None

    with ExitStack() as es:
        B = Builder(nc, es)
        sb = lambda name, shape, dt=F32: es.enter_context(nc.sbuf_tensor("s_" + name, list(shape), dt))

        ps = es.enter_context(nc.psum_tensor("ps", [128, 8, 512], F32))
        PSB = [T("ps%d" % i, excl=True) for i in range(8)]
        ring = sb("ring", [128, NSLOT, 4096], BF16)
        RING_T = [T("slot%d" % i) for i in range(NSLOT)]
        cst_f = sb("cst_f", [128, 5, 128], F32)
        cst_b = sb("cst_b", [128, 4, 128], BF16)
        smalls = sb("smalls", [128, 256], F32)
        RA = sb("RA", [128, 8 * S], BF16)
        RBm = sb("RBm", [128, 16 * 1024], BF16)
        RC = sb("RC", [128, 16512], F32)
        gts = sb("gts", [128, 16, 8], F32)
        gmath = sb("gmath", [128, 8, 64], F32)
        gmB = sb("gmB", [128, 512], F32)
        gdB = sb("gdB", [128, 512], F32)
        lamt = sb("lamt", [128, 4, 64], F32)
        cqk = sb("cqk", [128, 8, 5], F32)
        cffn = sb("cffn", [128, 44, 4], F32)
        bif = sb("bif", [128, 16, 8], F32)
        stg = sb("stg", [128, 4, 1040], BF16)
        stg32 = sb("stg32", [128, 2, 512], F32)
        stg32b = sb("stg32b", [128, 2, 512], F32)
        diag = sb("diag", [128, 16, 128], BF16)
        et = sb("et", [128, 3, 512], BF16)
        cst32 = sb("cst32", [128, 4, 130], F32)
        cdb = sb("cdb", [128, 4, 130], BF16)
        kwt = sb("kwt", [128, 2, 512], BF16)
        fin = sb("fin", [128, 4, 128], F32)
        finc = sb("finc", [128, 64], F32)
        halo = sb("halo", [128, 44, 2], BF16)
        cact_b = sb("cact_b", [128, 8], BF16)
        haloqk = sb("haloqk", [128, 8, 4], BF16)
        cmask = sb("cmask", [128, 4, 128], BF16)

        hT = RA[:].rearrange("p (c t) -> p c t", c=8)
        mixT = hT
        xin = RBm[:, 0:8192].bitcast(F32).rearrange("p (c t) -> p c t", c=8)
        mix = RBm[:].rearrange("p (n f) -> p n f", n=16)
        xinB = RC[:, 8192:12288].rearrange("p (c t) -> p c t", c=8)
        xinC = RBm[:, 8192:16384].bitcast(F32).rearrange("p (c t) -> p c t", c=8)
        xinD = RC[:, 4096:8192].rearrange("p (c t) -> p c t", c=8)
        og = mix[:, :, 512:1024]
        qkT = RC[:, 0:8192].bitcast(BF16).rearrange("p (c t) -> p c t", c=8)
        vaug = RC[:, 8192:8192 + 4128].bitcast(BF16).rearrange("p (n h e) -> p n h e", n=16, h=4)
        cs = RC[:, 12320:12320 + 4096].rearrange("p (k t) -> p k t", k=2)
        x1T = RC[:, 0:16384].rearrange("p (c t) -> p c t", c=8)
        h2T = RA[:, 0:8192].rearrange("p (c t) -> p c t", c=8)
        actA = RBm[:].rearrange("p (c t) -> p c t", c=16)
        actB = RA[:, 8192:8192 + 6144].rearrange("p (c t) -> p c t", c=6)

        def actT(i):
            return actA[:, i, :] if i < 16 else actB[:, i - 16, :]

        T_hT = [T("hT%d" % i) for i in range(NB)]
        T_xin = T("xin")
        T_xinB = T("xinB")
        T_xinC = T("xinC")
        T_xinD = T("xinD")
        T_mod2 = T("mod2")
        T_rs = T("ropescr")
        T_ss = T("ss")
        T_mix = [T("mix%d" % i) for i in range(NT)]
        T_og = [T("og%d" % i) for i in range(NT)]
        T_qk = [[T("qk%d_%d" % (c, b)) for b in range(NB)] for c in range(8)]
        T_vaug = [T("vaug%d" % i) for i in range(NT)]
        T_cs = T("cs")
        T_x1 = [[T("x1_%d_%d" % (c, b)) for b in range(NB)] for c in range(8)]
        T_cst = T("cst")
        T_small = T("small")
        T_bif = T("bif")
        T_gts = T("gts")
        T_gm = T("gmath")
        T_stg = [T("stg%d" % i) for i in range(4)]
        T_s32 = [T("s32_0"), T("s32_1")]
        T_s32b = [T("s32b_0"), T("s32b_1")]
        T_diag = [T("diag%d" % i) for i in range(4)]
        T_et = [T("et0"), T("et1"), T("et2")]
        T_C32 = [T("C32_%d" % h) for h in range(4)]
        T_cdb = [T("cdb_%d" % h) for h in range(4)]
        T_kwt = [T("kwt0"), T("kwt1")]
        T_fin = T("fin")
        T_finc = T("finc")
        T_halo = T("halo")
        T_hqk = [T("hqk%d" % i) for i in range(8)]
        T_mixT = [T("mixT%d" % i) for i in range(NB)]
        T_h2 = [T("h2_0"), T("h2_1")]
        T_act = [[T("act%d_%d" % (i, s)) for s in range(2)] for i in range(22)]
        ALL_RC = [t for row in T_qk for t in row] + T_vaug + [T_cs]
        ALL_MIX = T_mix + T_og + [T_xin, T_xinC]

        ident_b = cst_b[:, 0, :]
        triu_b = cst_b[:, 1, :]
        ones_b = cst_b[:, 2, :]
        perm_b = cst_b[:, 3, :]
        ident_f = cst_f[:, 0, :]
        triu_f = cst_f[:, 1, :]
        ones_f = cst_f[:, 2, :]
        invf = cst_f[:, 4, 0:1]
        sgn = cst_f[:, 4, 1:2]

        c_raw = smalls[:, 0:8]
        mod = smalls[:, 16:64]
        A1 = smalls[:, 64:72]
        A2 = smalls[:, 72:80]
        gcols = smalls[:, 80:104]
        bada = smalls[:, 104:152]
        epsc = smalls[:, 152:153]
        lamc = smalls[:, 153:154]
        nlamc = smalls[:, 154:155]
        onec = smalls[:, 155:156]
        ltmp = smalls[:, 156:160]
        sh_a, sc_a, gt_a = mod[:, 0:8], mod[:, 8:16], mod[:, 16:24]
        sh_f, sc_f, gt_f = mod[:, 24:32], mod[:, 32:40], mod[:, 40:48]
        g_fin = gcols[:, 16:24]

        rot = {"i": 0}

        def nextbank(lo=0, hi=8):
            b = lo + rot["i"] % (hi - lo)
            rot["i"] += 1
            return b

        WSEQ = ([(wada_d[i], 4096) for i in range(4)] +
                [(win_d[0], 4096), (win_d[1], 4096), (wgate_d, 64), (win_d[2], 4096), (win_d[3], 4096),
                 (win_d[4], 4096), (win_d[5], 4096), (win_d[6], 4096)] +
                [(wada_d[i], 4096) for i in range(4, 12)] +
                [(wout_d[0], 4096), (wout_d[1], 4096)])
        for _p in range(2):
            WSEQ += [(wup_d[g], 4096) for g in range(11)] + [(wdown_d[f], 2816) for f in range(8)] * (1 + _p)
        ws = {"issued": 0, "next": 0}

        def _issue_to(n):
            while ws["issued"] < min(n, len(WSEQ)):
                k = ws["issued"]
                src, nel = WSEQ[k]
                s = k % NSLOT
                a = 4 if nel == 4096 else (2 if nel == 2816 else 1)
                B.dma("pool", ring[:, s, 0:nel].rearrange("p (a n) -> p a n", a=a), src.rearrange("p (a n) -> p a n", a=a), W=[RING_T[s]],
                      R=([T_small, T_cst, T_rs, T_bif] if k == 0 else []))
                ws["issued"] += 1

        def wget(la=LOOKAHEAD):
            k = ws["next"]
            ws["next"] += 1
            _issue_to(k + 1 + la)
            return k % NSLOT

        def slot_view(s, kc, n):
            return ring[:, s, 0:kc * n].rearrange("p (k n) -> p k n", k=kc)

        class _Stop(Exception):
            pass

        def dbg_dump(ap, Ts):
            dd = dbg_d
            if tuple(ap.shape) == (128, 16, 1024):
                dd = dbg_d.rearrange("p c (a t) -> p (c a) t", a=2)
            B.dma("pool", dd, ap, R=Ts, is_out=True)
            raise _Stop()

        def body():
            B.dma("sp", RC[:, 0:2048].bitcast(I32), pos_d, W=[T_rs])
            B.dma("sp", cst_f[:], consts_d, W=[T_cst], soft=True)
            B.dma("pool", cmask[:], cmask_d, W=[T_cst], soft=True)
            B.dma("sp", c_raw, c_d, W=[T_small], soft=True)
            B.dma("sp", bada, bada_d, W=[T_small], soft=True)
            B.dma("sp", gcols, gcols_d, W=[T_small], soft=True)
            B.dma("sp", cqk[:], cqk_d, W=[T_small], soft=True)
            B.dma("sp", cffn[:], cffn_d, W=[T_small], soft=True)
            B.dma("sp", gmB[:], gmB_d, W=[T_small], soft=True)
            B.dma("sp", gdB[:], gdB_d, W=[T_small], soft=True)
            B.dma("sp", lamt[:], lam_d, W=[T_small], soft=True)
            B.dma("sp", bif[:], bif_d, W=[T_bif])
            B.op("dve", lambda e: e.tensor_copy(out=cst_b[:], in_=cst_f[:, 0:4, :]), R=[T_cst], W=[T_cst])
            B.op("dve", lambda e: e.memset(epsc, EPS), W=[T_small], soft=True)
            B.op("dve", lambda e: e.memset(onec, 1.0), W=[T_small], soft=True)
            B.op("dve", lambda e: e.memset(halo[:], 0.0), W=[T_halo])
            B.op("act", lambda e: e.activation(out=cact_b[:], in_=c_raw, func=AF.Silu), R=[T_small], W=[T_small])
            _issue_to(LOOKAHEAD)

            def rope_tables():
                sA = RC[:, 0:2048]
                sB = RC[:, 2048:4096]
                sC = RC[:, 4096:6144]
                sAi = sA.bitcast(I32)
                op = lambda fn, **kw: B.op("dve", fn, R=[T_rs, T_cst], W=[T_rs])
                op(lambda e: e.tensor_copy(out=sB, in_=sAi))
                op(lambda e: e.tensor_scalar(out=sB, in0=sB, scalar1=invf, scalar2=None, op0=ALU.mult))
                op(lambda e: e.tensor_scalar(out=sC, in0=sB, scalar1=1.0 / TWO_PI, scalar2=None, op0=ALU.mult))
                op(lambda e: e.tensor_copy(out=sAi, in_=sC))
                op(lambda e: e.tensor_copy(out=sC, in_=sAi))
                op(lambda e: e.scalar_tensor_tensor(out=sB, in0=sC, scalar=-TWO_PI, in1=sB, op0=ALU.mult, op1=ALU.add))
                op(lambda e: e.tensor_scalar(out=sC, in0=sB, scalar1=PI, scalar2=-TWO_PI, op0=ALU.is_gt, op1=ALU.mult))
                op(lambda e: e.tensor_tensor(out=sB, in0=sB, in1=sC, op=ALU.add))
                op(lambda e: e.tensor_scalar(out=sB, in0=sB, scalar1=-PI, scalar2=TWO_PI, op0=ALU.is_lt, op1=ALU.mult) if False else
                   e.tensor_scalar(out=sB, in0=sB, scalar1=PI, scalar2=-PI, op0=ALU.min, op1=ALU.max))
                B.op("act", lambda e: e.activation(out=cs[:, 1, :], in_=sB, func=AF.Sin, scale=sgn), R=[T_rs, T_cst], W=[T_cs])
                op(lambda e: e.tensor_scalar(out=sC, in0=sB, scalar1=PI / 2, scalar2=None, op0=ALU.add))
                sAf = sA
                op(lambda e: e.tensor_scalar(out=sAf, in0=sC, scalar1=PI, scalar2=-TWO_PI, op0=ALU.is_gt, op1=ALU.mult))
                op(lambda e: e.tensor_tensor(out=sC, in0=sC, in1=sAf, op=ALU.add))
                op(lambda e: e.tensor_scalar(out=sC, in0=sC, scalar1=PI, scalar2=-PI, op0=ALU.min, op1=ALU.max))
                B.op("act", lambda e: e.activation(out=cs[:, 0, :], in_=sC, func=AF.Sin), R=[T_rs], W=[T_cs])

            rope_tables()

            def adaln_cols(pieces, pb, col0=None):
                for piece in pieces:
                    s = wget()
                    wv = slot_view(s, 8, 512)
                    for jj in range(4):
                        j = (piece * 4 + jj) if col0 is None else (col0 + jj)
                        for kc in range(8):
                            B.op("pe", lambda e, wv=wv, jj=jj, kc=kc, j=j: e.matmul(
                                ps[:, pb, j:j + 1], lhsT=wv[:, kc, jj * 128:(jj + 1) * 128], rhs=cact_b[:, kc:kc + 1],
                                start=(kc == 0), stop=(kc == 7)),
                                R=[RING_T[s], T_small], W=[PSB[pb]], inc=(kc == 7))

            def adaln_piece_late(piece, pb):
                adaln_cols([piece], pb, col0=0)
                j0 = piece * 4
                B.op("act", lambda e: e.activation(out=mod[:, j0:j0 + 4], in_=ps[:, pb, 0:4], func=AF.Copy), R=[PSB[pb]], W=[T_mod2])
                B.op("dve", lambda e: e.tensor_tensor(out=mod[:, j0:j0 + 4], in0=mod[:, j0:j0 + 4], in1=bada[:, j0:j0 + 4], op=ALU.add),
                     R=[T_mod2, T_small], W=[T_mod2])

            pb_mod = nextbank()
            adaln_cols(range(4), pb_mod)
            B.op("dve", lambda e: e.tensor_tensor(out=mod[:, 0:16], in0=ps[:, pb_mod, 0:16], in1=bada[:, 0:16], op=ALU.add),
                 R=[PSB[pb_mod], T_small], W=[T_small])
            B.op("dve", lambda e: e.scalar_tensor_tensor(out=A1, in0=sc_a, scalar=1.0, in1=gcols[:, 0:8], op0=ALU.add, op1=ALU.mult),
                 R=[T_small], W=[T_small])
            B.op("dve", lambda e: e.tensor_tensor(out=stg32[:, 0, 0:64], in0=lamt[:, 0, :], in1=lamt[:, 1, :], op=ALU.mult), R=[T_small], W=[T_s32[0]])
            B.op("dve", lambda e: e.tensor_tensor(out=stg32[:, 0, 64:128], in0=lamt[:, 2, :], in1=lamt[:, 3, :], op=ALU.mult), R=[T_small], W=[T_s32[0]])
            B.op("dve", lambda e: e.tensor_reduce(out=ltmp[:, 0:2], in_=stg32[:, 0, 0:128].rearrange("p (a b) -> p a b", a=2), axis=AX.X, op=ALU.add),
                 R=[T_s32[0]], W=[T_small])
            B.op("act", lambda e: e.activation(out=ltmp[:, 2:4], in_=ltmp[:, 0:2], func=AF.Exp), R=[T_small], W=[T_small])
            B.op("dve", lambda e: e.scalar_tensor_tensor(out=lamc, in0=ltmp[:, 2:3], scalar=LAM_INIT, in1=ltmp[:, 3:4], op0=ALU.add, op1=ALU.subtract),
                 R=[T_small], W=[T_small])
            B.op("dve", lambda e: e.tensor_scalar(out=nlamc, in0=lamc, scalar1=-1.0, scalar2=None, op0=ALU.mult), R=[T_small], W=[T_small])
            B.op("dve", lambda e: e.tensor_scalar(out=gdB[:], in0=gdB[:], scalar1=(1.0 - LAM_INIT), scalar2=None, op0=ALU.mult), R=[T_small], W=[T_small])

            def norm_block(src, T_src, dst, T_dst, Acol, shcol, ncols, si, T_small=T_small, defer=False):
                pb = nextbank()
                for fc in range(8):
                    sj = 2 * si + fc % 2
                    if fc % 2 == 0:
                        B.op("act", lambda e, fc=fc, sj=sj: e.activation(out=stg[:, sj, 0:ncols], in_=src[:, fc, :], func=AF.Square),
                             R=T_src, W=[T_stg[sj]])
                    else:
                        B.op("dve", lambda e, fc=fc, sj=sj: e.tensor_tensor(out=stg[:, sj, 0:ncols], in0=src[:, fc, :], in1=src[:, fc, :], op=ALU.mult),
                             R=T_src, W=[T_stg[sj]])
                    B.op("pe", lambda e, fc=fc, sj=sj: e.matmul(ps[:, pb, 0:ncols], lhsT=ones_b, rhs=stg[:, sj, 0:ncols],
                                                              start=(fc == 0), stop=(fc == 7)),
                         R=[T_stg[sj], T_cst], W=[PSB[pb]], inc=True)
                rs = stg32[:, si, 0:ncols]
                B.op("act", lambda e: e.activation(out=rs, in_=ps[:, pb, 0:ncols], func=AF.Ln, bias=epsc, scale=1.0 / D),
                     R=[PSB[pb], T_small], W=[T_s32[si]])
                B.op("act", lambda e: e.activation(out=rs, in_=rs, func=AF.Exp, scale=-0.5), R=[T_s32[si]], W=[T_s32[si]])
                if defer:
                    return lambda: norm_mod(src, T_src, dst, T_dst, Acol, shcol, ncols, si, T_small)
                norm_mod(src, T_src, dst, T_dst, Acol, shcol, ncols, si, T_small)

            def norm_mod(src, T_src, dst, T_dst, Acol, shcol, ncols, si, T_small):
                rs = stg32[:, si, 0:ncols]
                for fc in range(8):
                    if shcol is not None:
                        tmp = stg32b[:, fc % 2, 0:ncols]
                        B.op("dve", lambda e, fc=fc, tmp=tmp: e.scalar_tensor_tensor(
                            out=tmp, in0=src[:, fc, :], scalar=Acol[:, fc:fc + 1], in1=rs, op0=ALU.mult, op1=ALU.mult),
                            R=list(T_src) + [T_s32[si], T_small], W=[T_s32b[fc % 2]])
                        B.op("act", lambda e, fc=fc, tmp=tmp: e.activation(out=dst[:, fc, :], in_=tmp, func=AF.Identity,
                                                                         bias=shcol[:, fc:fc + 1], scale=1.0),
                             R=[T_s32b[fc % 2], T_small], W=T_dst)
                    else:
                        B.op("dve", lambda e, fc=fc: e.scalar_tensor_tensor(
                            out=dst[:, fc, :], in0=src[:, fc, :], scalar=Acol[:, fc:fc + 1], in1=rs, op0=ALU.mult, op1=ALU.mult),
                            R=list(T_src) + [T_s32[si], T_small], W=T_dst)

            dg = {"i": 0}

            def build_diag(wcols, ntap, Tw):
                d = dg["i"] % 4
                dg["i"] += 1
                for j in range(ntap):
                    B.op("dve", lambda e, j=j, d=d: e.tensor_scalar(out=diag[:, d * 4 + j, :], in0=ident_f,
                                                                 scalar1=wcols[:, j:j + 1], scalar2=None, op0=ALU.mult),
                         R=[T_cst, Tw], W=[T_diag[d]])
                return d

            p2a = {"prev": None, "n": 0, "slots": None}

            def p2a_proj(cc, tb):
                if p2a["slots"] is None:
                    sq_ = wget()
                    sk_ = wget(la=2)
                    p2a["slots"] = (sq_, sk_)
                s = p2a["slots"][cc // 4]
                wv = slot_view(s, 8, 512)
                cj = cc % 4
                d = build_diag(cqk[:, cc, 0:4], 4, T_small)
                pb = nextbank()
                for kc in range(8):
                    B.op("pe", lambda e, kc=kc, pb=pb, tb=tb, wv=wv, cj=cj: e.matmul(
                        ps[:, pb, :], lhsT=wv[:, kc, cj * 128:(cj + 1) * 128], rhs=hT[:, kc, tb * 512:(tb + 1) * 512],
                        start=(kc == 0), stop=(kc == 7)), R=[T_hT[tb], RING_T[s]], W=[PSB[pb]], inc=(kc == 7))
                si = p2a["n"] % 2
                p2a["n"] += 1
                if tb == 0:
                    B.op("dve", lambda e, si=si: e.memset(stg[:, si, 0:3], 0.0), W=[T_stg[si]])
                else:
                    B.op("dve", lambda e, si=si, cc=cc: e.tensor_copy(out=stg[:, si, 0:3], in_=haloqk[:, cc, 0:3]), R=[T_hqk[cc]], W=[T_stg[si]])
                B.op("act", lambda e, si=si, pb=pb: e.activation(out=stg[:, si, 3:515], in_=ps[:, pb, :], func=AF.Copy),
                     R=[PSB[pb]], W=[T_stg[si]])
                B.op("dve", lambda e, si=si, cc=cc: e.tensor_copy(out=haloqk[:, cc, 0:3], in_=stg[:, si, 512:515]), R=[T_stg[si]], W=[T_hqk[cc]])
                return (cc, tb, si, d)

            def p2a_conv(cc, tb, si, d):
                pb2 = nextbank()
                for j in range(4):
                    B.op("pe", lambda e, j=j, si=si, pb2=pb2, d=d: e.matmul(
                        ps[:, pb2, :], lhsT=diag[:, d * 4 + j, :], rhs=stg[:, si, j:j + 512], start=(j == 0), stop=(j == 3)),
                        R=[T_stg[si], T_diag[d]], W=[PSB[pb2]], inc=(j == 3))
                B.op("act", lambda e, pb2=pb2, cc=cc, tb=tb: e.activation(
                    out=qkT[:, cc, tb * 512:(tb + 1) * 512], in_=ps[:, pb2, :], func=AF.Silu, bias=cqk[:, cc, 4:5], scale=1.0),
                    R=[PSB[pb2], T_small], W=[T_qk[cc][tb]] + ([T_rs] if cc < 6 else []))

            def p2a_push(cc, tb):
                h_ = p2a_proj(cc, tb)
                if p2a["prev"] is not None:
                    p2a_conv(*p2a["prev"])
                p2a["prev"] = h_

            def p2a_flush():
                if p2a["prev"] is not None:
                    p2a_conv(*p2a["prev"])
                    p2a["prev"] = None

            prev_mod = None
            for tb in range(NB):
                xb, Tx = [(xin, T_xin), (xinB, T_xinB), (xinC, T_xinC), (xin, T_xin)][tb]
                B.dma("sp", xb, xT_d[:, :, tb * 512:(tb + 1) * 512], W=[Tx], R=([RING_T[3]] if tb == 2 else []))
                m = norm_block(xb, [Tx], hT[:, :, tb * 512:(tb + 1) * 512], [T_hT[tb]], A1, sh_a, 512, tb % 2, defer=True)
                if tb >= 2:
                    for cc in range(4 * (tb - 2), 4 * (tb - 2) + 4):
                        p2a_push(cc, 0)
                if prev_mod is not None:
                    prev_mod()
                prev_mod = m
            for cc in range(4):
                p2a_push(cc, 1)
            prev_mod()
            for cc in range(4, 8):
                p2a_push(cc, 1)
            for tb in (2, 3):
                for cc in range(8):
                    p2a_push(cc, tb)
            p2a_flush()
            if DEBUG == "hT":
                dbg_dump(hT, T_hT)

            if DEBUG == "qkm":
                dbg_dump(qkT, [t for row in T_qk for t in row])

            def tokmajor_v(s_v, hook):
                wv_v = slot_view(s_v, 8, 512)
                for tt in range(NT):
                    if tt == 8 and hook is not None:
                        hook()
                    tb = tt // 4
                    pv = nextbank()
                    for kc in range(8):
                        B.op("pe", lambda e, kc=kc, pv=pv, tt=tt: e.matmul(ps[:, pv, :], lhsT=hT[:, kc, tt * 128:(tt + 1) * 128], rhs=wv_v[:, kc, :],
                                                                         start=(kc == 0), stop=(kc == 7)),
                             R=[T_hT[tb], RING_T[s_v]], W=[PSB[pv]], inc=(kc == 7))
                    B.op("dve", lambda e, pv=pv, tt=tt: e.tensor_copy(out=vaug[:, tt, :, 0:128], in_=ps[:, pv, :].rearrange("p (h e) -> p h e", h=4)),
                         R=[PSB[pv]], W=[T_vaug[tt]])

            B.op("dve", lambda e: e.memset(vaug[:, :, :, 128:129], 1.0), W=T_vaug + [T_xinB])
            s_g = wget()
            wv_g = slot_view(s_g, 8, 8)
            for tt in range(NT):
                tb = tt // 4
                pg = nextbank()
                for kc in range(8):
                    lhs = hT[:, kc, tt * 128:(tt + 1) * 128]
                    B.op("pe", lambda e, lhs=lhs, kc=kc, pg=pg: e.matmul(ps[:, pg, 0:8], lhsT=lhs, rhs=wv_g[:, kc, :], start=(kc == 0), stop=(kc == 7)),
                         R=[T_hT[tb], RING_T[s_g]], W=[PSB[pg]], inc=(kc == 7))
                B.op("dve", lambda e, pg=pg, tt=tt: e.tensor_tensor(out=gts[:, tt, :], in0=ps[:, pg, 0:8], in1=bif[:, tt, :], op=ALU.add),
                     R=[PSB[pg], T_bif], W=[T_gts])

            g3 = lambda i: gmath[:, i, :].rearrange("p (n h) -> p n h", n=16)
            B.op("act", lambda e: e.activation(out=g3(0), in_=gts[:, :, 4:8], func=AF.Exp, scale=-1.0), R=[T_gts], W=[T_gm])
            B.op("act", lambda e: e.activation(out=g3(1), in_=g3(0), func=AF.Ln, bias=onec, scale=1.0), R=[T_gm, T_small], W=[T_gm])
            B.op("act", lambda e: e.activation(out=g3(4), in_=gts[:, :, 0:4], func=AF.Exp), R=[T_gts], W=[T_gm])

            def gate_math_2():
                pbw, pbt = nextbank(), nextbank()
                B.op("pe", lambda e: e.matmul(ps[:, pbw, 0:64], lhsT=triu_f, rhs=gmath[:, 1, :], start=True, stop=True), R=[T_gm, T_cst], W=[PSB[pbw]])
                B.op("pe", lambda e: e.matmul(ps[:, pbt, 0:64], lhsT=ones_f, rhs=gmath[:, 1, :], start=True, stop=True), R=[T_gm, T_cst], W=[PSB[pbt]])
                B.op("dve", lambda e: e.tensor_copy(out=gmath[:, 6, :], in_=ps[:, pbt, 0:64]), R=[PSB[pbt]], W=[T_gm])
                B.op("dve", lambda e: e.tensor_tensor(out=gmath[:, 7, :], in0=ps[:, pbw, 0:64], in1=gmath[:, 6, :], op=ALU.subtract), R=[PSB[pbw], T_gm], W=[T_gm])
                B.op("act", lambda e: e.activation(out=gmath[:, 2, :], in_=gmath[:, 7, :], func=AF.Exp), R=[T_gm], W=[T_gm])
                B.op("act", lambda e: e.activation(out=gmath[:, 3, :], in_=gmath[:, 6, :], func=AF.Exp, scale=-1.0), R=[T_gm], W=[T_gm])
                B.op("dve", lambda e: e.scalar_tensor_tensor(out=gmath[:, 5, :], in0=gmath[:, 4, :], scalar=128.0 ** -0.5, in1=gmath[:, 2, :], op0=ALU.mult, op1=ALU.mult),
                     R=[T_gm], W=[T_gm])

            s_v = wget()
            tokmajor_v(s_v, gate_math_2)
            s_o = wget()
            wv_o = slot_view(s_o, 8, 512)
            for tt in range(NT):
                tb = tt // 4
                po = nextbank()
                for kc in range(8):
                    lhs = hT[:, kc, tt * 128:(tt + 1) * 128]
                    B.op("pe", lambda e, lhs=lhs, kc=kc, po=po: e.matmul(ps[:, po, :], lhsT=lhs, rhs=wv_o[:, kc, :], start=(kc == 0), stop=(kc == 7)),
                         R=[T_hT[tb], RING_T[s_o]], W=[PSB[po]], inc=(kc == 7))
                B.op("act", lambda e, po=po, tt=tt: e.activation(out=og[:, tt, :], in_=ps[:, po, :], func=AF.Sigmoid),
                     R=[PSB[po]], W=[T_og[tt], T_xin, T_xinC])

            def group_norm(src, Tsrc, gBs, dsts, Tdst, gate=None, Tgate=()):
                sq = stg32[:, 0, :].rearrange("p (h e) -> p h e", h=4)
                B.op("act", lambda e: e.activation(out=sq, in_=src, func=AF.Square), R=[Tsrc], W=[T_s32[0]])
                B.op("dve", lambda e: e.tensor_reduce(out=finc[:, 16:20], in_=sq, axis=AX.X, op=ALU.add), R=[T_s32[0]], W=[T_finc])
                B.op("act", lambda e: e.activation(out=finc[:, 20:24], in_=finc[:, 16:20], func=AF.Ln, bias=epsc, scale=1.0 / 128.0),
                     R=[T_finc, T_small], W=[T_finc])
                B.op("act", lambda e: e.activation(out=finc[:, 24:28], in_=finc[:, 20:24], func=AF.Exp, scale=-0.5), R=[T_finc], W=[T_finc])
                for i in range(4):
                    if gate is None:
                        B.op("dve", lambda e, i=i: e.scalar_tensor_tensor(
                            out=dsts[i], in0=src[:, i, :], scalar=finc[:, 24 + i:25 + i], in1=gBs[i], op0=ALU.mult, op1=ALU.mult),
                            R=[Tsrc, T_finc, T_small], W=Tdst)
                    else:
                        B.op("dve", lambda e, i=i: e.scalar_tensor_tensor(
                            out=stg32[:, 1, i * 128:(i + 1) * 128], in0=src[:, i, :], scalar=finc[:, 24 + i:25 + i], in1=gBs[i],
                            op0=ALU.mult, op1=ALU.mult), R=[Tsrc, T_finc, T_small], W=[T_s32[1]])
                if gate is not None:
                    B.op("dve", lambda e: e.tensor_tensor(out=dsts, in0=stg32[:, 1, :], in1=gate, op=ALU.mult),
                         R=[T_s32[1]] + list(Tgate), W=Tdst)

            for h in range(4):
                B.op("dve", lambda e, h=h: e.memset(cst32[:, h, :], 0.0), W=[T_C32[h]])
            s_qd = wget()
            s_kd = wget(la=2)
            s_vd = wget(la=1)
            wv_qd = [slot_view(s_qd, 8, 512), slot_view(s_kd, 8, 512)]
            wv_vd = slot_view(s_vd, 8, 512)
            ucnt = {"i": 0}

            def p2b_proj(cc, tb):
                g, cj = cc // 4, cc % 4
                s, wv = (s_qd, s_kd)[g], wv_qd[g]
                sl = slice(tb * 512, (tb + 1) * 512)
                pb = nextbank(0, 6)
                for kc in range(8):
                    B.op("pe", lambda e, kc=kc, pb=pb, sl=sl, wv=wv, cj=cj: e.matmul(
                        ps[:, pb, :], lhsT=wv[:, kc, cj * 128:(cj + 1) * 128], rhs=hT[:, kc, sl],
                        start=(kc == 0), stop=(kc == 7)), R=[T_hT[tb], RING_T[s]], W=[PSB[pb]], inc=(kc == 7))
                k = (ucnt["i"] % 2) * 2
                ucnt["i"] += 1
                B.op("dve", lambda e, k=k, pb=pb, sl=sl: e.tensor_tensor(out=stg[:, k, 0:512], in0=ps[:, pb, :], in1=cs[:, 1, sl], op=ALU.mult),
                     R=[PSB[pb], T_cs], W=[T_stg[k]])
                B.op("dve", lambda e, k=k, pb=pb, sl=sl: e.tensor_tensor(out=stg[:, k + 1, 0:512], in0=ps[:, pb, :], in1=cs[:, 0, sl], op=ALU.mult),
                     R=[PSB[pb], T_cs], W=[T_stg[k + 1]])
                return (cc, tb, k)

            def p2b_fin(cc, tb, k):
                sl = slice(tb * 512, (tb + 1) * 512)
                pb2 = nextbank(0, 6)
                B.op("pe", lambda e, k=k, pb2=pb2: e.matmul(ps[:, pb2, :], lhsT=perm_b, rhs=stg[:, k, 0:512], start=True, stop=False),
                     R=[T_stg[k], T_cst], W=[PSB[pb2]], inc=False)
                B.op("pe", lambda e, k=k, pb2=pb2: e.matmul(ps[:, pb2, :], lhsT=ident_b, rhs=stg[:, k + 1, 0:512], start=False, stop=True),
                     R=[T_stg[k + 1], T_cst], W=[PSB[pb2]], inc=True)
                B.op("act", lambda e, pb2=pb2, cc=cc, sl=sl: e.activation(out=qkT[:, cc, sl], in_=ps[:, pb2, :], func=AF.Copy),
                     R=[PSB[pb2]], W=[T_qk[cc][tb]])

            def vd_tile(tt):
                tb = tt // 4
                pv = nextbank(0, 6)
                for kc in range(8):
                    B.op("pe", lambda e, kc=kc, pv=pv, tt=tt: e.matmul(ps[:, pv, :], lhsT=hT[:, kc, tt * 128:(tt + 1) * 128], rhs=wv_vd[:, kc, :],
                                                                     start=(kc == 0), stop=(kc == 7)),
                         R=[T_hT[tb], RING_T[s_vd]], W=[PSB[pv]], inc=(kc == 7))
                B.op("act", lambda e, pv=pv, tt=tt: e.activation(out=vaug[:, tt, :, 0:128], in_=ps[:, pv, :].rearrange("p (h e) -> p h e", h=4), func=AF.Copy),
                     R=[PSB[pv]], W=[T_vaug[tt]])

            pending = []
            vw = fin[:].rearrange("p a b -> p (a b)").bitcast(BF16)[:, 0:516].rearrange("p (h e) -> p h e", h=4)
            ETq = et[:].rearrange("p a (b c) -> p (a b) c", c=128)
            T_etq = [T("etq%d" % i) for i in range(8)]

            def mmain(tt):
                tb = tt // 4
                tsl = slice(tt * 128, (tt + 1) * 128)
                ki = s = tt % 2
                pbk = nextbank(0, 6)
                psk = ps[:, pbk, :].bitcast(BF16)
                for h in range(4):
                    B.op("pe", lambda e, h=h: e.transpose(psk[:, h * 128:(h + 1) * 128], qkT[:, 4 + h, tsl], ident_b),
                         R=[T_qk[4 + h][tb], T_cst], W=[PSB[pbk]], inc=(h == 3))
                B.op("act", lambda e: e.activation(out=kwt[:, ki, :], in_=psk[:, 0:512], func=AF.Copy), R=[PSB[pbk]], W=[T_kwt[ki]])
                for h in range(4):
                    B.op("act", lambda e, h=h: e.activation(
                        out=vw[:, h, :], in_=vaug[:, tt, h, :], func=AF.Identity, scale=gmath[:, 5, tt * 4 + h:tt * 4 + h + 1]),
                        R=[T_vaug[tt], T_gm], W=[T_fin])
                for h in range(4):
                    dc = gmath[:, 3, tt * 4 + h:tt * 4 + h + 1]
                    B.op("act", lambda e, h=h, dc=dc: e.activation(out=cdb[:, h, 0:129], in_=cst32[:, h, 0:129], func=AF.Identity, scale=dc),
                         R=[T_C32[h], T_gm], W=[T_cdb[h]])
                pst = nextbank(0, 6)
                for h in range(4):
                    B.op("pe", lambda e, h=h: e.matmul(ps[:, pst, h * 128:(h + 1) * 128], lhsT=qkT[:, 4 + h, tsl], rhs=qkT[:, h, tsl], start=True, stop=True),
                         R=[T_qk[4 + h][tb], T_qk[h][tb]], W=[PSB[pst]], inc=(h == 3))
                B.op("dve", lambda e: e.tensor_tensor(
                    out=ETq[:, s * 4:s * 4 + 4, :], in0=ps[:, pst, :].rearrange("p (h t) -> p h t", h=4),
                    in1=triu_f.unsqueeze(1).broadcast_to([128, 4, 128]), op=ALU.mult),
                    R=[PSB[pst], T_cst], W=[T_etq[s * 4 + h_] for h_ in range(4)])
                return (tt, tb, tsl, ki, s)

            def mmain2(tt, tb, tsl, ki, s):
                pu = [nextbank(0, 6), nextbank(0, 6)]
                for h in range(4):
                    acc = ps[:, 6 + h // 2, (h % 2) * 130:(h % 2) * 130 + 129]
                    B.op("pe", lambda e, h=h, acc=acc: e.matmul(acc, lhsT=ETq[:, s * 4 + h, :], rhs=vw[:, h, :], start=True, stop=False),
                         R=[T_etq[s * 4 + h], T_fin], W=[PSB[6 + h // 2]], inc=False)
                    B.op("pe", lambda e, h=h, acc=acc: e.matmul(acc, lhsT=qkT[:, h, tsl], rhs=cdb[:, h, 0:129], start=False, stop=True),
                         R=[T_qk[h][tb], T_cdb[h]], W=[PSB[6 + h // 2]], inc=True)
                    B.op("pe", lambda e, h=h: e.matmul(ps[:, pu[h // 2], (h % 2) * 130:(h % 2) * 130 + 129], lhsT=kwt[:, ki, h * 128:(h + 1) * 128],
                                                       rhs=vw[:, h, :], start=True, stop=True),
                         R=[T_kwt[ki], T_fin], W=[PSB[pu[h // 2]]])
                for h in range(4):
                    dc = gmath[:, 3, tt * 4 + h:tt * 4 + h + 1]
                    B.op("dve", lambda e, h=h, dc=dc: e.scalar_tensor_tensor(
                        out=cst32[:, h, 0:129], in0=cst32[:, h, 0:129], scalar=dc, in1=ps[:, pu[h // 2], (h % 2) * 130:(h % 2) * 130 + 129],
                        op0=ALU.mult, op1=ALU.add), R=[T_C32[h], PSB[pu[h // 2]], T_gm], W=[T_C32[h]])
                for hp in range(2):
                    pv2 = ps[:, 6 + hp, 0:260].rearrange("p (a b) -> p a b", a=2)
                    B.op("act", lambda e, hp=hp, pv2=pv2: e.activation(
                        out=stg32b[:, s, hp * 256:(hp + 1) * 256].rearrange("p (a b) -> p a b", a=2), in_=pv2[:, :, 0:128], func=AF.Copy),
                        R=[PSB[6 + hp]], W=[T_s32b[s]])
                    B.op("act", lambda e, hp=hp, pv2=pv2: e.activation(
                        out=finc[:, 56 + 4 * s + 2 * hp:58 + 4 * s + 2 * hp], in_=pv2[:, :, 128], func=AF.Abs),
                        R=[PSB[6 + hp]], W=[T_finc])

            def mfin(tt):
                s = tt % 2
                nums = stg32b[:, s, :].rearrange("p (h e) -> p h e", h=4)
                for h in range(4):
                    B.op("act", lambda e, h=h: e.activation(out=stg32[:, 0, h * 128:(h + 1) * 128], in_=nums[:, h, :], func=AF.Square,
                                                            accum_out=finc[:, 16 + h:17 + h]), R=[T_s32b[s]], W=[T_s32[0], T_ss])
                B.op("dve", lambda e: e.tensor_tensor(out=finc[:, 4:8], in0=finc[:, 56 + 4 * s:60 + 4 * s], in1=gmath[:, 2, tt * 4:tt * 4 + 4], op=ALU.max),
                     R=[T_finc, T_gm], W=[T_finc])
                B.op("dve", lambda e: e.reciprocal(out=finc[:, 8:12], in_=finc[:, 4:8]), R=[T_finc], W=[T_finc])
                B.op("dve", lambda e: e.tensor_tensor(out=finc[:, 12:16], in0=finc[:, 8:12], in1=finc[:, 8:12], op=ALU.mult), R=[T_finc], W=[T_finc])
                B.op("dve", lambda e: e.tensor_tensor(out=finc[:, 20:24], in0=finc[:, 12:16], in1=finc[:, 16:20], op=ALU.mult), R=[T_finc, T_ss], W=[T_finc])
                B.op("act", lambda e: e.activation(out=finc[:, 24:28], in_=finc[:, 20:24], func=AF.Ln, bias=epsc, scale=1.0 / 128.0),
                     R=[T_finc, T_small], W=[T_finc])
                B.op("act", lambda e: e.activation(out=finc[:, 28:32], in_=finc[:, 24:28], func=AF.Exp, scale=-0.5), R=[T_finc], W=[T_finc])
                B.op("dve", lambda e: e.tensor_tensor(out=finc[:, 12:16], in0=finc[:, 28:32], in1=finc[:, 8:12], op=ALU.mult), R=[T_finc], W=[T_finc])
                for h in range(4):
                    B.op("pool", lambda e, h=h: e.tensor_scalar(
                        out=stg32[:, 1, h * 128:(h + 1) * 128], in0=nums[:, h, :], scalar1=finc[:, 12 + h:13 + h], scalar2=1.0,
                        op0=ALU.mult, op1=ALU.mult), R=[T_s32b[s], T_finc], W=[T_s32[1]])
                B.op("pool", lambda e: e.tensor_tensor(out=stg32[:, 1, :], in0=stg32[:, 1, :], in1=gmB[:], op=ALU.mult),
                     R=[T_s32[1], T_small], W=[T_s32[1]])
                B.op("pool", lambda e: e.tensor_tensor(out=mix[:, tt, 0:512], in0=stg32[:, 1, :], in1=og[:, tt, :], op=ALU.mult),
                     R=[T_s32[1], T_og[tt]], W=[T_mix[tt], T_xin, T_xinC])

            def next_units(n):
                out = []
                for _ in range(n):
                    if pending:
                        out.append(pending.pop(0))
                return out

            st = mmain(0)
            mmain2(*st)
            for tt in range(1, NT + 1):
                units = next_units(2)
                st = mmain(tt) if tt < NT else None
                hs = [p2b_proj(*u) for u in units]
                if st is not None:
                    mmain2(*st)
                vd_tile(tt - 1)
                for hnd in hs:
                    p2b_fin(*hnd)
                mfin(tt - 1)
                if (tt - 1) % 4 == 3:
                    pending.extend((cc, (tt - 1) // 4) for cc in range(8))
            while pending:
                units = next_units(2)
                hs = [p2b_proj(*u) for u in units]
                for hnd in hs:
                    p2b_fin(*hnd)
            if DEBUG == "hm":
                dbg_dump(mix, T_mix)
            if DEBUG == "qkd":
                dbg_dump(qkT, [t for row in T_qk for t in row])

            numsb = stg32b[:].rearrange("p a (q e) -> p (a q) e", q=4)
            nums_q = numsb.rearrange("p (c q) e -> p q c e", c=2)
            dens_q = finc[:, 32:40].rearrange("p (c q) -> p q c", c=2)
            blk = {"i": 0}
            gn_pending = []
            fin2 = stg32[:, 1, :].rearrange("p (q e) -> p q e", q=4)
            T_fin2 = T_s32[1]
            for h in range(4):
                for qb in range(NB):
                    nkt = 4 * qb + 4
                    if blk["i"] < 8:
                        adaln_piece_late(4 + blk["i"], 3)
                    blk["i"] += 1

                    def scores(kt, h=h, qb=qb):
                        c0 = max(0, kt * 128 - qb * 512)
                        diagk = kt >= 4 * qb
                        for c in range(2):
                            rows = slice(c * 64, (c + 1) * 64)
                            bank = (kt % 2) * 2 + c
                            B.op("pe", lambda e, bank=bank, c0=c0, rows=rows, kt=kt: e.matmul(
                                ps[:, bank, c0:512], lhsT=qkT[rows, 4 + h, kt * 128:(kt + 1) * 128], rhs=qkT[rows, h, qb * 512 + c0:(qb + 1) * 512],
                                start=True, stop=not diagk),
                                R=[T_qk[4 + h][kt // 4], T_qk[h][qb]], W=[PSB[bank]], inc=not diagk)
                        if diagk:
                            for c in range(2):
                                rows = slice(c * 64, (c + 1) * 64)
                                bank = (kt % 2) * 2 + c
                                for hh in range(2):
                                    B.op("pe", lambda e, bank=bank, c0=c0, rows=rows, hh=hh: e.matmul(
                                        ps[:, bank, c0:c0 + 128], lhsT=cmask[rows, hh, :], rhs=cmask[rows, 2 + hh, :],
                                        start=False, stop=(hh == 1)),
                                        R=[T_cst], W=[PSB[bank]], inc=(hh == 1))

                    def exps(kt, h=h, qb=qb):
                        c0 = max(0, kt * 128 - qb * 512)
                        for c in range(2):
                            bank = (kt % 2) * 2 + c
                            B.op("act", lambda e, bank=bank, c0=c0: e.activation(out=stg[:, bank, c0:512], in_=ps[:, bank, c0:512], func=AF.Exp, scale=0.125),
                                 R=[PSB[bank]], W=[T_stg[bank]])

                    def pv(kt, h=h, qb=qb):
                        for c in range(2):
                            bank = (kt % 2) * 2 + c
                            for qi in range(4):
                                qt = 4 * qb + qi
                                if qt < kt:
                                    continue
                                B.op("pe", lambda e, c=c, qi=qi, bank=bank, kt=kt: e.matmul(
                                    ps[:, 4 + qi, c * 256:c * 256 + 129], lhsT=stg[:, bank, qi * 128:(qi + 1) * 128], rhs=vaug[:, kt, h, :],
                                    start=(kt == 0 and c == 0), stop=(kt == 4 * qb + qi), skip_group_check=True),
                                    R=[T_stg[bank], T_vaug[kt]], W=[PSB[4 + qi]], inc=True)
                        if kt >= 4 * qb:
                            qi = kt - 4 * qb
                            pview = ps[:, 4 + qi, :].rearrange("p (c x) -> p c x", c=2)
                            B.op("dve", lambda e, qi=qi, pview=pview: e.tensor_copy(out=nums_q[:, qi], in_=pview[:, :, 0:128]),
                                 R=[PSB[4 + qi]], W=T_s32b)
                            B.op("dve", lambda e, qi=qi, pview=pview: e.tensor_copy(out=dens_q[:, qi], in_=pview[:, :, 128]),
                                 R=[PSB[4 + qi]], W=[T_finc])

                    scores(0)
                    for kt in range(nkt):
                        if kt + 1 < nkt:
                            scores(kt + 1)
                        exps(kt)
                        if kt == 1 and gn_pending:
                            gn_pending[0][0]()
                        if kt == 3 and gn_pending:
                            gn_pending.pop(0)[1]()
                        pv(kt)
                    B.op("dve", lambda e: e.reciprocal(out=finc[:, 40:48], in_=finc[:, 32:40]), R=[T_finc], W=[T_finc])
                    B.op("dve", lambda e: e.tensor_scalar(out=finc[:, 48:52], in0=finc[:, 44:48], scalar1=nlamc, scalar2=None, op0=ALU.mult),
                         R=[T_finc, T_small], W=[T_finc])
                    for qi in range(4):
                        B.op("dve", lambda e, qi=qi: e.tensor_scalar(out=fin[:, qi, :], in0=numsb[:, qi, :], scalar1=finc[:, 40 + qi:41 + qi],
                                                                   scalar2=None, op0=ALU.mult), R=T_s32b + [T_finc], W=[T_fin])
                    for qi in range(4):
                        B.op("dve", lambda e, qi=qi: e.scalar_tensor_tensor(out=fin[:, qi, :], in0=numsb[:, 4 + qi, :], scalar=finc[:, 48 + qi:49 + qi],
                                                                          in1=fin[:, qi, :], op0=ALU.mult, op1=ALU.add),
                             R=T_s32b + [T_finc, T_fin], W=[T_fin])
                    def _gnA():
                        sq = stg32[:, 0, :].rearrange("p (h e) -> p h e", h=4)
                        B.op("dve", lambda e: e.tensor_tensor(out=sq, in0=fin[:], in1=fin[:], op=ALU.mult), R=[T_fin], W=[T_s32[0]])
                        B.op("dve", lambda e: e.tensor_reduce(out=finc[:, 16:20], in_=sq, axis=AX.X, op=ALU.add), R=[T_s32[0]], W=[T_finc])

                    def _gnB(h=h, qb=qb):
                        gB = gdB[:, h * 128:(h + 1) * 128]
                        B.op("act", lambda e: e.activation(out=finc[:, 20:24], in_=finc[:, 16:20], func=AF.Ln, bias=epsc, scale=1.0 / 128.0),
                             R=[T_finc, T_small], W=[T_finc])
                        B.op("act", lambda e: e.activation(out=finc[:, 24:28], in_=finc[:, 20:24], func=AF.Exp, scale=-0.5), R=[T_finc], W=[T_finc])
                        for qi in range(4):
                            B.op("dve", lambda e, qi=qi: e.scalar_tensor_tensor(
                                out=mix[:, 4 * qb + qi, 512 + h * 128:512 + (h + 1) * 128], in0=fin[:, qi, :], scalar=finc[:, 24 + qi:25 + qi], in1=gB,
                                op0=ALU.mult, op1=ALU.mult), R=[T_fin, T_finc, T_small],
                                W=[T_mix[4 * qb + qi], T_og[4 * qb + qi], T_xin, T_xinC])
                    gn_pending.append((_gnA, _gnB))
            while gn_pending:
                a_, b_ = gn_pending.pop(0)
                a_()
                b_()
            B.op("dve", lambda e: e.scalar_tensor_tensor(out=A2, in0=sc_f, scalar=1.0, in1=gcols[:, 8:16], op0=ALU.add, op1=ALU.mult),
                 R=[T_small, T_mod2], W=[T_mod2])
            if DEBUG == "mix":
                dbg_dump(mix, T_mix)

            for tt in range(NT):
                tb = tt // 4
                pbk = nextbank()
                psk = ps[:, pbk, :].bitcast(BF16)
                for fc in range(8):
                    B.op("pe", lambda e, fc=fc, psk=psk, tt=tt: e.transpose(psk[:, fc * 128:(fc + 1) * 128], mix[:, tt, fc * 128:(fc + 1) * 128], ident_b),
                         R=[T_mix[tt], T_cst], W=[PSB[pbk]], inc=(fc == 7))
                if tt % 2 == 0:
                    B.op("act", lambda e, psk=psk, tt=tt: e.activation(out=mixT[:, :, tt * 128:(tt + 1) * 128], in_=psk.rearrange("p (c t) -> p c t", c=8), func=AF.Copy),
                         R=[PSB[pbk]], W=[T_mixT[tb]] + T_hT)
                else:
                    B.op("dve", lambda e, psk=psk, tt=tt: e.tensor_copy(out=mixT[:, :, tt * 128:(tt + 1) * 128], in_=psk.rearrange("p (c t) -> p c t", c=8)),
                         R=[PSB[pbk]], W=[T_mixT[tb]] + T_hT)
            s_w0 = wget()
            s_w1 = wget(la=2)
            first_x1 = True
            early_mod = []
            for tb in range(NB):
                sl = slice(tb * 512, (tb + 1) * 512)
                xb, Tx = (xin, T_xin) if tb % 2 == 0 else (xinC, T_xinC)
                B.dma("sp", xb, xT_d[:, :, sl], W=ALL_MIX)
                for fo in range(8):
                    s = s_w0 if fo < 4 else s_w1
                    wv = slot_view(s, 8, 512)
                    pb = nextbank()
                    for kc in range(8):
                        B.op("pe", lambda e, kc=kc, pb=pb, wv=wv, fo=fo, sl=sl: e.matmul(
                            ps[:, pb, :], lhsT=wv[:, kc, (fo % 4) * 128:(fo % 4 + 1) * 128], rhs=mixT[:, kc, sl], start=(kc == 0), stop=(kc == 7)),
                            R=[T_mixT[tb], RING_T[s]], W=[PSB[pb]], inc=(kc == 7))
                    Wl = [T_x1[fo][tb]] + (ALL_RC if first_x1 else [])
                    first_x1 = False
                    B.op("dve", lambda e, pb=pb, fo=fo, sl=sl, xb=xb: e.scalar_tensor_tensor(
                        out=x1T[:, fo, sl], in0=ps[:, pb, :], scalar=gt_a[:, fo:fo + 1], in1=xb[:, fo, :], op0=ALU.mult, op1=ALU.add),
                        R=[PSB[pb], T_mod2, Tx], W=Wl)
                if tb in (1, 2):
                    k0 = tb - 1
                    early_mod.append(norm_block(x1T[:, :, k0 * 512:(k0 + 1) * 512], [T_x1[c][k0] for c in range(8)],
                                                h2T[:, :, k0 * 512:(k0 + 1) * 512], [T_h2[k0]] + T_mixT + T_hT, A2, sh_f, 512, k0,
                                                T_small=T_mod2, defer=True))
            if DEBUG == "x1":
                dbg_dump(x1T, [t for row in T_x1 for t in row])

            def norm2(p, sbk):
                tb = p * 2 + sbk
                sl = slice(tb * 512, (tb + 1) * 512)
                norm_block(x1T[:, :, sl], [T_x1[c][tb] for c in range(8)], h2T[:, :, sbk * 512:(sbk + 1) * 512],
                           [T_h2[sbk]] + T_mixT + T_hT, A2, sh_f, 512, sbk, T_small=T_mod2)

            ostg = RA[:, 14336:16384].bitcast(F32).rearrange("p (a t) -> p a t", a=2)
            T_ostg = [T("ostg0"), T("ostg1")]
            sqb = [et[:, 0, :], et[:, 1, :], et[:, 2, :], kwt[:, 0, :]]
            T_sqb = [T_et[0], T_et[1], T_et[2], T_kwt[0]]
            SIDE_BANK = 7

            def staged_norm(tb, mode, sbk=None):
                sl = slice(tb * 512, (tb + 1) * 512)
                srcb = x1T[:, :, sl]
                Tsrc = [T_x1[c][tb] for c in range(8)]
                rs = stg32b[:, 0, :]

                def squares(f0):
                    for fc in range(f0, f0 + 4):
                        j = fc % 4
                        if fc % 2 == 0:
                            B.op("act", lambda e, fc=fc, j=j: e.activation(out=sqb[j], in_=srcb[:, fc, :], func=AF.Square), R=Tsrc, W=[T_sqb[j]])
                        else:
                            B.op("dve", lambda e, fc=fc, j=j: e.tensor_tensor(out=sqb[j], in0=srcb[:, fc, :], in1=srcb[:, fc, :], op=ALU.mult), R=Tsrc, W=[T_sqb[j]])

                def mms(f0):
                    for fc in range(f0, f0 + 4):
                        j = fc % 4
                        B.op("pe", lambda e, fc=fc, j=j: e.matmul(ps[:, SIDE_BANK, :], lhsT=ones_b, rhs=sqb[j], start=(fc == 0), stop=(fc == 7)),
                             R=[T_sqb[j], T_cst], W=[PSB[SIDE_BANK]], inc=True)

                squares(0)
                yield
                mms(0)
                squares(4)
                yield
                mms(4)
                B.op("act", lambda e: e.activation(out=rs, in_=ps[:, SIDE_BANK, :], func=AF.Ln, bias=epsc, scale=1.0 / D), R=[PSB[SIDE_BANK], T_small], W=[T_s32b[0]])
                B.op("act", lambda e: e.activation(out=rs, in_=rs, func=AF.Exp, scale=-0.5), R=[T_s32b[0]], W=[T_s32b[0]])
                yield
                for fc in range(8):
                    if mode == "out":
                        oj = fc % 2
                        B.op("dve", lambda e, fc=fc, oj=oj: e.scalar_tensor_tensor(
                            out=ostg[:, oj, :], in0=srcb[:, fc, :], scalar=g_fin[:, fc:fc + 1], in1=rs, op0=ALU.mult, op1=ALU.mult),
                            R=Tsrc + [T_s32b[0], T_small], W=[T_ostg[oj]])
                        B.dma("sp", out_d[:, fc, sl], ostg[:, oj, :], R=[T_ostg[oj]], is_out=True)
                        if fc < 7:
                            yield
                    else:
                        tmp = stg32b[:, 1, :]
                        B.op("dve", lambda e, fc=fc: e.scalar_tensor_tensor(
                            out=tmp, in0=srcb[:, fc, :], scalar=A2[:, fc:fc + 1], in1=rs, op0=ALU.mult, op1=ALU.mult),
                            R=Tsrc + [T_s32b[0], T_mod2], W=[T_s32b[1]])
                        B.op("act", lambda e, fc=fc: e.activation(out=h2T[:, fc, sbk * 512:(sbk + 1) * 512], in_=tmp, func=AF.Identity,
                                                                 bias=sh_f[:, fc:fc + 1], scale=1.0),
                             R=[T_s32b[1], T_mod2], W=[T_h2[sbk]])
                yield

            side = {"gens": []}

            def side_step():
                while side["gens"]:
                    try:
                        next(side["gens"][0])
                        return
                    except StopIteration:
                        side["gens"].pop(0)

            def final_big(tb):
                sl = slice(tb * 512, (tb + 1) * 512)
                ob = tb % 2
                T_oc = [T("oc%d" % fc) for fc in range(8)]
                modf = norm_block(x1T[:, :, sl], [T_x1[c][tb] for c in range(8)], outb[ob], [T_outb[ob]] + [t for row in T_act[0:16] for t in row],
                                  g_fin, None, 512, tb % 2, defer=True)
                rs_ = stg32[:, tb % 2, 0:512]
                first = True
                for fc in range(8):
                    B.op("dve", lambda e, fc=fc: e.scalar_tensor_tensor(
                        out=outb[ob][:, fc, :], in0=x1T[:, fc, sl], scalar=g_fin[:, fc:fc + 1], in1=rs_, op0=ALU.mult, op1=ALU.mult),
                        R=[T_x1[fc][tb], T_s32[tb % 2], T_small], W=[T_oc[fc]] + ([T_outb[ob]] + [t for row in T_act[0:16] for t in row] if first else []))
                    first = False
                    B.dma("sp", out_d[:, fc, sl], outb[ob][:, fc, :], R=[T_oc[fc]], is_out=True)

            T_outb = [T_xin, T_xinC]
            outb = [RBm[:, 0:8192].bitcast(F32).rearrange("p (c t) -> p c t", c=8),
                    RBm[:, 8192:16384].bitcast(F32).rearrange("p (c t) -> p c t", c=8)]

            for m_ in early_mod:
                m_()
            for p in range(2):
                for grp in range(11):
                    if p == 1 and grp == 1:
                        side["gens"] += [staged_norm(0, "out"), staged_norm(1, "out")]
                    s = wget()
                    wv = slot_view(s, 8, 512)
                    for pr in range(2):
                        i = grp * 2 + pr
                        if p == 1 and grp >= 1:
                            side_step()
                        chunks = (i, 22 + i)
                        dsets = [build_diag(cffn[:, ch, 0:3], 3, T_small) for ch in chunks]
                        for wi, ch in enumerate(chunks):
                            bi = (i % 2) * 2 + wi
                            B.op("dve", lambda e, bi=bi, ch=ch: e.tensor_copy(out=stg[:, bi, 0:2], in_=halo[:, ch, :]), R=[T_halo], W=[T_stg[bi]])
                            for sbk in range(2):
                                pb = nextbank(0, 7)
                                for kc in range(8):
                                    B.op("pe", lambda e, kc=kc, pb=pb, wv=wv, wi=wi, pr=pr, sbk=sbk: e.matmul(
                                        ps[:, pb, :], lhsT=wv[:, kc, wi * 256 + pr * 128:wi * 256 + (pr + 1) * 128],
                                        rhs=h2T[:, kc, sbk * 512:(sbk + 1) * 512], start=(kc == 0), stop=(kc == 7)),
                                        R=[T_h2[sbk], RING_T[s]], W=[PSB[pb]], inc=(kc == 7))
                                B.op("act", lambda e, bi=bi, pb=pb, sbk=sbk: e.activation(out=stg[:, bi, 2 + sbk * 512:2 + (sbk + 1) * 512], in_=ps[:, pb, :], func=AF.Copy),
                                     R=[PSB[pb]], W=[T_stg[bi]])
                            B.op("dve", lambda e, bi=bi, ch=ch: e.tensor_copy(out=halo[:, ch, :], in_=stg[:, bi, 1024:1026]), R=[T_stg[bi]], W=[T_halo])
                        ba, bg = (i % 2) * 2, (i % 2) * 2 + 1
                        pa2s, pg2s = [], []
                        for sbk in range(2):
                            pa2 = nextbank(0, 7)
                            pa2s.append(pa2)
                            for j in range(3):
                                B.op("pe", lambda e, j=j, pa2=pa2, sbk=sbk, ba=ba, d=dsets[0]: e.matmul(
                                    ps[:, pa2, :], lhsT=diag[:, d * 4 + j, :], rhs=stg[:, ba, sbk * 512 + j:sbk * 512 + j + 512], start=(j == 0), stop=(j == 2)),
                                    R=[T_stg[ba], T_diag[dsets[0]]], W=[PSB[pa2]], inc=(j == 2))
                        for sbk in range(2):
                            pg2 = nextbank(0, 7)
                            pg2s.append(pg2)
                            for j in range(3):
                                B.op("pe", lambda e, j=j, pg2=pg2, sbk=sbk, bg=bg, d=dsets[1]: e.matmul(
                                    ps[:, pg2, :], lhsT=diag[:, d * 4 + j, :], rhs=stg[:, bg, sbk * 512 + j:sbk * 512 + j + 512], start=(j == 0), stop=(j == 2)),
                                    R=[T_stg[bg], T_diag[dsets[1]]], W=[PSB[pg2]], inc=(j == 2))
                        for sbk in range(2):
                            pa2, pg2, si = pa2s[sbk], pg2s[sbk], sbk
                            B.op("act", lambda e, pg2=pg2, si=si, i=i: e.activation(out=stg32[:, si, :], in_=ps[:, pg2, :], func=AF.Silu, bias=cffn[:, 22 + i, 3:4], scale=1.0),
                                 R=[PSB[pg2], T_small], W=[T_s32[si]])
                            B.op("dve", lambda e, pa2=pa2, si=si, i=i, sbk=sbk: e.scalar_tensor_tensor(
                                out=actT(i)[:, sbk * 512:(sbk + 1) * 512], in0=ps[:, pa2, :], scalar=cffn[:, i, 3:4], in1=stg32[:, si, :], op0=ALU.add, op1=ALU.mult),
                                R=[PSB[pa2], T_s32[si], T_small], W=[T_act[i][sbk]] + (ALL_MIX if i < 16 else T_mixT + T_hT))
                def down(fo, sbks, s, hi=7):
                    wd = ring[:, s, 0:2816].rearrange("p (k n) -> p k n", k=22)
                    for sbk in sbks:
                        tb = p * 2 + sbk
                        sl = slice(tb * 512, (tb + 1) * 512)
                        pb = nextbank(0, hi)
                        for kc in range(22):
                            B.op("pe", lambda e, kc=kc, pb=pb, wd=wd, sbk=sbk: e.matmul(
                                ps[:, pb, :], lhsT=wd[:, kc, :], rhs=actT(kc)[:, sbk * 512:(sbk + 1) * 512], start=(kc == 0), stop=(kc == 21)),
                                R=[T_act[kc][sbk], RING_T[s]], W=[PSB[pb]], inc=(kc == 21))
                        B.op("dve", lambda e, pb=pb, fo=fo, sl=sl, tb=tb: e.scalar_tensor_tensor(
                            out=x1T[:, fo, sl], in0=ps[:, pb, :], scalar=gt_f[:, fo:fo + 1], in1=x1T[:, fo, sl], op0=ALU.mult, op1=ALU.add),
                            R=[PSB[pb], T_mod2, T_x1[fo][tb]], W=[T_x1[fo][tb]])

                if p == 0:
                    side["gens"] += [staged_norm(2, "ffn", 0), staged_norm(3, "ffn", 1)]
                    side_step()
                    for fo in range(8):
                        if fo < 7:
                            side_step()
                        down(fo, (0, 1), wget())
                else:
                    sl3 = slice(3 * 512, 4 * 512)

                    def last_sq(fo):
                        j = fo % 4
                        if fo % 2 == 0:
                            B.op("act", lambda e, fo=fo, j=j: e.activation(out=stg[:, j, 0:512], in_=x1T[:, fo, sl3], func=AF.Square), R=[T_x1[fo][3]], W=[T_stg[j]])
                        else:
                            B.op("dve", lambda e, fo=fo, j=j: e.tensor_tensor(out=stg[:, j, 0:512], in0=x1T[:, fo, sl3], in1=x1T[:, fo, sl3], op=ALU.mult), R=[T_x1[fo][3]], W=[T_stg[j]])

                    def last_mm(fo):
                        j = fo % 4
                        B.op("pe", lambda e, fo=fo, j=j: e.matmul(ps[:, 6, :], lhsT=ones_b, rhs=stg[:, j, 0:512], start=(fo == 0), stop=(fo == 7)),
                             R=[T_stg[j], T_cst], W=[PSB[6]], inc=True)

                    for sbk in range(2):
                        for fo in range(8):
                            down(fo, (sbk,), wget(), hi=(6 if sbk == 1 else 7))
                            if sbk == 1:
                                if fo == 0:
                                    side["gens"] += [staged_norm(2, "out")]
                                side_step()
                                if fo >= 1:
                                    last_mm(fo - 1)
                                last_sq(fo)
                    last_mm(7)
            if DEBUG == "x2":
                dbg_dump(x1T, [t for row in T_x1 for t in row])

            while side["gens"]:
                side_step()
            rs3 = stg32[:, 1, :]
            B.op("act", lambda e: e.activation(out=rs3, in_=ps[:, 6, :], func=AF.Ln, bias=epsc, scale=1.0 / D), R=[PSB[6], T_small], W=[T_s32[1]])
            B.op("act", lambda e: e.activation(out=rs3, in_=rs3, func=AF.Exp, scale=-0.5), R=[T_s32[1]], W=[T_s32[1]])
            T_oc = [T("oc%d" % fc) for fc in range(8)]
            for fc in range(8):
                B.op("dve", lambda e, fc=fc: e.scalar_tensor_tensor(
                    out=outb[1][:, fc, :], in0=x1T[:, fc, 3 * 512:4 * 512], scalar=g_fin[:, fc:fc + 1], in1=rs3, op0=ALU.mult, op1=ALU.mult),
                    R=[T_x1[fc][3], T_s32[1], T_small], W=[T_oc[fc]] + ([T_outb[1]] + [t for row in T_act[0:16] for t in row] if fc == 0 else []))
                B.dma("sp", out_d[:, fc, 3 * 512:4 * 512], outb[1][:, fc, :], R=[T_oc[fc]], is_out=True)

        try:
            body()
        except _Stop:
            pass
        B.finish()
        block = es.enter_context(nc.Block())
        B.emit(block)
    return nc


def _chunk_rows(w, kc):
    n = w.shape[1]
    return np.ascontiguousarray(w.reshape(kc, 128, n).transpose(1, 0, 2))


def _consts():
    cst = np.zeros((128, 5, 128), np.float32)
    cst[:, 0, :] = np.eye(128, dtype=np.float32)
    cst[:, 1, :] = np.triu(np.ones((128, 128), np.float32))
    cst[:, 2, :] = 1.0
    p = np.arange(128)
    perm = np.zeros((128, 128), np.float32)
    perm[p ^ 32, p] = 1.0
    cst[:, 3, :] = perm
    half = 32
    inv_freq = (np.float32(10000.0) ** (-np.arange(half, dtype=np.float32) / np.float32(half))).astype(np.float32)
    cst[:, 4, 0] = inv_freq[p % 32]
    cst[:, 4, 1] = np.where((p % 64) < 32, 1.0, -1.0)
    return cst


_NC_CACHE = {}


def kernel(x, c, positions, w_ada, b_ada, g_mix, w_in, conv_qk_w, conv_qk_b, b_if, g_mlstm,
           lam_q1, lam_k1, lam_q2, lam_k2, g_diff, w_out, g_ffn, w_up, conv_ffn_w, conv_ffn_b,
           w_down, g_final):
    f32 = lambda a: np.asarray(a, dtype=np.float32)
    x, c = f32(x), f32(c)
    positions = np.asarray(positions, dtype=np.int32)
    w_ada, b_ada, w_in, w_out, w_up, w_down = f32(w_ada)[0], f32(b_ada)[0], f32(w_in)[0], f32(w_out)[0], f32(w_up)[0], f32(w_down)[0]
    nb = x.shape[0]

    def pieces(w, cols_list):
        out = []
        for cols in cols_list:
            out.append(_chunk_rows(w[:, cols], 8).reshape(128, 4096))
        return np.ascontiguousarray(np.stack(out))

    wada_p = pieces(w_ada, [np.arange(i * 512, (i + 1) * 512) for i in range(12)])
    win_cols = [np.arange(0, 512), np.arange(512, 1024), np.arange(1024, 1536), np.arange(1536, 2048),
                np.arange(2056, 2568), np.arange(2568, 3080), np.arange(3080, 3592)]
    win_p = pieces(w_in, win_cols)
    wgate_p = np.ascontiguousarray(_chunk_rows(w_in[:, 2048:2056], 8).reshape(128, 64))
    wout_p = pieces(w_out, [np.arange(0, 512), np.arange(512, 1024)])
    up_cols = []
    for g in range(11):
        a = np.arange(g * 256, (g + 1) * 256)
        up_cols.append(np.concatenate([a, 2816 + a]))
    wup_p = pieces(w_up, up_cols)
    wdown_p = np.ascontiguousarray(np.stack([_chunk_rows(w_down[:, f * 128:(f + 1) * 128], 22).reshape(128, 2816) for f in range(8)]))
    col8 = lambda v: np.ascontiguousarray(f32(v).reshape(-1, 128).T)
    bada_p = col8(b_ada)
    gcols = np.ascontiguousarray(np.concatenate([col8(f32(g_mix)[0]), col8(f32(g_ffn)[0]), col8(f32(g_final))], axis=1))
    cqk = np.ascontiguousarray(np.concatenate([f32(conv_qk_w)[0].reshape(4, 8, 128).transpose(2, 1, 0),
                                               f32(conv_qk_b)[0].reshape(8, 128).T[:, :, None]], axis=2))
    cffn = np.ascontiguousarray(np.concatenate([f32(conv_ffn_w)[0].reshape(3, 44, 128).transpose(2, 1, 0),
                                                f32(conv_ffn_b)[0].reshape(44, 128).T[:, :, None]], axis=2))
    bif = np.ascontiguousarray(np.broadcast_to(f32(b_if)[0][None, None, :], (128, 16, 8)))
    gmB = np.ascontiguousarray(np.broadcast_to(f32(g_mlstm)[0][None, :], (128, 512)))
    gdB = np.ascontiguousarray(np.broadcast_to(f32(g_diff)[0][None, :], (128, 512)))
    lam = np.ascontiguousarray(np.broadcast_to(np.stack([f32(lam_q1)[0], f32(lam_k1)[0], f32(lam_q2)[0], f32(lam_k2)[0]])[None], (128, 4, 64)))
    cst = _consts()
    pp = np.arange(128)
    cmk = np.zeros((128, 4, 128), np.float32)
    for hh in range(2):
        cmk[pp, hh, (pp % 64) + 64 * hh] = 1.0
        cmk[:, 2 + hh, :] = np.where(((pp % 64) + 64 * hh)[:, None] > np.arange(128)[None, :], -30000.0, 0.0)

    in_maps = []
    for b in range(nb):
        xT = np.ascontiguousarray(x[b].T.reshape(8, 128, S).transpose(1, 0, 2))
        in_maps.append({
            "xT": xT, "c": np.ascontiguousarray(c[b].reshape(8, 128).T),
            "pos": np.ascontiguousarray(np.broadcast_to(positions[b][None, :], (128, S))),
            "w_ada": wada_p, "b_ada": bada_p, "gcols": gcols, "w_in": win_p, "w_gate": wgate_p, "cqk": cqk, "bif": bif,
            "gmB": gmB, "gdB": gdB, "lam": lam, "w_out": wout_p, "w_up": wup_p, "cffn": cffn, "w_down": wdown_p, "consts": cst, "cmask": cmk,
        })
    if "nc" not in _NC_CACHE:
        _NC_CACHE["nc"] = build_program()
    nc = _NC_CACHE["nc"]
    res = run_bass_kernel_spmd(nc, in_maps, core_ids=list(range(nb)))
    outs = []
    for b in range(nb):
        oT = np.asarray(res.results[b]["outT"], dtype=np.float32)
        outs.append(oT.transpose(1, 0, 2).reshape(D, S).T)
    out = np.ascontiguousarray(np.stack(outs)).astype(np.float32)
    if DEBUG:
        kernel.dbg = [np.asarray(res.results[b]["dbg"]) for b in range(nb)]
    return out
```

```python
import os
import math
import numpy as np
from contextlib import ExitStack
import concourse.bass as bass
import concourse.mybir as mybir
from concourse.bass_utils import run_bass_kernel_spmd

F32 = mybir.dt.float32
BF16 = mybir.dt.bfloat16
I32 = mybir.dt.int32
AF = mybir.ActivationFunctionType
ALU = mybir.AluOpType
AX = mybir.AxisListType

S = 2048
D = 1024
NT = 16
NB = 4
EPS = 1e-6
LAM_INIT = 0.8 - 0.6 * math.exp(-0.3 * 0)
PI = math.pi
TWO_PI = 2.0 * math.pi
CW1 = 6.28125
CW2 = TWO_PI - 6.28125
NSLOT = 4
LOOKAHEAD = 3
DEBUG = os.environ.get("MK_DEBUG", "")


class T:
    __slots__ = ("name", "w", "r", "excl", "wl")

    def __init__(self, name="", excl=False):
        self.name = name
        self.w = None
        self.r = {}
        self.wl = []
        self.excl = excl


class Builder:
    ENG = ("pe", "dve", "act", "pool", "sp")

    def __init__(self, nc, es):
        self.nc = nc
        self.sem = {k: es.enter_context(nc.semaphore("sem_" + k)) for k in self.ENG}
        self.cnt = {k: 0 for k in self.ENG}
        self.ops = {k: [] for k in self.ENG}
        self.known = {k: {} for k in self.ENG}
        self.NRING = 24
        self.ring = [es.enter_context(nc.semaphore("dma%d" % i)) for i in range(self.NRING)]
        self.ring_cnt = [0] * self.NRING
        self.ring_i = 0
        self.out_tokens = []
        self.swsem = []
        self._es = es

    def _semobj(self, key):
        if isinstance(key, tuple):
            return self.swsem[key[1]]
        return self.sem[key] if isinstance(key, str) else self.ring[key]

    def _need(self, eng, tok, waits, raw):
        if tok is None:
            return
        key, val = tok
        if key == eng and not raw and eng != "pool":
            return
        if self.known[eng].get(key, 0) >= val:
            return
        self.known[eng][key] = val
        waits.append((key, val))

    def _deps(self, eng, R, W, soft=False):
        waits = []
        for t in R:
            self._need(eng, t.w, waits, True)
            for tk in t.wl:
                self._need(eng, tk, waits, True)
            if t.excl:
                for k, v in t.r.items():
                    self._need(eng, (k, v), waits, False)
        for t in W:
            if not soft:
                self._need(eng, t.w, waits, False)
                for tk in t.wl:
                    self._need(eng, tk, waits, False)
            for k, v in t.r.items():
                self._need(eng, (k, v), waits, False)
        return waits

    def _commit(self, tok, R, W, soft=False):
        for t in W:
            if soft:
                t.wl.append(tok)
            else:
                t.w = tok
                t.wl = []
            t.r = {}
        k, v = tok
        for t in R:
            if t.r.get(k, 0) < v:
                t.r[k] = v

    def op(self, eng, fn, R=(), W=(), inc=True, soft=False):
        waits = self._deps(eng, R, W, soft)
        tok = (eng, self.cnt[eng] + 1)
        if inc:
            self.cnt[eng] += 1
        self._commit(tok, R, W, soft)
        self.ops[eng].append((waits, fn, (eng, 1) if inc else None))
        return tok

    def dma(self, q, out, in_, R=(), W=(), is_out=False, soft=False, **kw):
        waits = self._deps(q, R, W, soft)
        if q == "pool":
            n = len(self.swsem)
            self.swsem.append(self._es.enter_context(self.nc.semaphore("swdma%d" % n)))
            tok = (("sw", n), 16)
            self._commit(tok, R, W, soft)
            self.ops[q].append((waits, lambda e, out=out, in_=in_, kw=kw: e.dma_start(out=out, in_=in_, **kw), (tok[0], 16)))
            if is_out:
                self.out_tokens.append(tok)
            return tok
        i = self.ring_i
        self.ring_i = (self.ring_i + 1) % self.NRING
        if self.ring_cnt[i] > 0:
            self._need(q, (i, self.ring_cnt[i]), waits, True)
        self.ring_cnt[i] += 16
        tok = (i, self.ring_cnt[i])
        self._commit(tok, R, W, soft)
        self.ops[q].append((waits, lambda e, out=out, in_=in_, kw=kw: e.dma_start(out=out, in_=in_, **kw), (i, 16)))
        if is_out:
            self.out_tokens.append(tok)
        return tok

    def finish(self):
        waits = []
        for tok in self.out_tokens:
            self._need("sp", tok, waits, True)
        self.ops["sp"].append((waits, None, None))

    def emit(self, block):
        def run(eng_key):
            def body(e):
                for waits, fn, inc in self.ops[eng_key]:
                    for key, val in waits:
                        e.wait_ge(self._semobj(key), val)
                    if fn is None:
                        continue
                    ins = fn(e)
                    if inc is not None:
                        ins.then_inc(self._semobj(inc[0]), inc[1])
            return body
        block.tensor(run("pe"))
        block.vector(run("dve"))
        block.scalar(run("act"))
        block.gpsimd(run("pool"))
        block.sync(run("sp"))


def build_program(stop_after=None):
    nc = bass.Bass("TRN2", target_bir_lowering=False)
    dt_in = lambda name, shape, dt=F32: nc.dram_tensor(name, list(shape), dt, kind="ExternalInput").ap()
    xT_d = dt_in("xT", [128, 8, S])
    c_d = dt_in("c", [128, 8])
    pos_d = dt_in("pos", [128, S], I32)
    wada_d = dt_in("w_ada", [12, 128, 4096])
    bada_d = dt_in("b_ada", [128, 48])
    gcols_d = dt_in("gcols", [128, 24])
    win_d = dt_in("w_in", [7, 128, 4096])
    wgate_d = dt_in("w_gate", [128, 64])
    cqk_d = dt_in("cqk", [128, 8, 5])
    bif_d = dt_in("bif", [128, 16, 8])
    gmB_d = dt_in("gmB", [128, 512])
    gdB_d = dt_in("gdB", [128, 512])
    lam_d = dt_in("lam", [128, 4, 64])
    wout_d = dt_in("w_out", [2, 128, 4096])
    wup_d = dt_in("w_up", [11, 128, 4096])
    cffn_d = dt_in("cffn", [128, 44, 4])
    wdown_d = dt_in("w_down", [8, 128, 2816])
    cmask_d = dt_in("cmask", [128, 4, 128])
    consts_d = dt_in("consts", [128, 5, 128])
    out_d = nc.dram_tensor("outT", [128, 8, S], F32, kind="ExternalOutput").ap()
    dbg_d = nc.dram_tensor("dbg", [128, 8, S], F32, kind="ExternalOutput").ap() if DEBUG else None

    with ExitStack() as es:
        B = Builder(nc, es)
        sb = lambda name, shape, dt=F32: es.enter_context(nc.sbuf_tensor("s_" + name, list(shape), dt))

        ps = es.enter_context(nc.psum_tensor("ps", [128, 8, 512], F32))
        PSB = [T("ps%d" % i, excl=True) for i in range(8)]
        ring = sb("ring", [128, NSLOT, 4096], BF16)
        RING_T = [T("slot%d" % i) for i in range(NSLOT)]
        cst_f = sb("cst_f", [128, 5, 128], F32)
        cst_b = sb("cst_b", [128, 4, 128], BF16)
        smalls = sb("smalls", [128, 256], F32)
        RA = sb("RA", [128, 8 * S], BF16)
        RBm = sb("RBm", [128, 16 * 1024], BF16)
        RC = sb("RC", [128, 16512], F32)
        gts = sb("gts", [128, 16, 8], F32)
        gmath = sb("gmath", [128, 8, 64], F32)
        gmB = sb("gmB", [128, 512], F32)
        gdB = sb("gdB", [128, 512], F32)
        lamt = sb("lamt", [128, 4, 64], F32)
        cqk = sb("cqk", [128, 8, 5], F32)
        cffn = sb("cffn", [128, 44, 4], F32)
        bif = sb("bif", [128, 16, 8], F32)
        stg = sb("stg", [128, 4, 1040], BF16)
        stg32 = sb("stg32", [128, 2, 512], F32)
        stg32b = sb("stg32b", [128, 2, 512], F32)
        diag = sb("diag", [128, 16, 128], BF16)
        et = sb("et", [128, 3, 512], BF16)
        cst32 = sb("cst32", [128, 4, 130], F32)
        cdb = sb("cdb", [128, 4, 130], BF16)
        kwt = sb("kwt", [128, 2, 512], BF16)
        fin = sb("fin", [128, 4, 128], F32)
        finc = sb("finc", [128, 64], F32)
        halo = sb("halo", [128, 44, 2], BF16)
        cact_b = sb("cact_b", [128, 8], BF16)
        haloqk = sb("haloqk", [128, 8, 4], BF16)
        cmask = sb("cmask", [128, 4, 128], BF16)

        hT = RA[:].rearrange("p (c t) -> p c t", c=8)
        mixT = hT
        xin = RBm[:, 0:8192].bitcast(F32).rearrange("p (c t) -> p c t", c=8)
        mix = RBm[:].rearrange("p (n f) -> p n f", n=16)
        xinB = RC[:, 8192:12288].rearrange("p (c t) -> p c t", c=8)
        xinC = RBm[:, 8192:16384].bitcast(F32).rearrange("p (c t) -> p c t", c=8)
        xinD = RC[:, 4096:8192].rearrange("p (c t) -> p c t", c=8)
        og = mix[:, :, 512:1024]
        qkT = RC[:, 0:8192].bitcast(BF16).rearrange("p (c t) -> p c t", c=8)
        vaug = RC[:, 8192:8192 + 4128].bitcast(BF16).rearrange("p (n h e) -> p n h e", n=16, h=4)
        cs = RC[:, 12320:12320 + 4096].rearrange("p (k t) -> p k t", k=2)
        x1T = RC[:, 0:16384].rearrange("p (c t) -> p c t", c=8)
        h2T = RA[:, 0:8192].rearrange("p (c t) -> p c t", c=8)
        actA = RBm[:].rearrange("p (c t) -> p c t", c=16)
        actB = RA[:, 8192:8192 + 6144].rearrange("p (c t) -> p c t", c=6)

        def actT(i):
            return actA[:, i, :] if i < 16 else actB[:, i - 16, :]

        T_hT = [T("hT%d" % i) for i in range(NB)]
        T_xin = T("xin")
        T_xinB = T("xinB")
        T_xinC = T("xinC")
        T_xinD = T("xinD")
        T_mod2 = T("mod2")
        T_rs = T("ropescr")
        T_ss = T("ss")
        T_mix = [T("mix%d" % i) for i in range(NT)]
        T_og = [T("og%d" % i) for i in range(NT)]
        T_qk = [[T("qk%d_%d" % (c, b)) for b in range(NB)] for c in range(8)]
        T_vaug = [T("vaug%d" % i) for i in range(NT)]
        T_cs = T("cs")
        T_x1 = [[T("x1_%d_%d" % (c, b)) for b in range(NB)] for c in range(8)]
        T_cst = T("cst")
        T_small = T("small")
        T_bif = T("bif")
        T_gts = T("gts")
        T_gm = T("gmath")
        T_stg = [T("stg%d" % i) for i in range(4)]
        T_s32 = [T("s32_0"), T("s32_1")]
        T_s32b = [T("s32b_0"), T("s32b_1")]
        T_diag = [T("diag%d" % i) for i in range(4)]
        T_et = [T("et0"), T("et1"), T("et2")]
        T_C32 = [T("C32_%d" % h) for h in range(4)]
        T_cdb = [T("cdb_%d" % h) for h in range(4)]
        T_kwt = [T("kwt0"), T("kwt1")]
        T_fin = T("fin")
        T_finc = T("finc")
        T_halo = T("halo")
        T_hqk = [T("hqk%d" % i) for i in range(8)]
        T_mixT = [T("mixT%d" % i) for i in range(NB)]
        T_h2 = [T("h2_0"), T("h2_1")]
        T_act = [[T("act%d_%d" % (i, s)) for s in range(2)] for i in range(22)]
        ALL_RC = [t for row in T_qk for t in row] + T_vaug + [T_cs]
        ALL_MIX = T_mix + T_og + [T_xin, T_xinC]

        ident_b = cst_b[:, 0, :]
        triu_b = cst_b[:, 1, :]
        ones_b = cst_b[:, 2, :]
        perm_b = cst_b[:, 3, :]
        ident_f = cst_f[:, 0, :]
        triu_f = cst_f[:, 1, :]
        ones_f = cst_f[:, 2, :]
        invf = cst_f[:, 4, 0:1]
        sgn = cst_f[:, 4, 1:2]

        c_raw = smalls[:, 0:8]
        mod = smalls[:, 16:64]
        A1 = smalls[:, 64:72]
        A2 = smalls[:, 72:80]
        gcols = smalls[:, 80:104]
        bada = smalls[:, 104:152]
        epsc = smalls[:, 152:153]
        lamc = smalls[:, 153:154]
        nlamc = smalls[:, 154:155]
        onec = smalls[:, 155:156]
        ltmp = smalls[:, 156:160]
        sh_a, sc_a, gt_a = mod[:, 0:8], mod[:, 8:16], mod[:, 16:24]
        sh_f, sc_f, gt_f = mod[:, 24:32], mod[:, 32:40], mod[:, 40:48]
        g_fin = gcols[:, 16:24]

        rot = {"i": 0}

        def nextbank(lo=0, hi=8):
            b = lo + rot["i"] % (hi - lo)
            rot["i"] += 1
            return b

        WSEQ = ([(wada_d[i], 4096) for i in range(4)] +
                [(win_d[0], 4096), (win_d[1], 4096), (wgate_d, 64), (win_d[2], 4096), (win_d[3], 4096),
                 (win_d[4], 4096), (win_d[5], 4096), (win_d[6], 4096)] +
                [(wada_d[i], 4096) for i in range(4, 12)] +
                [(wout_d[0], 4096), (wout_d[1], 4096)])
        for _p in range(2):
            WSEQ += [(wup_d[g], 4096) for g in range(11)] + [(wdown_d[f], 2816) for f in range(8)] * (1 + _p)
        ws = {"issued": 0, "next": 0}

        def _issue_to(n):
            while ws["issued"] < min(n, len(WSEQ)):
                k = ws["issued"]
                src, nel = WSEQ[k]
                s = k % NSLOT
                a = 4 if nel == 4096 else (2 if nel == 2816 else 1)
                B.dma("pool", ring[:, s, 0:nel].rearrange("p (a n) -> p a n", a=a), src.rearrange("p (a n) -> p a n", a=a), W=[RING_T[s]],
                      R=([T_small, T_cst, T_rs, T_bif] if k == 0 else []))
                ws["issued"] += 1

        def wget(la=LOOKAHEAD):
            k = ws["next"]
            ws["next"] += 1
            _issue_to(k + 1 + la)
            return k % NSLOT

        def slot_view(s, kc, n):
            return ring[:, s, 0:kc * n].rearrange("p (k n) -> p k n", k=kc)

        class _Stop(Exception):
            pass

        def dbg_dump(ap, Ts):
            dd = dbg_d
            if tuple(ap.shape) == (128, 16, 1024):
                dd = dbg_d.rearrange("p c (a t) -> p (c a) t", a=2)
            B.dma("pool", dd, ap, R=Ts, is_out=True)
            raise _Stop()

        def body():
            B.dma("sp", RC[:, 0:2048].bitcast(I32), pos_d, W=[T_rs])
            B.dma("sp", cst_f[:], consts_d, W=[T_cst], soft=True)
            B.dma("pool", cmask[:], cmask_d, W=[T_cst], soft=True)
            B.dma("sp", c_raw, c_d, W=[T_small], soft=True)
            B.dma("sp", bada, bada_d, W=[T_small], soft=True)
            B.dma("sp", gcols, gcols_d, W=[T_small], soft=True)
            B.dma("sp", cqk[:], cqk_d, W=[T_small], soft=True)
            B.dma("sp", cffn[:], cffn_d, W=[T_small], soft=True)
            B.dma("sp", gmB[:], gmB_d, W=[T_small], soft=True)
            B.dma("sp", gdB[:], gdB_d, W=[T_small], soft=True)
            B.dma("sp", lamt[:], lam_d, W=[T_small], soft=True)
            B.dma("sp", bif[:], bif_d, W=[T_bif])
            B.op("dve", lambda e: e.tensor_copy(out=cst_b[:], in_=cst_f[:, 0:4, :]), R=[T_cst], W=[T_cst])
            B.op("dve", lambda e: e.memset(epsc, EPS), W=[T_small], soft=True)
            B.op("dve", lambda e: e.memset(onec, 1.0), W=[T_small], soft=True)
            B.op("dve", lambda e: e.memset(halo[:], 0.0), W=[T_halo])
            B.op("act", lambda e: e.activation(out=cact_b[:], in_=c_raw, func=AF.Silu), R=[T_small], W=[T_small])
            _issue_to(LOOKAHEAD)

            def rope_tables():
                sA = RC[:, 0:2048]
                sB = RC[:, 2048:4096]
                sC = RC[:, 4096:6144]
                sAi = sA.bitcast(I32)
                op = lambda fn, **kw: B.op("dve", fn, R=[T_rs, T_cst], W=[T_rs])
                op(lambda e: e.tensor_copy(out=sB, in_=sAi))
                op(lambda e: e.tensor_scalar(out=sB, in0=sB, scalar1=invf, scalar2=None, op0=ALU.mult))
                op(lambda e: e.tensor_scalar(out=sC, in0=sB, scalar1=1.0 / TWO_PI, scalar2=None, op0=ALU.mult))
                op(lambda e: e.tensor_copy(out=sAi, in_=sC))
                op(lambda e: e.tensor_copy(out=sC, in_=sAi))
                op(lambda e: e.scalar_tensor_tensor(out=sB, in0=sC, scalar=-TWO_PI, in1=sB, op0=ALU.mult, op1=ALU.add))
                op(lambda e: e.tensor_scalar(out=sC, in0=sB, scalar1=PI, scalar2=-TWO_PI, op0=ALU.is_gt, op1=ALU.mult))
                op(lambda e: e.tensor_tensor(out=sB, in0=sB, in1=sC, op=ALU.add))
                op(lambda e: e.tensor_scalar(out=sB, in0=sB, scalar1=-PI, scalar2=TWO_PI, op0=ALU.is_lt, op1=ALU.mult) if False else
                   e.tensor_scalar(out=sB, in0=sB, scalar1=PI, scalar2=-PI, op0=ALU.min, op1=ALU.max))
                B.op("act", lambda e: e.activation(out=cs[:, 1, :], in_=sB, func=AF.Sin, scale=sgn), R=[T_rs, T_cst], W=[T_cs])
                op(lambda e: e.tensor_scalar(out=sC, in0=sB, scalar1=PI / 2, scalar2=None, op0=ALU.add))
                sAf = sA
                op(lambda e: e.tensor_scalar(out=sAf, in0=sC, scalar1=PI, scalar2=-TWO_PI, op0=ALU.is_gt, op1=ALU.mult))
                op(lambda e: e.tensor_tensor(out=sC, in0=sC, in1=sAf, op=ALU.add))
                op(lambda e: e.tensor_scalar(out=sC, in0=sC, scalar1=PI, scalar2=-PI, op0=ALU.min, op1=ALU.max))
                B.op("act", lambda e: e.activation(out=cs[:, 0, :], in_=sC, func=AF.Sin), R=[T_rs], W=[T_cs])

            rope_tables()

            def adaln_cols(pieces, pb, col0=None):
                for piece in pieces:
                    s = wget()
                    wv = slot_view(s, 8, 512)
                    for jj in range(4):
                        j = (piece * 4 + jj) if col0 is None else (col0 + jj)
                        for kc in range(8):
                            B.op("pe", lambda e, wv=wv, jj=jj, kc=kc, j=j: e.matmul(
                                ps[:, pb, j:j + 1], lhsT=wv[:, kc, jj * 128:(jj + 1) * 128], rhs=cact_b[:, kc:kc + 1],
                                start=(kc == 0), stop=(kc == 7)),
                                R=[RING_T[s], T_small], W=[PSB[pb]], inc=(kc == 7))

            def adaln_piece_late(piece, pb):
                adaln_cols([piece], pb, col0=0)
                j0 = piece * 4
                B.op("act", lambda e: e.activation(out=mod[:, j0:j0 + 4], in_=ps[:, pb, 0:4], func=AF.Copy), R=[PSB[pb]], W=[T_mod2])
                B.op("dve", lambda e: e.tensor_tensor(out=mod[:, j0:j0 + 4], in0=mod[:, j0:j0 + 4], in1=bada[:, j0:j0 + 4], op=ALU.add),
                     R=[T_mod2, T_small], W=[T_mod2])

            pb_mod = nextbank()
            adaln_cols(range(4), pb_mod)
            B.op("dve", lambda e: e.tensor_tensor(out=mod[:, 0:16], in0=ps[:, pb_mod, 0:16], in1=bada[:, 0:16], op=ALU.add),
                 R=[PSB[pb_mod], T_small], W=[T_small])
            B.op("dve", lambda e: e.scalar_tensor_tensor(out=A1, in0=sc_a, scalar=1.0, in1=gcols[:, 0:8], op0=ALU.add, op1=ALU.mult),
                 R=[T_small], W=[T_small])
            B.op("dve", lambda e: e.tensor_tensor(out=stg32[:, 0, 0:64], in0=lamt[:, 0, :], in1=lamt[:, 1, :], op=ALU.mult), R=[T_small], W=[T_s32[0]])
            B.op("dve", lambda e: e.tensor_tensor(out=stg32[:, 0, 64:128], in0=lamt[:, 2, :], in1=lamt[:, 3, :], op=ALU.mult), R=[T_small], W=[T_s32[0]])
            B.op("dve", lambda e: e.tensor_reduce(out=ltmp[:, 0:2], in_=stg32[:, 0, 0:128].rearrange("p (a b) -> p a b", a=2), axis=AX.X, op=ALU.add),
                 R=[T_s32[0]], W=[T_small])
            B.op("act", lambda e: e.activation(out=ltmp[:, 2:4], in_=ltmp[:, 0:2], func=AF.Exp), R=[T_small], W=[T_small])
            B.op("dve", lambda e: e.scalar_tensor_tensor(out=lamc, in0=ltmp[:, 2:3], scalar=LAM_INIT, in1=ltmp[:, 3:4], op0=ALU.add, op1=ALU.subtract),
                 R=[T_small], W=[T_small])
            B.op("dve", lambda e: e.tensor_scalar(out=nlamc, in0=lamc, scalar1=-1.0, scalar2=None, op0=ALU.mult), R=[T_small], W=[T_small])
            B.op("dve", lambda e: e.tensor_scalar(out=gdB[:], in0=gdB[:], scalar1=(1.0 - LAM_INIT), scalar2=None, op0=ALU.mult), R=[T_small], W=[T_small])

            def norm_block(src, T_src, dst, T_dst, Acol, shcol, ncols, si, T_small=T_small, defer=False, presq=None):
                pb = nextbank()
                for fc in range(8):
                    if presq is not None:
                        qa, qT = presq[fc]
                        B.op("pe", lambda e, fc=fc, qa=qa: e.matmul(ps[:, pb, 0:ncols], lhsT=ones_b, rhs=qa, start=(fc == 0), stop=(fc == 7)),
                             R=[qT, T_cst], W=[PSB[pb]], inc=True)
                        continue
                    sj = 2 * si + fc % 2
                    if fc % 2 == 0:
                        B.op("act", lambda e, fc=fc, sj=sj: e.activation(out=stg[:, sj, 0:ncols], in_=src[:, fc, :], func=AF.Square),
                             R=T_src, W=[T_stg[sj]])
                    else:
                        B.op("dve", lambda e, fc=fc, sj=sj: e.tensor_tensor(out=stg[:, sj, 0:ncols], in0=src[:, fc, :], in1=src[:, fc, :], op=ALU.mult),
                             R=T_src, W=[T_stg[sj]])
                    B.op("pe", lambda e, fc=fc, sj=sj: e.matmul(ps[:, pb, 0:ncols], lhsT=ones_b, rhs=stg[:, sj, 0:ncols],
                                                              start=(fc == 0), stop=(fc == 7)),
                         R=[T_stg[sj], T_cst], W=[PSB[pb]], inc=True)
                rs = stg32[:, si, 0:ncols]
                B.op("act", lambda e: e.activation(out=rs, in_=ps[:, pb, 0:ncols], func=AF.Ln, bias=epsc, scale=1.0 / D),
                     R=[PSB[pb], T_small], W=[T_s32[si]])
                B.op("act", lambda e: e.activation(out=rs, in_=rs, func=AF.Exp, scale=-0.5), R=[T_s32[si]], W=[T_s32[si]])
                if defer:
                    return lambda: norm_mod(src, T_src, dst, T_dst, Acol, shcol, ncols, si, T_small)
                norm_mod(src, T_src, dst, T_dst, Acol, shcol, ncols, si, T_small)

            def norm_mod(src, T_src, dst, T_dst, Acol, shcol, ncols, si, T_small):
                rs = stg32[:, si, 0:ncols]
                for fc in range(8):
                    if shcol is not None:
                        tmp = stg32b[:, fc % 2, 0:ncols]
                        B.op("dve", lambda e, fc=fc, tmp=tmp: e.scalar_tensor_tensor(
                            out=tmp, in0=src[:, fc, :], scalar=Acol[:, fc:fc + 1], in1=rs, op0=ALU.mult, op1=ALU.mult),
                            R=list(T_src) + [T_s32[si], T_small], W=[T_s32b[fc % 2]])
                        B.op("act", lambda e, fc=fc, tmp=tmp: e.activation(out=dst[:, fc, :], in_=tmp, func=AF.Identity,
                                                                         bias=shcol[:, fc:fc + 1], scale=1.0),
                             R=[T_s32b[fc % 2], T_small], W=T_dst)
                    else:
                        B.op("dve", lambda e, fc=fc: e.scalar_tensor_tensor(
                            out=dst[:, fc, :], in0=src[:, fc, :], scalar=Acol[:, fc:fc + 1], in1=rs, op0=ALU.mult, op1=ALU.mult),
                            R=list(T_src) + [T_s32[si], T_small], W=T_dst)

            dg = {"i": 0}

            def build_diag(wcols, ntap, Tw):
                d = dg["i"] % 4
                dg["i"] += 1
                for j in range(ntap):
                    B.op("dve", lambda e, j=j, d=d: e.tensor_scalar(out=diag[:, d * 4 + j, :], in0=ident_f,
                                                                 scalar1=wcols[:, j:j + 1], scalar2=None, op0=ALU.mult),
                         R=[T_cst, Tw], W=[T_diag[d]])
                return d

            p2a = {"prev": None, "n": 0, "slots": None}

            def p2a_proj(cc, tb):
                if p2a["slots"] is None:
                    sq_ = wget()
                    sk_ = wget(la=2)
                    p2a["slots"] = (sq_, sk_)
                s = p2a["slots"][cc // 4]
                wv = slot_view(s, 8, 512)
                cj = cc % 4
                d = build_diag(cqk[:, cc, 0:4], 4, T_small)
                pb = nextbank()
                for kc in range(8):
                    B.op("pe", lambda e, kc=kc, pb=pb, tb=tb, wv=wv, cj=cj: e.matmul(
                        ps[:, pb, :], lhsT=wv[:, kc, cj * 128:(cj + 1) * 128], rhs=hT[:, kc, tb * 512:(tb + 1) * 512],
                        start=(kc == 0), stop=(kc == 7)), R=[T_hT[tb], RING_T[s]], W=[PSB[pb]], inc=(kc == 7))
                si = p2a["n"] % 2
                p2a["n"] += 1
                if tb == 0:
                    B.op("dve", lambda e, si=si: e.memset(stg[:, si, 0:3], 0.0), W=[T_stg[si]])
                else:
                    B.op("dve", lambda e, si=si, cc=cc: e.tensor_copy(out=stg[:, si, 0:3], in_=haloqk[:, cc, 0:3]), R=[T_hqk[cc]], W=[T_stg[si]])
                B.op("act", lambda e, si=si, pb=pb: e.activation(out=stg[:, si, 3:515], in_=ps[:, pb, :], func=AF.Copy),
                     R=[PSB[pb]], W=[T_stg[si]])
                B.op("dve", lambda e, si=si, cc=cc: e.tensor_copy(out=haloqk[:, cc, 0:3], in_=stg[:, si, 512:515]), R=[T_stg[si]], W=[T_hqk[cc]])
                return (cc, tb, si, d)

            def p2a_conv(cc, tb, si, d):
                pb2 = nextbank()
                for j in range(4):
                    B.op("pe", lambda e, j=j, si=si, pb2=pb2, d=d: e.matmul(
                        ps[:, pb2, :], lhsT=diag[:, d * 4 + j, :], rhs=stg[:, si, j:j + 512], start=(j == 0), stop=(j == 3)),
                        R=[T_stg[si], T_diag[d]], W=[PSB[pb2]], inc=(j == 3))
                B.op("act", lambda e, pb2=pb2, cc=cc, tb=tb: e.activation(
                    out=qkT[:, cc, tb * 512:(tb + 1) * 512], in_=ps[:, pb2, :], func=AF.Silu, bias=cqk[:, cc, 4:5], scale=1.0),
                    R=[PSB[pb2], T_small], W=[T_qk[cc][tb]] + ([T_rs] if cc < 6 else []))

            def p2a_push(cc, tb):
                h_ = p2a_proj(cc, tb)
                if p2a["prev"] is not None:
                    p2a_conv(*p2a["prev"])
                p2a["prev"] = h_

            def p2a_flush():
                if p2a["prev"] is not None:
                    p2a_conv(*p2a["prev"])
                    p2a["prev"] = None

            prev_mod = None
            for tb in range(NB):
                xb, Tx = [(xin, T_xin), (xinB, T_xinB), (xinC, T_xinC), (xin, T_xin)][tb]
                B.dma("sp", xb, xT_d[:, :, tb * 512:(tb + 1) * 512], W=[Tx], R=([RING_T[3]] if tb == 2 else []))
                m = norm_block(xb, [Tx], hT[:, :, tb * 512:(tb + 1) * 512], [T_hT[tb]], A1, sh_a, 512, tb % 2, defer=True)
                if tb >= 2:
                    for cc in range(4 * (tb - 2), 4 * (tb - 2) + 4):
                        p2a_push(cc, 0)
                if prev_mod is not None:
                    prev_mod()
                prev_mod = m
            for cc in range(4):
                p2a_push(cc, 1)
            prev_mod()
            for cc in range(4, 8):
                p2a_push(cc, 1)
            for tb in (2, 3):
                for cc in range(8):
                    p2a_push(cc, tb)
            p2a_flush()
            if DEBUG == "hT":
                dbg_dump(hT, T_hT)

            if DEBUG == "qkm":
                dbg_dump(qkT, [t for row in T_qk for t in row])

            def tokmajor_v(s_v, hook):
                wv_v = slot_view(s_v, 8, 512)
                for tt in range(NT):
                    if tt == 8 and hook is not None:
                        hook()
                    tb = tt // 4
                    pv = nextbank()
                    for kc in range(8):
                        B.op("pe", lambda e, kc=kc, pv=pv, tt=tt: e.matmul(ps[:, pv, :], lhsT=hT[:, kc, tt * 128:(tt + 1) * 128], rhs=wv_v[:, kc, :],
                                                                         start=(kc == 0), stop=(kc == 7)),
                             R=[T_hT[tb], RING_T[s_v]], W=[PSB[pv]], inc=(kc == 7))
                    B.op("dve", lambda e, pv=pv, tt=tt: e.tensor_copy(out=vaug[:, tt, :, 0:128], in_=ps[:, pv, :].rearrange("p (h e) -> p h e", h=4)),
                         R=[PSB[pv]], W=[T_vaug[tt]])

            B.op("dve", lambda e: e.memset(vaug[:, :, :, 128:129], 1.0), W=T_vaug + [T_xinB])
            s_g = wget()
            wv_g = slot_view(s_g, 8, 8)
            for tt in range(NT):
                tb = tt // 4
                pg = nextbank()
                for kc in range(8):
                    lhs = hT[:, kc, tt * 128:(tt + 1) * 128]
                    B.op("pe", lambda e, lhs=lhs, kc=kc, pg=pg: e.matmul(ps[:, pg, 0:8], lhsT=lhs, rhs=wv_g[:, kc, :], start=(kc == 0), stop=(kc == 7)),
                         R=[T_hT[tb], RING_T[s_g]], W=[PSB[pg]], inc=(kc == 7))
                B.op("dve", lambda e, pg=pg, tt=tt: e.tensor_tensor(out=gts[:, tt, :], in0=ps[:, pg, 0:8], in1=bif[:, tt, :], op=ALU.add),
                     R=[PSB[pg], T_bif], W=[T_gts])

            g3 = lambda i: gmath[:, i, :].rearrange("p (n h) -> p n h", n=16)
            B.op("act", lambda e: e.activation(out=g3(0), in_=gts[:, :, 4:8], func=AF.Exp, scale=-1.0), R=[T_gts], W=[T_gm])
            B.op("act", lambda e: e.activation(out=g3(1), in_=g3(0), func=AF.Ln, bias=onec, scale=1.0), R=[T_gm, T_small], W=[T_gm])
            B.op("act", lambda e: e.activation(out=g3(4), in_=gts[:, :, 0:4], func=AF.Exp), R=[T_gts], W=[T_gm])

            def gate_math_2():
                pbw, pbt = nextbank(), nextbank()
                B.op("pe", lambda e: e.matmul(ps[:, pbw, 0:64], lhsT=triu_f, rhs=gmath[:, 1, :], start=True, stop=True), R=[T_gm, T_cst], W=[PSB[pbw]])
                B.op("pe", lambda e: e.matmul(ps[:, pbt, 0:64], lhsT=ones_f, rhs=gmath[:, 1, :], start=True, stop=True), R=[T_gm, T_cst], W=[PSB[pbt]])
                B.op("dve", lambda e: e.tensor_copy(out=gmath[:, 6, :], in_=ps[:, pbt, 0:64]), R=[PSB[pbt]], W=[T_gm])
                B.op("dve", lambda e: e.tensor_tensor(out=gmath[:, 7, :], in0=ps[:, pbw, 0:64], in1=gmath[:, 6, :], op=ALU.subtract), R=[PSB[pbw], T_gm], W=[T_gm])
                B.op("act", lambda e: e.activation(out=gmath[:, 2, :], in_=gmath[:, 7, :], func=AF.Exp), R=[T_gm], W=[T_gm])
                B.op("act", lambda e: e.activation(out=gmath[:, 3, :], in_=gmath[:, 6, :], func=AF.Exp, scale=-1.0), R=[T_gm], W=[T_gm])
                B.op("dve", lambda e: e.scalar_tensor_tensor(out=gmath[:, 5, :], in0=gmath[:, 4, :], scalar=128.0 ** -0.5, in1=gmath[:, 2, :], op0=ALU.mult, op1=ALU.mult),
                     R=[T_gm], W=[T_gm])

            s_v = wget()
            tokmajor_v(s_v, gate_math_2)
            s_o = wget()
            wv_o = slot_view(s_o, 8, 512)
            for tt in range(NT):
                tb = tt // 4
                po = nextbank()
                for kc in range(8):
                    lhs = hT[:, kc, tt * 128:(tt + 1) * 128]
                    B.op("pe", lambda e, lhs=lhs, kc=kc, po=po: e.matmul(ps[:, po, :], lhsT=lhs, rhs=wv_o[:, kc, :], start=(kc == 0), stop=(kc == 7)),
                         R=[T_hT[tb], RING_T[s_o]], W=[PSB[po]], inc=(kc == 7))
                B.op("act", lambda e, po=po, tt=tt: e.activation(out=og[:, tt, :], in_=ps[:, po, :], func=AF.Sigmoid),
                     R=[PSB[po]], W=[T_og[tt], T_xin, T_xinC])

            def group_norm(src, Tsrc, gBs, dsts, Tdst, gate=None, Tgate=()):
                sq = stg32[:, 0, :].rearrange("p (h e) -> p h e", h=4)
                B.op("act", lambda e: e.activation(out=sq, in_=src, func=AF.Square), R=[Tsrc], W=[T_s32[0]])
                B.op("dve", lambda e: e.tensor_reduce(out=finc[:, 16:20], in_=sq, axis=AX.X, op=ALU.add), R=[T_s32[0]], W=[T_finc])
                B.op("act", lambda e: e.activation(out=finc[:, 20:24], in_=finc[:, 16:20], func=AF.Ln, bias=epsc, scale=1.0 / 128.0),
                     R=[T_finc, T_small], W=[T_finc])
                B.op("act", lambda e: e.activation(out=finc[:, 24:28], in_=finc[:, 20:24], func=AF.Exp, scale=-0.5), R=[T_finc], W=[T_finc])
                for i in range(4):
                    if gate is None:
                        B.op("dve", lambda e, i=i: e.scalar_tensor_tensor(
                            out=dsts[i], in0=src[:, i, :], scalar=finc[:, 24 + i:25 + i], in1=gBs[i], op0=ALU.mult, op1=ALU.mult),
                            R=[Tsrc, T_finc, T_small], W=Tdst)
                    else:
                        B.op("dve", lambda e, i=i: e.scalar_tensor_tensor(
                            out=stg32[:, 1, i * 128:(i + 1) * 128], in0=src[:, i, :], scalar=finc[:, 24 + i:25 + i], in1=gBs[i],
                            op0=ALU.mult, op1=ALU.mult), R=[Tsrc, T_finc, T_small], W=[T_s32[1]])
                if gate is not None:
                    B.op("dve", lambda e: e.tensor_tensor(out=dsts, in0=stg32[:, 1, :], in1=gate, op=ALU.mult),
                         R=[T_s32[1]] + list(Tgate), W=Tdst)

            for h in range(4):
                B.op("dve", lambda e, h=h: e.memset(cst32[:, h, :], 0.0), W=[T_C32[h]])
            s_qd = wget()
            s_kd = wget(la=2)
            s_vd = wget(la=1)
            wv_qd = [slot_view(s_qd, 8, 512), slot_view(s_kd, 8, 512)]
            wv_vd = slot_view(s_vd, 8, 512)
            ucnt = {"i": 0}

            def p2b_proj(cc, tb):
                g, cj = cc // 4, cc % 4
                s, wv = (s_qd, s_kd)[g], wv_qd[g]
                sl = slice(tb * 512, (tb + 1) * 512)
                pb = nextbank(0, 6)
                for kc in range(8):
                    B.op("pe", lambda e, kc=kc, pb=pb, sl=sl, wv=wv, cj=cj: e.matmul(
                        ps[:, pb, :], lhsT=wv[:, kc, cj * 128:(cj + 1) * 128], rhs=hT[:, kc, sl],
                        start=(kc == 0), stop=(kc == 7)), R=[T_hT[tb], RING_T[s]], W=[PSB[pb]], inc=(kc == 7))
                k = (ucnt["i"] % 2) * 2
                ucnt["i"] += 1
                B.op("dve", lambda e, k=k, pb=pb, sl=sl: e.tensor_tensor(out=stg[:, k, 0:512], in0=ps[:, pb, :], in1=cs[:, 1, sl], op=ALU.mult),
                     R=[PSB[pb], T_cs], W=[T_stg[k]])
                B.op("dve", lambda e, k=k, pb=pb, sl=sl: e.tensor_tensor(out=stg[:, k + 1, 0:512], in0=ps[:, pb, :], in1=cs[:, 0, sl], op=ALU.mult),
                     R=[PSB[pb], T_cs], W=[T_stg[k + 1]])
                return (cc, tb, k)

            def p2b_fin(cc, tb, k):
                sl = slice(tb * 512, (tb + 1) * 512)
                pb2 = nextbank(0, 6)
                B.op("pe", lambda e, k=k, pb2=pb2: e.matmul(ps[:, pb2, :], lhsT=perm_b, rhs=stg[:, k, 0:512], start=True, stop=False),
                     R=[T_stg[k], T_cst], W=[PSB[pb2]], inc=False)
                B.op("pe", lambda e, k=k, pb2=pb2: e.matmul(ps[:, pb2, :], lhsT=ident_b, rhs=stg[:, k + 1, 0:512], start=False, stop=True),
                     R=[T_stg[k + 1], T_cst], W=[PSB[pb2]], inc=True)
                B.op("act", lambda e, pb2=pb2, cc=cc, sl=sl: e.activation(out=qkT[:, cc, sl], in_=ps[:, pb2, :], func=AF.Copy),
                     R=[PSB[pb2]], W=[T_qk[cc][tb]])

            def vd_tile(tt):
                tb = tt // 4
                pv = nextbank(0, 6)
                for kc in range(8):
                    B.op("pe", lambda e, kc=kc, pv=pv, tt=tt: e.matmul(ps[:, pv, :], lhsT=hT[:, kc, tt * 128:(tt + 1) * 128], rhs=wv_vd[:, kc, :],
                                                                     start=(kc == 0), stop=(kc == 7)),
                         R=[T_hT[tb], RING_T[s_vd]], W=[PSB[pv]], inc=(kc == 7))
                B.op("act", lambda e, pv=pv, tt=tt: e.activation(out=vaug[:, tt, :, 0:128], in_=ps[:, pv, :].rearrange("p (h e) -> p h e", h=4), func=AF.Copy),
                     R=[PSB[pv]], W=[T_vaug[tt]])

            pending = []
            ETq = et[:].rearrange("p a (b c) -> p (a b) c", c=128)
            T_etq = [T("etq%d" % i) for i in range(8)]

            def mmain(tt):
                tb = tt // 4
                tsl = slice(tt * 128, (tt + 1) * 128)
                ki = s = tt % 2
                pbk = nextbank(0, 6)
                psk = ps[:, pbk, :].bitcast(BF16)
                for h in range(4):
                    B.op("pe", lambda e, h=h: e.transpose(psk[:, h * 128:(h + 1) * 128], qkT[:, 4 + h, tsl], ident_b),
                         R=[T_qk[4 + h][tb], T_cst], W=[PSB[pbk]], inc=(h == 3))
                for h in range(4):
                    B.op("act", lambda e, h=h: e.activation(
                        out=kwt[:, ki, h * 128:(h + 1) * 128], in_=psk[:, h * 128:(h + 1) * 128], func=AF.Identity,
                        scale=gmath[:, 5, tt * 4 + h:tt * 4 + h + 1]),
                        R=[PSB[pbk], T_gm], W=[T_kwt[ki]])
                for h in range(4):
                    dc = gmath[:, 3, tt * 4 + h:tt * 4 + h + 1]
                    B.op("act", lambda e, h=h, dc=dc: e.activation(out=cdb[:, h, 0:129], in_=cst32[:, h, 0:129], func=AF.Identity, scale=dc),
                         R=[T_C32[h], T_gm], W=[T_cdb[h]])
                pst = nextbank(0, 6)
                for h in range(4):
                    B.op("pe", lambda e, h=h: e.matmul(ps[:, pst, h * 128:(h + 1) * 128], lhsT=qkT[:, 4 + h, tsl], rhs=qkT[:, h, tsl], start=True, stop=True),
                         R=[T_qk[4 + h][tb], T_qk[h][tb]], W=[PSB[pst]], inc=(h == 3))
                for h in range(4):
                    wc = gmath[:, 5, tt * 4 + h:tt * 4 + h + 1]
                    B.op("dve", lambda e, h=h, wc=wc: e.scalar_tensor_tensor(
                        out=ETq[:, s * 4 + h, :], in0=ps[:, pst, h * 128:(h + 1) * 128], scalar=wc, in1=triu_f, op0=ALU.mult, op1=ALU.mult),
                        R=[PSB[pst], T_gm, T_cst], W=[T_etq[s * 4 + h]])
                return (tt, tb, tsl, ki, s)

            def mmain2(tt, tb, tsl, ki, s):
                pu = [nextbank(0, 6), nextbank(0, 6)]
                for h in range(4):
                    acc = ps[:, 6 + h // 2, (h % 2) * 130:(h % 2) * 130 + 129]
                    B.op("pe", lambda e, h=h, acc=acc: e.matmul(acc, lhsT=ETq[:, s * 4 + h, :], rhs=vaug[:, tt, h, :], start=True, stop=False),
                         R=[T_etq[s * 4 + h], T_vaug[tt]], W=[PSB[6 + h // 2]], inc=False)
                    B.op("pe", lambda e, h=h, acc=acc: e.matmul(acc, lhsT=qkT[:, h, tsl], rhs=cdb[:, h, 0:129], start=False, stop=True),
                         R=[T_qk[h][tb], T_cdb[h]], W=[PSB[6 + h // 2]], inc=True)
                    B.op("pe", lambda e, h=h: e.matmul(ps[:, pu[h // 2], (h % 2) * 130:(h % 2) * 130 + 129], lhsT=kwt[:, ki, h * 128:(h + 1) * 128],
                                                       rhs=vaug[:, tt, h, :], start=True, stop=True),
                         R=[T_kwt[ki], T_vaug[tt]], W=[PSB[pu[h // 2]]])
                for h in range(4):
                    dc = gmath[:, 3, tt * 4 + h:tt * 4 + h + 1]
                    B.op("dve", lambda e, h=h, dc=dc: e.scalar_tensor_tensor(
                        out=cst32[:, h, 0:129], in0=cst32[:, h, 0:129], scalar=dc, in1=ps[:, pu[h // 2], (h % 2) * 130:(h % 2) * 130 + 129],
                        op0=ALU.mult, op1=ALU.add), R=[T_C32[h], PSB[pu[h // 2]], T_gm], W=[T_C32[h]])
                for hp in range(2):
                    pv2 = ps[:, 6 + hp, 0:260].rearrange("p (a b) -> p a b", a=2)
                    B.op("act", lambda e, hp=hp, pv2=pv2: e.activation(
                        out=stg32b[:, s, hp * 256:(hp + 1) * 256].rearrange("p (a b) -> p a b", a=2), in_=pv2[:, :, 0:128], func=AF.Copy),
                        R=[PSB[6 + hp]], W=[T_s32b[s]])
                    B.op("act", lambda e, hp=hp, pv2=pv2: e.activation(
                        out=finc[:, 56 + 4 * s + 2 * hp:58 + 4 * s + 2 * hp], in_=pv2[:, :, 128], func=AF.Abs),
                        R=[PSB[6 + hp]], W=[T_finc])

            def mfin(tt):
                s = tt % 2
                nums = stg32b[:, s, :].rearrange("p (h e) -> p h e", h=4)
                for h in range(4):
                    B.op("act", lambda e, h=h: e.activation(out=stg32[:, 0, h * 128:(h + 1) * 128], in_=nums[:, h, :], func=AF.Square,
                                                            accum_out=finc[:, 16 + h:17 + h]), R=[T_s32b[s]], W=[T_s32[0], T_ss])
                B.op("dve", lambda e: e.tensor_tensor(out=finc[:, 4:8], in0=finc[:, 56 + 4 * s:60 + 4 * s], in1=gmath[:, 2, tt * 4:tt * 4 + 4], op=ALU.max),
                     R=[T_finc, T_gm], W=[T_finc])
                B.op("dve", lambda e: e.reciprocal(out=finc[:, 8:12], in_=finc[:, 4:8]), R=[T_finc], W=[T_finc])
                B.op("dve", lambda e: e.tensor_tensor(out=finc[:, 12:16], in0=finc[:, 8:12], in1=finc[:, 8:12], op=ALU.mult), R=[T_finc], W=[T_finc])
                B.op("dve", lambda e: e.tensor_tensor(out=finc[:, 20:24], in0=finc[:, 12:16], in1=finc[:, 16:20], op=ALU.mult), R=[T_finc, T_ss], W=[T_finc])
                B.op("act", lambda e: e.activation(out=finc[:, 24:28], in_=finc[:, 20:24], func=AF.Ln, bias=epsc, scale=1.0 / 128.0),
                     R=[T_finc, T_small], W=[T_finc])
                B.op("act", lambda e: e.activation(out=finc[:, 28:32], in_=finc[:, 24:28], func=AF.Exp, scale=-0.5), R=[T_finc], W=[T_finc])
                B.op("dve", lambda e: e.tensor_tensor(out=finc[:, 12:16], in0=finc[:, 28:32], in1=finc[:, 8:12], op=ALU.mult), R=[T_finc], W=[T_finc])
                for h in range(4):
                    B.op("pool", lambda e, h=h: e.tensor_scalar(
                        out=stg32[:, 1, h * 128:(h + 1) * 128], in0=nums[:, h, :], scalar1=finc[:, 12 + h:13 + h], scalar2=1.0,
                        op0=ALU.mult, op1=ALU.mult), R=[T_s32b[s], T_finc], W=[T_s32[1]])
                B.op("pool", lambda e: e.tensor_tensor(out=stg32[:, 1, :], in0=stg32[:, 1, :], in1=gmB[:], op=ALU.mult),
                     R=[T_s32[1], T_small], W=[T_s32[1]])
                B.op("pool", lambda e: e.tensor_tensor(out=mix[:, tt, 0:512], in0=stg32[:, 1, :], in1=og[:, tt, :], op=ALU.mult),
                     R=[T_s32[1], T_og[tt]], W=[T_mix[tt], T_xin, T_xinC])

            def next_units(n):
                out = []
                for _ in range(n):
                    if pending:
                        out.append(pending.pop(0))
                return out

            st = mmain(0)
            mmain2(*st)
            for tt in range(1, NT + 1):
                units = next_units(2)
                st = mmain(tt) if tt < NT else None
                hs = [p2b_proj(*u) for u in units]
                if st is not None:
                    mmain2(*st)
                vd_tile(tt - 1)
                for hnd in hs:
                    p2b_fin(*hnd)
                mfin(tt - 1)
                if (tt - 1) % 4 == 3:
                    pending.extend((cc, (tt - 1) // 4) for cc in range(8))
            while pending:
                units = next_units(2)
                hs = [p2b_proj(*u) for u in units]
                for hnd in hs:
                    p2b_fin(*hnd)
            if DEBUG == "hm":
                dbg_dump(mix, T_mix)
            if DEBUG == "qkd":
                dbg_dump(qkT, [t for row in T_qk for t in row])

            numsb = stg32b[:].rearrange("p a (q e) -> p (a q) e", q=4)
            nums_q = numsb.rearrange("p (c q) e -> p q c e", c=2)
            dens_q = finc[:, 32:40].rearrange("p (c q) -> p q c", c=2)
            blk = {"i": 0}
            gn_pending = []
            fin2 = stg32[:, 1, :].rearrange("p (q e) -> p q e", q=4)
            T_fin2 = T_s32[1]
            for h in range(4):
                for qb in range(NB):
                    nkt = 4 * qb + 4
                    if blk["i"] < 8:
                        adaln_piece_late(4 + blk["i"], 3)
                    blk["i"] += 1

                    def scores(kt, h=h, qb=qb):
                        c0 = max(0, kt * 128 - qb * 512)
                        diagk = kt >= 4 * qb
                        for c in range(2):
                            rows = slice(c * 64, (c + 1) * 64)
                            bank = (kt % 2) * 2 + c
                            B.op("pe", lambda e, bank=bank, c0=c0, rows=rows, kt=kt: e.matmul(
                                ps[:, bank, c0:512], lhsT=qkT[rows, 4 + h, kt * 128:(kt + 1) * 128], rhs=qkT[rows, h, qb * 512 + c0:(qb + 1) * 512],
                                start=True, stop=not diagk),
                                R=[T_qk[4 + h][kt // 4], T_qk[h][qb]], W=[PSB[bank]], inc=not diagk)
                        if diagk:
                            for c in range(2):
                                rows = slice(c * 64, (c + 1) * 64)
                                bank = (kt % 2) * 2 + c
                                for hh in range(2):
                                    B.op("pe", lambda e, bank=bank, c0=c0, rows=rows, hh=hh: e.matmul(
                                        ps[:, bank, c0:c0 + 128], lhsT=cmask[rows, hh, :], rhs=cmask[rows, 2 + hh, :],
                                        start=False, stop=(hh == 1)),
                                        R=[T_cst], W=[PSB[bank]], inc=(hh == 1))

                    def exps(kt, h=h, qb=qb):
                        c0 = max(0, kt * 128 - qb * 512)
                        for c in range(2):
                            bank = (kt % 2) * 2 + c
                            B.op("act", lambda e, bank=bank, c0=c0: e.activation(out=stg[:, bank, c0:512], in_=ps[:, bank, c0:512], func=AF.Exp, scale=0.125),
                                 R=[PSB[bank]], W=[T_stg[bank]])

                    def pv(kt, h=h, qb=qb):
                        for c in range(2):
                            bank = (kt % 2) * 2 + c
                            for qi in range(4):
                                qt = 4 * qb + qi
                                if qt < kt:
                                    continue
                                B.op("pe", lambda e, c=c, qi=qi, bank=bank, kt=kt: e.matmul(
                                    ps[:, 4 + qi, c * 256:c * 256 + 129], lhsT=stg[:, bank, qi * 128:(qi + 1) * 128], rhs=vaug[:, kt, h, :],
                                    start=(kt == 0 and c == 0), stop=(kt == 4 * qb + qi), skip_group_check=True),
                                    R=[T_stg[bank], T_vaug[kt]], W=[PSB[4 + qi]], inc=True)
                        if kt >= 4 * qb:
                            qi = kt - 4 * qb
                            pview = ps[:, 4 + qi, :].rearrange("p (c x) -> p c x", c=2)
                            B.op("dve", lambda e, qi=qi, pview=pview: e.tensor_copy(out=nums_q[:, qi], in_=pview[:, :, 0:128]),
                                 R=[PSB[4 + qi]], W=T_s32b)
                            B.op("dve", lambda e, qi=qi, pview=pview: e.tensor_copy(out=dens_q[:, qi], in_=pview[:, :, 128]),
                                 R=[PSB[4 + qi]], W=[T_finc])

                    scores(0)
                    for kt in range(nkt):
                        if kt + 1 < nkt:
                            scores(kt + 1)
                        exps(kt)
                        if kt == 1 and gn_pending:
                            gn_pending[0][0]()
                        if kt == 3 and gn_pending:
                            gn_pending.pop(0)[1]()
                        pv(kt)
                    B.op("dve", lambda e: e.reciprocal(out=finc[:, 40:48], in_=finc[:, 32:40]), R=[T_finc], W=[T_finc])
                    B.op("dve", lambda e: e.tensor_scalar(out=finc[:, 48:52], in0=finc[:, 44:48], scalar1=nlamc, scalar2=None, op0=ALU.mult),
                         R=[T_finc, T_small], W=[T_finc])
                    for qi in range(4):
                        B.op("dve", lambda e, qi=qi: e.tensor_scalar(out=fin[:, qi, :], in0=numsb[:, qi, :], scalar1=finc[:, 40 + qi:41 + qi],
                                                                   scalar2=None, op0=ALU.mult), R=T_s32b + [T_finc], W=[T_fin])
                    for qi in range(4):
                        B.op("dve", lambda e, qi=qi: e.scalar_tensor_tensor(out=fin[:, qi, :], in0=numsb[:, 4 + qi, :], scalar=finc[:, 48 + qi:49 + qi],
                                                                          in1=fin[:, qi, :], op0=ALU.mult, op1=ALU.add),
                             R=T_s32b + [T_finc, T_fin], W=[T_fin])
                    def _gnA():
                        sq = stg32[:, 0, :].rearrange("p (h e) -> p h e", h=4)
                        B.op("dve", lambda e: e.tensor_tensor(out=sq, in0=fin[:], in1=fin[:], op=ALU.mult), R=[T_fin], W=[T_s32[0]])
                        B.op("dve", lambda e: e.tensor_reduce(out=finc[:, 16:20], in_=sq, axis=AX.X, op=ALU.add), R=[T_s32[0]], W=[T_finc])

                    def _gnB(h=h, qb=qb):
                        gB = gdB[:, h * 128:(h + 1) * 128]
                        B.op("act", lambda e: e.activation(out=finc[:, 20:24], in_=finc[:, 16:20], func=AF.Ln, bias=epsc, scale=1.0 / 128.0),
                             R=[T_finc, T_small], W=[T_finc])
                        B.op("act", lambda e: e.activation(out=finc[:, 24:28], in_=finc[:, 20:24], func=AF.Exp, scale=-0.5), R=[T_finc], W=[T_finc])
                        for qi in range(4):
                            B.op("dve", lambda e, qi=qi: e.scalar_tensor_tensor(
                                out=mix[:, 4 * qb + qi, 512 + h * 128:512 + (h + 1) * 128], in0=fin[:, qi, :], scalar=finc[:, 24 + qi:25 + qi], in1=gB,
                                op0=ALU.mult, op1=ALU.mult), R=[T_fin, T_finc, T_small],
                                W=[T_mix[4 * qb + qi], T_og[4 * qb + qi], T_xin, T_xinC])
                    gn_pending.append((_gnA, _gnB))
            while gn_pending:
                a_, b_ = gn_pending.pop(0)
                a_()
                b_()
            B.op("dve", lambda e: e.scalar_tensor_tensor(out=A2, in0=sc_f, scalar=1.0, in1=gcols[:, 8:16], op0=ALU.add, op1=ALU.mult),
                 R=[T_small, T_mod2], W=[T_mod2])
            if DEBUG == "mix":
                dbg_dump(mix, T_mix)

            for tt in range(NT):
                tb = tt // 4
                pbk = nextbank()
                psk = ps[:, pbk, :].bitcast(BF16)
                for fc in range(8):
                    B.op("pe", lambda e, fc=fc, psk=psk, tt=tt: e.transpose(psk[:, fc * 128:(fc + 1) * 128], mix[:, tt, fc * 128:(fc + 1) * 128], ident_b),
                         R=[T_mix[tt], T_cst], W=[PSB[pbk]], inc=(fc == 7))
                if tt % 2 == 0:
                    B.op("act", lambda e, psk=psk, tt=tt: e.activation(out=mixT[:, :, tt * 128:(tt + 1) * 128], in_=psk.rearrange("p (c t) -> p c t", c=8), func=AF.Copy),
                         R=[PSB[pbk]], W=[T_mixT[tb]] + T_hT)
                else:
                    B.op("dve", lambda e, psk=psk, tt=tt: e.tensor_copy(out=mixT[:, :, tt * 128:(tt + 1) * 128], in_=psk.rearrange("p (c t) -> p c t", c=8)),
                         R=[PSB[pbk]], W=[T_mixT[tb]] + T_hT)
            s_w0 = wget()
            s_w1 = wget(la=2)
            first_x1 = True
            early_mod = []
            sq8 = [(stg[:, i, 0:512], T_stg[i]) for i in range(4)] + [(et[:, i, :], T_et[i]) for i in range(3)] + [(kwt[:, 0, :], T_kwt[0])]
            for tb in range(NB):
                sl = slice(tb * 512, (tb + 1) * 512)
                xb, Tx = (xin, T_xin) if tb % 2 == 0 else (xinC, T_xinC)
                B.dma("sp", xb, xT_d[:, :, sl], W=ALL_MIX)
                if tb in (1, 2):
                    k0 = tb - 1
                    for fc in range(8):
                        qa, qT = sq8[fc]
                        srcc = x1T[:, fc, k0 * 512:(k0 + 1) * 512]
                        if fc % 2 == 0:
                            B.op("act", lambda e, qa=qa, srcc=srcc: e.activation(out=qa, in_=srcc, func=AF.Square), R=[T_x1[fc][k0]], W=[qT])
                        else:
                            B.op("dve", lambda e, qa=qa, srcc=srcc: e.tensor_tensor(out=qa, in0=srcc, in1=srcc, op=ALU.mult), R=[T_x1[fc][k0]], W=[qT])
                for fo in range(8):
                    s = s_w0 if fo < 4 else s_w1
                    wv = slot_view(s, 8, 512)
                    pb = nextbank()
                    for kc in range(8):
                        B.op("pe", lambda e, kc=kc, pb=pb, wv=wv, fo=fo, sl=sl: e.matmul(
                            ps[:, pb, :], lhsT=wv[:, kc, (fo % 4) * 128:(fo % 4 + 1) * 128], rhs=mixT[:, kc, sl], start=(kc == 0), stop=(kc == 7)),
                            R=[T_mixT[tb], RING_T[s]], W=[PSB[pb]], inc=(kc == 7))
                    Wl = [T_x1[fo][tb]] + (ALL_RC if first_x1 else [])
                    first_x1 = False
                    B.op("dve", lambda e, pb=pb, fo=fo, sl=sl, xb=xb: e.scalar_tensor_tensor(
                        out=x1T[:, fo, sl], in0=ps[:, pb, :], scalar=gt_a[:, fo:fo + 1], in1=xb[:, fo, :], op0=ALU.mult, op1=ALU.add),
                        R=[PSB[pb], T_mod2, Tx], W=Wl)
                if tb in (1, 2):
                    k0 = tb - 1
                    early_mod.append(norm_block(x1T[:, :, k0 * 512:(k0 + 1) * 512], [T_x1[c][k0] for c in range(8)],
                                                h2T[:, :, k0 * 512:(k0 + 1) * 512], [T_h2[k0]] + T_mixT + T_hT, A2, sh_f, 512, k0,
                                                T_small=T_mod2, defer=True, presq=sq8))
            if DEBUG == "x1":
                dbg_dump(x1T, [t for row in T_x1 for t in row])

            def norm2(p, sbk):
                tb = p * 2 + sbk
                sl = slice(tb * 512, (tb + 1) * 512)
                norm_block(x1T[:, :, sl], [T_x1[c][tb] for c in range(8)], h2T[:, :, sbk * 512:(sbk + 1) * 512],
                           [T_h2[sbk]] + T_mixT + T_hT, A2, sh_f, 512, sbk, T_small=T_mod2)

            ostg = RA[:, 14336:16384].bitcast(F32).rearrange("p (a t) -> p a t", a=2)
            T_ostg = [T("ostg0"), T("ostg1")]
            sqb = [et[:, 0, :], et[:, 1, :], et[:, 2, :], kwt[:, 0, :]]
            T_sqb = [T_et[0], T_et[1], T_et[2], T_kwt[0]]
            SIDE_BANK = 7

            def staged_norm(tb, mode, sbk=None):
                sl = slice(tb * 512, (tb + 1) * 512)
                srcb = x1T[:, :, sl]
                Tsrc = [T_x1[c][tb] for c in range(8)]
                rs = stg32b[:, 0, :]

                def squares(f0):
                    for fc in range(f0, f0 + 4):
                        j = fc % 4
                        if fc % 2 == 0:
                            B.op("act", lambda e, fc=fc, j=j: e.activation(out=sqb[j], in_=srcb[:, fc, :], func=AF.Square), R=Tsrc, W=[T_sqb[j]])
                        else:
                            B.op("dve", lambda e, fc=fc, j=j: e.tensor_tensor(out=sqb[j], in0=srcb[:, fc, :], in1=srcb[:, fc, :], op=ALU.mult), R=Tsrc, W=[T_sqb[j]])

                def mms(f0):
                    for fc in range(f0, f0 + 4):
                        j = fc % 4
                        B.op("pe", lambda e, fc=fc, j=j: e.matmul(ps[:, SIDE_BANK, :], lhsT=ones_b, rhs=sqb[j], start=(fc == 0), stop=(fc == 7)),
                             R=[T_sqb[j], T_cst], W=[PSB[SIDE_BANK]], inc=True)

                squares(0)
                yield
                mms(0)
                squares(4)
                yield
                mms(4)
                B.op("act", lambda e: e.activation(out=rs, in_=ps[:, SIDE_BANK, :], func=AF.Ln, bias=epsc, scale=1.0 / D), R=[PSB[SIDE_BANK], T_small], W=[T_s32b[0]])
                B.op("act", lambda e: e.activation(out=rs, in_=rs, func=AF.Exp, scale=-0.5), R=[T_s32b[0]], W=[T_s32b[0]])
                yield
                for fc in range(8):
                    if mode == "out":
                        oj = fc % 2
                        B.op("dve", lambda e, fc=fc, oj=oj: e.scalar_tensor_tensor(
                            out=ostg[:, oj, :], in0=srcb[:, fc, :], scalar=g_fin[:, fc:fc + 1], in1=rs, op0=ALU.mult, op1=ALU.mult),
                            R=Tsrc + [T_s32b[0], T_small], W=[T_ostg[oj]])
                        B.dma("sp", out_d[:, fc, sl], ostg[:, oj, :], R=[T_ostg[oj]], is_out=True)
                        if fc < 7:
                            yield
                    else:
                        tmp = stg32b[:, 1, :]
                        B.op("dve", lambda e, fc=fc: e.scalar_tensor_tensor(
                            out=tmp, in0=srcb[:, fc, :], scalar=A2[:, fc:fc + 1], in1=rs, op0=ALU.mult, op1=ALU.mult),
                            R=Tsrc + [T_s32b[0], T_mod2], W=[T_s32b[1]])
                        B.op("act", lambda e, fc=fc: e.activation(out=h2T[:, fc, sbk * 512:(sbk + 1) * 512], in_=tmp, func=AF.Identity,
                                                                 bias=sh_f[:, fc:fc + 1], scale=1.0),
                             R=[T_s32b[1], T_mod2], W=[T_h2[sbk]])
                yield

            side = {"gens": []}

            def side_step():
                while side["gens"]:
                    try:
                        next(side["gens"][0])
                        return
                    except StopIteration:
                        side["gens"].pop(0)

            def final_big(tb):
                sl = slice(tb * 512, (tb + 1) * 512)
                ob = tb % 2
                T_oc = [T("oc%d" % fc) for fc in range(8)]
                modf = norm_block(x1T[:, :, sl], [T_x1[c][tb] for c in range(8)], outb[ob], [T_outb[ob]] + [t for row in T_act[0:16] for t in row],
                                  g_fin, None, 512, tb % 2, defer=True)
                rs_ = stg32[:, tb % 2, 0:512]
                first = True
                for fc in range(8):
                    B.op("dve", lambda e, fc=fc: e.scalar_tensor_tensor(
                        out=outb[ob][:, fc, :], in0=x1T[:, fc, sl], scalar=g_fin[:, fc:fc + 1], in1=rs_, op0=ALU.mult, op1=ALU.mult),
                        R=[T_x1[fc][tb], T_s32[tb % 2], T_small], W=[T_oc[fc]] + ([T_outb[ob]] + [t for row in T_act[0:16] for t in row] if first else []))
                    first = False
                    B.dma("sp", out_d[:, fc, sl], outb[ob][:, fc, :], R=[T_oc[fc]], is_out=True)

            T_outb = [T_xin, T_xinC]
            outb = [RBm[:, 0:8192].bitcast(F32).rearrange("p (c t) -> p c t", c=8),
                    RBm[:, 8192:16384].bitcast(F32).rearrange("p (c t) -> p c t", c=8)]

            for m_ in early_mod:
                m_()
            for p in range(2):
                for grp in range(11):
                    if p == 1 and grp == 1:
                        side["gens"] += [staged_norm(0, "out"), staged_norm(1, "out")]
                    s = wget()
                    wv = slot_view(s, 8, 512)
                    for pr in range(2):
                        i = grp * 2 + pr
                        if p == 1 and grp >= 1:
                            side_step()
                        chunks = (i, 22 + i)
                        dsets = [build_diag(cffn[:, ch, 0:3], 3, T_small) for ch in chunks]
                        for wi, ch in enumerate(chunks):
                            bi = (i % 2) * 2 + wi
                            B.op("dve", lambda e, bi=bi, ch=ch: e.tensor_copy(out=stg[:, bi, 0:2], in_=halo[:, ch, :]), R=[T_halo], W=[T_stg[bi]])
                            for sbk in range(2):
                                pb = nextbank(0, 7)
                                for kc in range(8):
                                    B.op("pe", lambda e, kc=kc, pb=pb, wv=wv, wi=wi, pr=pr, sbk=sbk: e.matmul(
                                        ps[:, pb, :], lhsT=wv[:, kc, wi * 256 + pr * 128:wi * 256 + (pr + 1) * 128],
                                        rhs=h2T[:, kc, sbk * 512:(sbk + 1) * 512], start=(kc == 0), stop=(kc == 7)),
                                        R=[T_h2[sbk], RING_T[s]], W=[PSB[pb]], inc=(kc == 7))
                                B.op("act", lambda e, bi=bi, pb=pb, sbk=sbk: e.activation(out=stg[:, bi, 2 + sbk * 512:2 + (sbk + 1) * 512], in_=ps[:, pb, :], func=AF.Copy),
                                     R=[PSB[pb]], W=[T_stg[bi]])
                            B.op("dve", lambda e, bi=bi, ch=ch: e.tensor_copy(out=halo[:, ch, :], in_=stg[:, bi, 1024:1026]), R=[T_stg[bi]], W=[T_halo])
                        ba, bg = (i % 2) * 2, (i % 2) * 2 + 1
                        pa2s, pg2s = [], []
                        for sbk in range(2):
                            pa2 = nextbank(0, 7)
                            pa2s.append(pa2)
                            for j in range(3):
                                B.op("pe", lambda e, j=j, pa2=pa2, sbk=sbk, ba=ba, d=dsets[0]: e.matmul(
                                    ps[:, pa2, :], lhsT=diag[:, d * 4 + j, :], rhs=stg[:, ba, sbk * 512 + j:sbk * 512 + j + 512], start=(j == 0), stop=(j == 2)),
                                    R=[T_stg[ba], T_diag[dsets[0]]], W=[PSB[pa2]], inc=(j == 2))
                        for sbk in range(2):
                            pg2 = nextbank(0, 7)
                            pg2s.append(pg2)
                            for j in range(3):
                                B.op("pe", lambda e, j=j, pg2=pg2, sbk=sbk, bg=bg, d=dsets[1]: e.matmul(
                                    ps[:, pg2, :], lhsT=diag[:, d * 4 + j, :], rhs=stg[:, bg, sbk * 512 + j:sbk * 512 + j + 512], start=(j == 0), stop=(j == 2)),
                                    R=[T_stg[bg], T_diag[dsets[1]]], W=[PSB[pg2]], inc=(j == 2))
                        for sbk in range(2):
                            pa2, pg2, si = pa2s[sbk], pg2s[sbk], sbk
                            B.op("act", lambda e, pg2=pg2, si=si, i=i: e.activation(out=stg32[:, si, :], in_=ps[:, pg2, :], func=AF.Silu, bias=cffn[:, 22 + i, 3:4], scale=1.0),
                                 R=[PSB[pg2], T_small], W=[T_s32[si]])
                            B.op("dve", lambda e, pa2=pa2, si=si, i=i, sbk=sbk: e.scalar_tensor_tensor(
                                out=actT(i)[:, sbk * 512:(sbk + 1) * 512], in0=ps[:, pa2, :], scalar=cffn[:, i, 3:4], in1=stg32[:, si, :], op0=ALU.add, op1=ALU.mult),
                                R=[PSB[pa2], T_s32[si], T_small], W=[T_act[i][sbk]] + (ALL_MIX if i < 16 else T_mixT + T_hT))
                def down(fo, sbks, s, hi=7):
                    wd = ring[:, s, 0:2816].rearrange("p (k n) -> p k n", k=22)
                    for sbk in sbks:
                        tb = p * 2 + sbk
                        sl = slice(tb * 512, (tb + 1) * 512)
                        pb = nextbank(0, hi)
                        for kc in range(22):
                            B.op("pe", lambda e, kc=kc, pb=pb, wd=wd, sbk=sbk: e.matmul(
                                ps[:, pb, :], lhsT=wd[:, kc, :], rhs=actT(kc)[:, sbk * 512:(sbk + 1) * 512], start=(kc == 0), stop=(kc == 21)),
                                R=[T_act[kc][sbk], RING_T[s]], W=[PSB[pb]], inc=(kc == 21))
                        B.op("dve", lambda e, pb=pb, fo=fo, sl=sl, tb=tb: e.scalar_tensor_tensor(
                            out=x1T[:, fo, sl], in0=ps[:, pb, :], scalar=gt_f[:, fo:fo + 1], in1=x1T[:, fo, sl], op0=ALU.mult, op1=ALU.add),
                            R=[PSB[pb], T_mod2, T_x1[fo][tb]], W=[T_x1[fo][tb]])

                if p == 0:
                    side["gens"] += [staged_norm(2, "ffn", 0), staged_norm(3, "ffn", 1)]
                    side_step()
                    for fo in range(8):
                        if fo < 7:
                            side_step()
                        down(fo, (0, 1), wget())
                else:
                    sl3 = slice(3 * 512, 4 * 512)

                    def last_sq(fo):
                        j = fo % 4
                        if fo % 2 == 0:
                            B.op("act", lambda e, fo=fo, j=j: e.activation(out=stg[:, j, 0:512], in_=x1T[:, fo, sl3], func=AF.Square), R=[T_x1[fo][3]], W=[T_stg[j]])
                        else:
                            B.op("dve", lambda e, fo=fo, j=j: e.tensor_tensor(out=stg[:, j, 0:512], in0=x1T[:, fo, sl3], in1=x1T[:, fo, sl3], op=ALU.mult), R=[T_x1[fo][3]], W=[T_stg[j]])

                    def last_mm(fo):
                        j = fo % 4
                        B.op("pe", lambda e, fo=fo, j=j: e.matmul(ps[:, 6, :], lhsT=ones_b, rhs=stg[:, j, 0:512], start=(fo == 0), stop=(fo == 7)),
                             R=[T_stg[j], T_cst], W=[PSB[6]], inc=True)

                    for sbk in range(2):
                        for fo in range(8):
                            down(fo, (sbk,), wget(), hi=(6 if sbk == 1 else 7))
                            if sbk == 1:
                                if fo == 0:
                                    side["gens"] += [staged_norm(2, "out")]
                                side_step()
                                if fo >= 1:
                                    last_mm(fo - 1)
                                last_sq(fo)
                    last_mm(7)
            if DEBUG == "x2":
                dbg_dump(x1T, [t for row in T_x1 for t in row])

            while side["gens"]:
                side_step()
            rs3 = stg32[:, 1, :]
            B.op("act", lambda e: e.activation(out=rs3, in_=ps[:, 6, :], func=AF.Ln, bias=epsc, scale=1.0 / D), R=[PSB[6], T_small], W=[T_s32[1]])
            B.op("act", lambda e: e.activation(out=rs3, in_=rs3, func=AF.Exp, scale=-0.5), R=[T_s32[1]], W=[T_s32[1]])
            T_oc = [T("oc%d" % fc) for fc in range(8)]
            for fc in range(8):
                B.op("dve", lambda e, fc=fc: e.scalar_tensor_tensor(
                    out=outb[1][:, fc, :], in0=x1T[:, fc, 3 * 512:4 * 512], scalar=g_fin[:, fc:fc + 1], in1=rs3, op0=ALU.mult, op1=ALU.mult),
                    R=[T_x1[fc][3], T_s32[1], T_small], W=[T_oc[fc]] + ([T_outb[1]] + [t for row in T_act[0:16] for t in row] if fc == 0 else []))
                B.dma("sp", out_d[:, fc, 3 * 512:4 * 512], outb[1][:, fc, :], R=[T_oc[fc]], is_out=True)

        try:
            body()
        except _Stop:
            pass
        B.finish()
        block = es.enter_context(nc.Block())
        B.emit(block)
    return nc


def _chunk_rows(w, kc):
    n = w.shape[1]
    return np.ascontiguousarray(w.reshape(kc, 128, n).transpose(1, 0, 2))


def _consts():
    cst = np.zeros((128, 5, 128), np.float32)
    cst[:, 0, :] = np.eye(128, dtype=np.float32)
    cst[:, 1, :] = np.triu(np.ones((128, 128), np.float32))
    cst[:, 2, :] = 1.0
    p = np.arange(128)
    perm = np.zeros((128, 128), np.float32)
    perm[p ^ 32, p] = 1.0
    cst[:, 3, :] = perm
    half = 32
    inv_freq = (np.float32(10000.0) ** (-np.arange(half, dtype=np.float32) / np.float32(half))).astype(np.float32)
    cst[:, 4, 0] = inv_freq[p % 32]
    cst[:, 4, 1] = np.where((p % 64) < 32, 1.0, -1.0)
    return cst


_NC_CACHE = {}


def kernel(x, c, positions, w_ada, b_ada, g_mix, w_in, conv_qk_w, conv_qk_b, b_if, g_mlstm,
           lam_q1, lam_k1, lam_q2, lam_k2, g_diff, w_out, g_ffn, w_up, conv_ffn_w, conv_ffn_b,
           w_down, g_final):
    f32 = lambda a: np.asarray(a, dtype=np.float32)
    x, c = f32(x), f32(c)
    positions = np.asarray(positions, dtype=np.int32)
    w_ada, b_ada, w_in, w_out, w_up, w_down = f32(w_ada)[0], f32(b_ada)[0], f32(w_in)[0], f32(w_out)[0], f32(w_up)[0], f32(w_down)[0]
    nb = x.shape[0]

    def pieces(w, cols_list):
        out = []
        for cols in cols_list:
            out.append(_chunk_rows(w[:, cols], 8).reshape(128, 4096))
        return np.ascontiguousarray(np.stack(out))

    wada_p = pieces(w_ada, [np.arange(i * 512, (i + 1) * 512) for i in range(12)])
    win_cols = [np.arange(0, 512), np.arange(512, 1024), np.arange(1024, 1536), np.arange(1536, 2048),
                np.arange(2056, 2568), np.arange(2568, 3080), np.arange(3080, 3592)]
    win_p = pieces(w_in, win_cols)
    wgate_p = np.ascontiguousarray(_chunk_rows(w_in[:, 2048:2056], 8).reshape(128, 64))
    wout_p = pieces(w_out, [np.arange(0, 512), np.arange(512, 1024)])
    up_cols = []
    for g in range(11):
        a = np.arange(g * 256, (g + 1) * 256)
        up_cols.append(np.concatenate([a, 2816 + a]))
    wup_p = pieces(w_up, up_cols)
    wdown_p = np.ascontiguousarray(np.stack([_chunk_rows(w_down[:, f * 128:(f + 1) * 128], 22).reshape(128, 2816) for f in range(8)]))
    col8 = lambda v: np.ascontiguousarray(f32(v).reshape(-1, 128).T)
    bada_p = col8(b_ada)
    gcols = np.ascontiguousarray(np.concatenate([col8(f32(g_mix)[0]), col8(f32(g_ffn)[0]), col8(f32(g_final))], axis=1))
    cqk = np.ascontiguousarray(np.concatenate([f32(conv_qk_w)[0].reshape(4, 8, 128).transpose(2, 1, 0),
                                               f32(conv_qk_b)[0].reshape(8, 128).T[:, :, None]], axis=2))
    cffn = np.ascontiguousarray(np.concatenate([f32(conv_ffn_w)[0].reshape(3, 44, 128).transpose(2, 1, 0),
                                                f32(conv_ffn_b)[0].reshape(44, 128).T[:, :, None]], axis=2))
    bif = np.ascontiguousarray(np.broadcast_to(f32(b_if)[0][None, None, :], (128, 16, 8)))
    gmB = np.ascontiguousarray(np.broadcast_to(f32(g_mlstm)[0][None, :], (128, 512)))
    gdB = np.ascontiguousarray(np.broadcast_to(f32(g_diff)[0][None, :], (128, 512)))
    lam = np.ascontiguousarray(np.broadcast_to(np.stack([f32(lam_q1)[0], f32(lam_k1)[0], f32(lam_q2)[0], f32(lam_k2)[0]])[None], (128, 4, 64)))
    cst = _consts()
    pp = np.arange(128)
    cmk = np.zeros((128, 4, 128), np.float32)
    for hh in range(2):
        cmk[pp, hh, (pp % 64) + 64 * hh] = 1.0
        cmk[:, 2 + hh, :] = np.where(((pp % 64) + 64 * hh)[:, None] > np.arange(128)[None, :], -30000.0, 0.0)

    in_maps = []
    for b in range(nb):
        xT = np.ascontiguousarray(x[b].T.reshape(8, 128, S).transpose(1, 0, 2))
        in_maps.append({
            "xT": xT, "c": np.ascontiguousarray(c[b].reshape(8, 128).T),
            "pos": np.ascontiguousarray(np.broadcast_to(positions[b][None, :], (128, S))),
            "w_ada": wada_p, "b_ada": bada_p, "gcols": gcols, "w_in": win_p, "w_gate": wgate_p, "cqk": cqk, "bif": bif,
            "gmB": gmB, "gdB": gdB, "lam": lam, "w_out": wout_p, "w_up": wup_p, "cffn": cffn, "w_down": wdown_p, "consts": cst, "cmask": cmk,
        })
    if "nc" not in _NC_CACHE:
        _NC_CACHE["nc"] = build_program()
    nc = _NC_CACHE["nc"]
    res = run_bass_kernel_spmd(nc, in_maps, core_ids=list(range(nb)))
    outs = []
    for b in range(nb):
        oT = np.asarray(res.results[b]["outT"], dtype=np.float32)
        outs.append(oT.transpose(1, 0, 2).reshape(D, S).T)
    out = np.ascontiguousarray(np.stack(outs)).astype(np.float32)
    if DEBUG:
        kernel.dbg = [np.asarray(res.results[b]["dbg"]) for b in range(nb)]
    return out
```

```python
import os
import math
import numpy as np
from contextlib import ExitStack
import concourse.bass as bass
import concourse.mybir as mybir
from concourse.bass_utils import run_bass_kernel_spmd

F32 = mybir.dt.float32
BF16 = mybir.dt.bfloat16
I32 = mybir.dt.int32
AF = mybir.ActivationFunctionType
ALU = mybir.AluOpType
AX = mybir.AxisListType

S = 2048
D = 1024
NT = 16
NB = 4
EPS = 1e-6
LAM_INIT = 0.8 - 0.6 * math.exp(-0.3 * 0)
PI = math.pi
TWO_PI = 2.0 * math.pi
CW1 = 6.28125
CW2 = TWO_PI - 6.28125
NSLOT = 4
LOOKAHEAD = 3
DEBUG = os.environ.get("MK_DEBUG", "")


class T:
    __slots__ = ("name", "w", "r", "excl", "wl")

    def __init__(self, name="", excl=False):
        self.name = name
        self.w = None
        self.r = {}
        self.wl = []
        self.excl = excl


class Builder:
    ENG = ("pe", "dve", "act", "pool", "sp")

    def __init__(self, nc, es):
        self.nc = nc
        self.sem = {k: es.enter_context(nc.semaphore("sem_" + k)) for k in self.ENG}
        self.cnt = {k: 0 for k in self.ENG}
        self.ops = {k: [] for k in self.ENG}
        self.known = {k: {} for k in self.ENG}
        self.NRING = 24
        self.ring = [es.enter_context(nc.semaphore("dma%d" % i)) for i in range(self.NRING)]
        self.ring_cnt = [0] * self.NRING
        self.ring_i = 0
        self.out_tokens = []
        self.swsem = []
        self._es = es

    def _semobj(self, key):
        if isinstance(key, tuple):
            return self.swsem[key[1]]
        return self.sem[key] if isinstance(key, str) else self.ring[key]

    def _need(self, eng, tok, waits, raw):
        if tok is None:
            return
        key, val = tok
        if key == eng and not raw and eng != "pool":
            return
        if self.known[eng].get(key, 0) >= val:
            return
        self.known[eng][key] = val
        waits.append((key, val))

    def _deps(self, eng, R, W, soft=False):
        waits = []
        for t in R:
            self._need(eng, t.w, waits, True)
            for tk in t.wl:
                self._need(eng, tk, waits, True)
            if t.excl:
                for k, v in t.r.items():
                    self._need(eng, (k, v), waits, False)
        for t in W:
            if not soft:
                self._need(eng, t.w, waits, False)
                for tk in t.wl:
                    self._need(eng, tk, waits, False)
            for k, v in t.r.items():
                self._need(eng, (k, v), waits, False)
        return waits

    def _commit(self, tok, R, W, soft=False):
        for t in W:
            if soft:
                t.wl.append(tok)
            else:
                t.w = tok
                t.wl = []
            t.r = {}
        k, v = tok
        for t in R:
            if t.r.get(k, 0) < v:
                t.r[k] = v

    def op(self, eng, fn, R=(), W=(), inc=True, soft=False):
        waits = self._deps(eng, R, W, soft)
        tok = (eng, self.cnt[eng] + 1)
        if inc:
            self.cnt[eng] += 1
        self._commit(tok, R, W, soft)
        self.ops[eng].append((waits, fn, (eng, 1) if inc else None))
        return tok

    def dma(self, q, out, in_, R=(), W=(), is_out=False, soft=False, **kw):
        waits = self._deps(q, R, W, soft)
        if q == "pool":
            n = len(self.swsem)
            self.swsem.append(self._es.enter_context(self.nc.semaphore("swdma%d" % n)))
            tok = (("sw", n), 16)
            self._commit(tok, R, W, soft)
            self.ops[q].append((waits, lambda e, out=out, in_=in_, kw=kw: e.dma_start(out=out, in_=in_, **kw), (tok[0], 16)))
            if is_out:
                self.out_tokens.append(tok)
            return tok
        i = self.ring_i
        self.ring_i = (self.ring_i + 1) % self.NRING
        if self.ring_cnt[i] > 0:
            self._need(q, (i, self.ring_cnt[i]), waits, True)
        self.ring_cnt[i] += 16
        tok = (i, self.ring_cnt[i])
        self._commit(tok, R, W, soft)
        self.ops[q].append((waits, lambda e, out=out, in_=in_, kw=kw: e.dma_start(out=out, in_=in_, **kw), (i, 16)))
        if is_out:
            self.out_tokens.append(tok)
        return tok

    def finish(self):
        waits = []
        for tok in self.out_tokens:
            self._need("sp", tok, waits, True)
        self.ops["sp"].append((waits, None, None))

    def emit(self, block):
        def run(eng_key):
            def body(e):
                for waits, fn, inc in self.ops[eng_key]:
                    for key, val in waits:
                        e.wait_ge(self._semobj(key), val)
                    if fn is None:
                        continue
                    ins = fn(e)
                    if inc is not None:
                        ins.then_inc(self._semobj(inc[0]), inc[1])
            return body
        block.tensor(run("pe"))
        block.vector(run("dve"))
        block.scalar(run("act"))
        block.gpsimd(run("pool"))
        block.sync(run("sp"))


def build_program(stop_after=None):
    nc = bass.Bass("TRN2", target_bir_lowering=False)
    dt_in = lambda name, shape, dt=F32: nc.dram_tensor(name, list(shape), dt, kind="ExternalInput").ap()
    xT_d = dt_in("xT", [128, 8, S])
    c_d = dt_in("c", [128, 8])
    pos_d = dt_in("pos", [128, S], I32)
    wada_d = dt_in("w_ada", [12, 128, 4096])
    bada_d = dt_in("b_ada", [128, 48])
    gcols_d = dt_in("gcols", [128, 24])
    win_d = dt_in("w_in", [7, 128, 4096])
    wgate_d = dt_in("w_gate", [128, 64])
    cqk_d = dt_in("cqk", [128, 8, 5])
    bif_d = dt_in("bif", [128, 16, 8])
    gmB_d = dt_in("gmB", [128, 512])
    gdB_d = dt_in("gdB", [128, 512])
    lam_d = dt_in("lam", [128, 4, 64])
    wout_d = dt_in("w_out", [2, 128, 4096])
    wup_d = dt_in("w_up", [11, 128, 4096])
    cffn_d = dt_in("cffn", [128, 44, 4])
    wdown_d = dt_in("w_down", [8, 128, 2816])
    cmask_d = dt_in("cmask", [128, 4, 128])
    consts_d = dt_in("consts", [128, 5, 128])
    out_d = nc.dram_tensor("outT", [128, 8, S], F32, kind="ExternalOutput").ap()
    dbg_d = nc.dram_tensor("dbg", [128, 8, S], F32, kind="ExternalOutput").ap() if DEBUG else None

    with ExitStack() as es:
        B = Builder(nc, es)
        sb = lambda name, shape, dt=F32: es.enter_context(nc.sbuf_tensor("s_" + name, list(shape), dt))

        ps = es.enter_context(nc.psum_tensor("ps", [128, 8, 512], F32))
        PSB = [T("ps%d" % i, excl=True) for i in range(8)]
        ring = sb("ring", [128, NSLOT, 4096], BF16)
        RING_T = [T("slot%d" % i) for i in range(NSLOT)]
        cst_f = sb("cst_f", [128, 5, 128], F32)
        cst_b = sb("cst_b", [128, 4, 128], BF16)
        smalls = sb("smalls", [128, 256], F32)
        RA = sb("RA", [128, 8 * S], BF16)
        RBm = sb("RBm", [128, 16 * 1024], BF16)
        RC = sb("RC", [128, 16512], F32)
        gts = sb("gts", [128, 16, 8], F32)
        gmath = sb("gmath", [128, 8, 64], F32)
        gmB = sb("gmB", [128, 512], F32)
        gdB = sb("gdB", [128, 512], F32)
        lamt = sb("lamt", [128, 4, 64], F32)
        cqk = sb("cqk", [128, 8, 5], F32)
        cffn = sb("cffn", [128, 44, 4], F32)
        bif = sb("bif", [128, 16, 8], F32)
        stg = sb("stg", [128, 4, 1040], BF16)
        stg32 = sb("stg32", [128, 2, 512], F32)
        stg32b = sb("stg32b", [128, 2, 512], F32)
        diag = sb("diag", [128, 16, 128], BF16)
        et = sb("et", [128, 3, 512], BF16)
        cst32 = sb("cst32", [128, 4, 130], F32)
        cdb = sb("cdb", [128, 4, 130], BF16)
        kwt = sb("kwt", [128, 2, 512], BF16)
        fin = sb("fin", [128, 4, 128], F32)
        finc = sb("finc", [128, 64], F32)
        halo = sb("halo", [128, 44, 2], BF16)
        cact_b = sb("cact_b", [128, 8], BF16)
        haloqk = sb("haloqk", [128, 8, 4], BF16)
        cmask = sb("cmask", [128, 4, 128], BF16)

        hT = RA[:].rearrange("p (c t) -> p c t", c=8)
        mixT = hT
        xin = RBm[:, 0:8192].bitcast(F32).rearrange("p (c t) -> p c t", c=8)
        mix = RBm[:].rearrange("p (n f) -> p n f", n=16)
        xinB = RC[:, 8192:12288].rearrange("p (c t) -> p c t", c=8)
        xinC = RBm[:, 8192:16384].bitcast(F32).rearrange("p (c t) -> p c t", c=8)
        xinD = RC[:, 4096:8192].rearrange("p (c t) -> p c t", c=8)
        og = mix[:, :, 512:1024]
        qkT = RC[:, 0:8192].bitcast(BF16).rearrange("p (c t) -> p c t", c=8)
        vaug = RC[:, 8192:8192 + 4128].bitcast(BF16).rearrange("p (n h e) -> p n h e", n=16, h=4)
        cs = RC[:, 12320:12320 + 4096].rearrange("p (k t) -> p k t", k=2)
        x1T = RC[:, 0:16384].rearrange("p (c t) -> p c t", c=8)
        h2T = RA[:, 0:8192].rearrange("p (c t) -> p c t", c=8)
        actA = RBm[:].rearrange("p (c t) -> p c t", c=16)
        actB = RA[:, 8192:8192 + 6144].rearrange("p (c t) -> p c t", c=6)

        def actT(i):
            return actA[:, i, :] if i < 16 else actB[:, i - 16, :]

        T_hT = [T("hT%d" % i) for i in range(NB)]
        T_xin = T("xin")
        T_xinB = T("xinB")
        T_xinC = T("xinC")
        T_xinD = T("xinD")
        T_mod2 = T("mod2")
        T_rs = T("ropescr")
        T_ss = T("ss")
        T_mix = [T("mix%d" % i) for i in range(NT)]
        T_og = [T("og%d" % i) for i in range(NT)]
        T_qk = [[T("qk%d_%d" % (c, b)) for b in range(NB)] for c in range(8)]
        T_vaug = [T("vaug%d" % i) for i in range(NT)]
        T_cs = T("cs")
        T_x1 = [[T("x1_%d_%d" % (c, b)) for b in range(NB)] for c in range(8)]
        T_cst = T("cst")
        T_small = T("small")
        T_bif = T("bif")
        T_gts = T("gts")
        T_gm = T("gmath")
        T_stg = [T("stg%d" % i) for i in range(4)]
        T_s32 = [T("s32_0"), T("s32_1")]
        T_s32b = [T("s32b_0"), T("s32b_1")]
        T_diag = [T("diag%d" % i) for i in range(4)]
        T_et = [T("et0"), T("et1"), T("et2")]
        T_C32 = [T("C32_%d" % h) for h in range(4)]
        T_cdb = [T("cdb_%d" % h) for h in range(4)]
        T_kwt = [T("kwt0"), T("kwt1")]
        T_fin = T("fin")
        T_finc = T("finc")
        T_halo = T("halo")
        T_hqk = [T("hqk%d" % i) for i in range(8)]
        T_mixT = [T("mixT%d" % i) for i in range(NB)]
        T_h2 = [T("h2_0"), T("h2_1")]
        T_act = [[T("act%d_%d" % (i, s)) for s in range(2)] for i in range(22)]
        ALL_RC = [t for row in T_qk for t in row] + T_vaug + [T_cs]
        ALL_MIX = T_mix + T_og + [T_xin, T_xinC]

        ident_b = cst_b[:, 0, :]
        triu_b = cst_b[:, 1, :]
        ones_b = cst_b[:, 2, :]
        perm_b = cst_b[:, 3, :]
        ident_f = cst_f[:, 0, :]
        triu_f = cst_f[:, 1, :]
        ones_f = cst_f[:, 2, :]
        invf = cst_f[:, 4, 0:1]
        sgn = cst_f[:, 4, 1:2]

        c_raw = smalls[:, 0:8]
        mod = smalls[:, 16:64]
        A1 = smalls[:, 64:72]
        A2 = smalls[:, 72:80]
        gcols = smalls[:, 80:104]
        bada = smalls[:, 104:152]
        epsc = smalls[:, 152:153]
        lamc = smalls[:, 153:154]
        nlamc = smalls[:, 154:155]
        onec = smalls[:, 155:156]
        ltmp = smalls[:, 156:160]
        sh_a, sc_a, gt_a = mod[:, 0:8], mod[:, 8:16], mod[:, 16:24]
        sh_f, sc_f, gt_f = mod[:, 24:32], mod[:, 32:40], mod[:, 40:48]
        g_fin = gcols[:, 16:24]

        rot = {"i": 0}

        def nextbank(lo=0, hi=8):
            b = lo + rot["i"] % (hi - lo)
            rot["i"] += 1
            return b

        WSEQ = ([(wada_d[i], 4096) for i in range(4)] +
                [(win_d[0], 4096), (win_d[1], 4096), (wgate_d, 64), (win_d[2], 4096), (win_d[3], 4096),
                 (win_d[4], 4096), (win_d[5], 4096), (win_d[6], 4096)] +
                [(wada_d[i], 4096) for i in range(4, 12)] +
                [(wout_d[0], 4096), (wout_d[1], 4096)])
        for _p in range(2):
            WSEQ += [(wup_d[g], 4096) for g in range(11)] + [(wdown_d[f], 2816) for f in range(8)] * (1 + _p)
        ws = {"issued": 0, "next": 0}

        def _issue_to(n):
            while ws["issued"] < min(n, len(WSEQ)):
                k = ws["issued"]
                src, nel = WSEQ[k]
                s = k % NSLOT
                a = 4 if nel == 4096 else (2 if nel == 2816 else 1)
                B.dma("pool", ring[:, s, 0:nel].rearrange("p (a n) -> p a n", a=a), src.rearrange("p (a n) -> p a n", a=a), W=[RING_T[s]],
                      R=([T_small, T_cst, T_rs, T_bif] if k == 0 else []))
                ws["issued"] += 1

        def wget(la=LOOKAHEAD):
            k = ws["next"]
            ws["next"] += 1
            _issue_to(k + 1 + la)
            return k % NSLOT

        def slot_view(s, kc, n):
            return ring[:, s, 0:kc * n].rearrange("p (k n) -> p k n", k=kc)

        class _Stop(Exception):
            pass

        def dbg_dump(ap, Ts):
            dd = dbg_d
            if tuple(ap.shape) == (128, 16, 1024):
                dd = dbg_d.rearrange("p c (a t) -> p (c a) t", a=2)
            B.dma("pool", dd, ap, R=Ts, is_out=True)
            raise _Stop()

        def body():
            B.dma("sp", RC[:, 0:2048].bitcast(I32), pos_d, W=[T_rs])
            B.dma("sp", cst_f[:], consts_d, W=[T_cst], soft=True)
            B.dma("pool", cmask[:], cmask_d, W=[T_cst], soft=True)
            B.dma("sp", c_raw, c_d, W=[T_small], soft=True)
            B.dma("sp", bada, bada_d, W=[T_small], soft=True)
            B.dma("sp", gcols, gcols_d, W=[T_small], soft=True)
            B.dma("sp", cqk[:], cqk_d, W=[T_small], soft=True)
            B.dma("sp", cffn[:], cffn_d, W=[T_small], soft=True)
            B.dma("sp", gmB[:], gmB_d, W=[T_small], soft=True)
            B.dma("sp", gdB[:], gdB_d, W=[T_small], soft=True)
            B.dma("sp", lamt[:], lam_d, W=[T_small], soft=True)
            B.dma("sp", bif[:], bif_d, W=[T_bif])
            B.op("dve", lambda e: e.tensor_copy(out=cst_b[:], in_=cst_f[:, 0:4, :]), R=[T_cst], W=[T_cst])
            B.op("dve", lambda e: e.memset(epsc, EPS), W=[T_small], soft=True)
            B.op("dve", lambda e: e.memset(onec, 1.0), W=[T_small], soft=True)
            B.op("dve", lambda e: e.memset(halo[:], 0.0), W=[T_halo])
            B.op("act", lambda e: e.activation(out=cact_b[:], in_=c_raw, func=AF.Silu), R=[T_small], W=[T_small])
            _issue_to(LOOKAHEAD)

            def rope_tables():
                sA = RC[:, 0:2048]
                sB = RC[:, 2048:4096]
                sC = RC[:, 4096:6144]
                sAi = sA.bitcast(I32)
                op = lambda fn, **kw: B.op("dve", fn, R=[T_rs, T_cst], W=[T_rs])
                op(lambda e: e.tensor_copy(out=sB, in_=sAi))
                op(lambda e: e.tensor_scalar(out=sB, in0=sB, scalar1=invf, scalar2=None, op0=ALU.mult))
                op(lambda e: e.tensor_scalar(out=sC, in0=sB, scalar1=1.0 / TWO_PI, scalar2=None, op0=ALU.mult))
                op(lambda e: e.tensor_copy(out=sAi, in_=sC))
                op(lambda e: e.tensor_copy(out=sC, in_=sAi))
                op(lambda e: e.scalar_tensor_tensor(out=sB, in0=sC, scalar=-TWO_PI, in1=sB, op0=ALU.mult, op1=ALU.add))
                op(lambda e: e.tensor_scalar(out=sC, in0=sB, scalar1=PI, scalar2=-TWO_PI, op0=ALU.is_gt, op1=ALU.mult))
                op(lambda e: e.tensor_tensor(out=sB, in0=sB, in1=sC, op=ALU.add))
                op(lambda e: e.tensor_scalar(out=sB, in0=sB, scalar1=-PI, scalar2=TWO_PI, op0=ALU.is_lt, op1=ALU.mult) if False else
                   e.tensor_scalar(out=sB, in0=sB, scalar1=PI, scalar2=-PI, op0=ALU.min, op1=ALU.max))
                B.op("act", lambda e: e.activation(out=cs[:, 1, :], in_=sB, func=AF.Sin, scale=sgn), R=[T_rs, T_cst], W=[T_cs])
                op(lambda e: e.tensor_scalar(out=sC, in0=sB, scalar1=PI / 2, scalar2=None, op0=ALU.add))
                sAf = sA
                op(lambda e: e.tensor_scalar(out=sAf, in0=sC, scalar1=PI, scalar2=-TWO_PI, op0=ALU.is_gt, op1=ALU.mult))
                op(lambda e: e.tensor_tensor(out=sC, in0=sC, in1=sAf, op=ALU.add))
                op(lambda e: e.tensor_scalar(out=sC, in0=sC, scalar1=PI, scalar2=-PI, op0=ALU.min, op1=ALU.max))
                B.op("act", lambda e: e.activation(out=cs[:, 0, :], in_=sC, func=AF.Sin), R=[T_rs], W=[T_cs])

            rope_tables()

            def adaln_cols(pieces, pb, col0=None):
                for piece in pieces:
                    s = wget()
                    wv = slot_view(s, 8, 512)
                    for jj in range(4):
                        j = (piece * 4 + jj) if col0 is None else (col0 + jj)
                        for kc in range(8):
                            B.op("pe", lambda e, wv=wv, jj=jj, kc=kc, j=j: e.matmul(
                                ps[:, pb, j:j + 1], lhsT=wv[:, kc, jj * 128:(jj + 1) * 128], rhs=cact_b[:, kc:kc + 1],
                                start=(kc == 0), stop=(kc == 7)),
                                R=[RING_T[s], T_small], W=[PSB[pb]], inc=(kc == 7))

            def adaln_piece_late(piece, pb):
                adaln_cols([piece], pb, col0=0)
                j0 = piece * 4
                B.op("act", lambda e: e.activation(out=mod[:, j0:j0 + 4], in_=ps[:, pb, 0:4], func=AF.Copy), R=[PSB[pb]], W=[T_mod2])
                B.op("dve", lambda e: e.tensor_tensor(out=mod[:, j0:j0 + 4], in0=mod[:, j0:j0 + 4], in1=bada[:, j0:j0 + 4], op=ALU.add),
                     R=[T_mod2, T_small], W=[T_mod2])

            pb_mod = nextbank()
            adaln_cols(range(4), pb_mod)
            B.op("dve", lambda e: e.tensor_tensor(out=mod[:, 0:16], in0=ps[:, pb_mod, 0:16], in1=bada[:, 0:16], op=ALU.add),
                 R=[PSB[pb_mod], T_small], W=[T_small])
            B.op("dve", lambda e: e.scalar_tensor_tensor(out=A1, in0=sc_a, scalar=1.0, in1=gcols[:, 0:8], op0=ALU.add, op1=ALU.mult),
                 R=[T_small], W=[T_small])
            B.op("dve", lambda e: e.tensor_tensor(out=stg32[:, 0, 0:64], in0=lamt[:, 0, :], in1=lamt[:, 1, :], op=ALU.mult), R=[T_small], W=[T_s32[0]])
            B.op("dve", lambda e: e.tensor_tensor(out=stg32[:, 0, 64:128], in0=lamt[:, 2, :], in1=lamt[:, 3, :], op=ALU.mult), R=[T_small], W=[T_s32[0]])
            B.op("dve", lambda e: e.tensor_reduce(out=ltmp[:, 0:2], in_=stg32[:, 0, 0:128].rearrange("p (a b) -> p a b", a=2), axis=AX.X, op=ALU.add),
                 R=[T_s32[0]], W=[T_small])
            B.op("act", lambda e: e.activation(out=ltmp[:, 2:4], in_=ltmp[:, 0:2], func=AF.Exp), R=[T_small], W=[T_small])
            B.op("dve", lambda e: e.scalar_tensor_tensor(out=lamc, in0=ltmp[:, 2:3], scalar=LAM_INIT, in1=ltmp[:, 3:4], op0=ALU.add, op1=ALU.subtract),
                 R=[T_small], W=[T_small])
            B.op("dve", lambda e: e.tensor_scalar(out=nlamc, in0=lamc, scalar1=-1.0, scalar2=None, op0=ALU.mult), R=[T_small], W=[T_small])
            B.op("dve", lambda e: e.tensor_scalar(out=gdB[:], in0=gdB[:], scalar1=(1.0 - LAM_INIT), scalar2=None, op0=ALU.mult), R=[T_small], W=[T_small])

            def norm_block(src, T_src, dst, T_dst, Acol, shcol, ncols, si, T_small=T_small, defer=False, presq=None):
                pb = nextbank()
                for fc in range(8):
                    if presq is not None:
                        qa, qT = presq[fc]
                        B.op("pe", lambda e, fc=fc, qa=qa: e.matmul(ps[:, pb, 0:ncols], lhsT=ones_b, rhs=qa, start=(fc == 0), stop=(fc == 7)),
                             R=[qT, T_cst], W=[PSB[pb]], inc=True)
                        continue
                    sj = 2 * si + fc % 2
                    if fc % 2 == 0:
                        B.op("act", lambda e, fc=fc, sj=sj: e.activation(out=stg[:, sj, 0:ncols], in_=src[:, fc, :], func=AF.Square),
                             R=T_src, W=[T_stg[sj]])
                    else:
                        B.op("dve", lambda e, fc=fc, sj=sj: e.tensor_tensor(out=stg[:, sj, 0:ncols], in0=src[:, fc, :], in1=src[:, fc, :], op=ALU.mult),
                             R=T_src, W=[T_stg[sj]])
                    B.op("pe", lambda e, fc=fc, sj=sj: e.matmul(ps[:, pb, 0:ncols], lhsT=ones_b, rhs=stg[:, sj, 0:ncols],
                                                              start=(fc == 0), stop=(fc == 7)),
                         R=[T_stg[sj], T_cst], W=[PSB[pb]], inc=True)
                rs = stg32[:, si, 0:ncols]
                B.op("act", lambda e: e.activation(out=rs, in_=ps[:, pb, 0:ncols], func=AF.Ln, bias=epsc, scale=1.0 / D),
                     R=[PSB[pb], T_small], W=[T_s32[si]])
                B.op("act", lambda e: e.activation(out=rs, in_=rs, func=AF.Exp, scale=-0.5), R=[T_s32[si]], W=[T_s32[si]])
                if defer:
                    return lambda: norm_mod(src, T_src, dst, T_dst, Acol, shcol, ncols, si, T_small)
                norm_mod(src, T_src, dst, T_dst, Acol, shcol, ncols, si, T_small)

            def norm_mod(src, T_src, dst, T_dst, Acol, shcol, ncols, si, T_small):
                rs = stg32[:, si, 0:ncols]
                for fc in range(8):
                    if shcol is not None:
                        tmp = stg32b[:, fc % 2, 0:ncols]
                        B.op("dve", lambda e, fc=fc, tmp=tmp: e.scalar_tensor_tensor(
                            out=tmp, in0=src[:, fc, :], scalar=Acol[:, fc:fc + 1], in1=rs, op0=ALU.mult, op1=ALU.mult),
                            R=list(T_src) + [T_s32[si], T_small], W=[T_s32b[fc % 2]])
                        B.op("act", lambda e, fc=fc, tmp=tmp: e.activation(out=dst[:, fc, :], in_=tmp, func=AF.Identity,
                                                                         bias=shcol[:, fc:fc + 1], scale=1.0),
                             R=[T_s32b[fc % 2], T_small], W=T_dst)
                    else:
                        B.op("dve", lambda e, fc=fc: e.scalar_tensor_tensor(
                            out=dst[:, fc, :], in0=src[:, fc, :], scalar=Acol[:, fc:fc + 1], in1=rs, op0=ALU.mult, op1=ALU.mult),
                            R=list(T_src) + [T_s32[si], T_small], W=T_dst)

            dg = {"i": 0}

            def build_diag(wcols, ntap, Tw):
                d = dg["i"] % 4
                dg["i"] += 1
                for j in range(ntap):
                    B.op("dve", lambda e, j=j, d=d: e.tensor_scalar(out=diag[:, d * 4 + j, :], in0=ident_f,
                                                                 scalar1=wcols[:, j:j + 1], scalar2=None, op0=ALU.mult),
                         R=[T_cst, Tw], W=[T_diag[d]])
                return d

            p2a = {"prev": None, "n": 0, "slots": None}

            def p2a_proj(cc, tb):
                if p2a["slots"] is None:
                    sq_ = wget()
                    sk_ = wget(la=2)
                    p2a["slots"] = (sq_, sk_)
                s = p2a["slots"][cc // 4]
                wv = slot_view(s, 8, 512)
                cj = cc % 4
                d = build_diag(cqk[:, cc, 0:4], 4, T_small)
                pb = nextbank()
                for kc in range(8):
                    B.op("pe", lambda e, kc=kc, pb=pb, tb=tb, wv=wv, cj=cj: e.matmul(
                        ps[:, pb, :], lhsT=wv[:, kc, cj * 128:(cj + 1) * 128], rhs=hT[:, kc, tb * 512:(tb + 1) * 512],
                        start=(kc == 0), stop=(kc == 7)), R=[T_hT[tb], RING_T[s]], W=[PSB[pb]], inc=(kc == 7))
                si = p2a["n"] % 2
                p2a["n"] += 1
                if tb == 0:
                    B.op("dve", lambda e, si=si: e.memset(stg[:, si, 0:3], 0.0), W=[T_stg[si]])
                else:
                    B.op("dve", lambda e, si=si, cc=cc: e.tensor_copy(out=stg[:, si, 0:3], in_=haloqk[:, cc, 0:3]), R=[T_hqk[cc]], W=[T_stg[si]])
                B.op("act", lambda e, si=si, pb=pb: e.activation(out=stg[:, si, 3:515], in_=ps[:, pb, :], func=AF.Copy),
                     R=[PSB[pb]], W=[T_stg[si]])
                B.op("dve", lambda e, si=si, cc=cc: e.tensor_copy(out=haloqk[:, cc, 0:3], in_=stg[:, si, 512:515]), R=[T_stg[si]], W=[T_hqk[cc]])
                return (cc, tb, si, d)

            def p2a_conv(cc, tb, si, d):
                pb2 = nextbank()
                for j in range(4):
                    B.op("pe", lambda e, j=j, si=si, pb2=pb2, d=d: e.matmul(
                        ps[:, pb2, :], lhsT=diag[:, d * 4 + j, :], rhs=stg[:, si, j:j + 512], start=(j == 0), stop=(j == 3)),
                        R=[T_stg[si], T_diag[d]], W=[PSB[pb2]], inc=(j == 3))
                B.op("act", lambda e, pb2=pb2, cc=cc, tb=tb: e.activation(
                    out=qkT[:, cc, tb * 512:(tb + 1) * 512], in_=ps[:, pb2, :], func=AF.Silu, bias=cqk[:, cc, 4:5], scale=1.0),
                    R=[PSB[pb2], T_small], W=[T_qk[cc][tb]] + ([T_rs] if cc < 6 else []))

            def p2a_push(cc, tb):
                h_ = p2a_proj(cc, tb)
                if p2a["prev"] is not None:
                    p2a_conv(*p2a["prev"])
                p2a["prev"] = h_

            def p2a_flush():
                if p2a["prev"] is not None:
                    p2a_conv(*p2a["prev"])
                    p2a["prev"] = None

            prev_mod = None
            for tb in range(NB):
                xb, Tx = [(xin, T_xin), (xinB, T_xinB), (xinC, T_xinC), (xin, T_xin)][tb]
                B.dma("sp", xb, xT_d[:, :, tb * 512:(tb + 1) * 512], W=[Tx], R=([RING_T[3]] if tb == 2 else []))
                m = norm_block(xb, [Tx], hT[:, :, tb * 512:(tb + 1) * 512], [T_hT[tb]], A1, sh_a, 512, tb % 2, defer=True)
                if tb >= 2:
                    for cc in range(4 * (tb - 2), 4 * (tb - 2) + 4):
                        p2a_push(cc, 0)
                if prev_mod is not None:
                    prev_mod()
                prev_mod = m
            for cc in range(4):
                p2a_push(cc, 1)
            prev_mod()
            for cc in range(4, 8):
                p2a_push(cc, 1)
            for tb in (2, 3):
                for cc in range(8):
                    p2a_push(cc, tb)
            p2a_flush()
            if DEBUG == "hT":
                dbg_dump(hT, T_hT)

            if DEBUG == "qkm":
                dbg_dump(qkT, [t for row in T_qk for t in row])

            def tokmajor_v(s_v, hook):
                wv_v = slot_view(s_v, 8, 512)
                for tt in range(NT):
                    if tt == 8 and hook is not None:
                        hook()
                    tb = tt // 4
                    pv = nextbank()
                    for kc in range(8):
                        B.op("pe", lambda e, kc=kc, pv=pv, tt=tt: e.matmul(ps[:, pv, :], lhsT=hT[:, kc, tt * 128:(tt + 1) * 128], rhs=wv_v[:, kc, :],
                                                                         start=(kc == 0), stop=(kc == 7)),
                             R=[T_hT[tb], RING_T[s_v]], W=[PSB[pv]], inc=(kc == 7))
                    B.op("dve", lambda e, pv=pv, tt=tt: e.tensor_copy(out=vaug[:, tt, :, 0:128], in_=ps[:, pv, :].rearrange("p (h e) -> p h e", h=4)),
                         R=[PSB[pv]], W=[T_vaug[tt]])

            B.op("dve", lambda e: e.memset(vaug[:, :, :, 128:129], 1.0), W=T_vaug + [T_xinB])
            s_g = wget()
            wv_g = slot_view(s_g, 8, 8)
            for tt in range(NT):
                tb = tt // 4
                pg = nextbank()
                for kc in range(8):
                    lhs = hT[:, kc, tt * 128:(tt + 1) * 128]
                    B.op("pe", lambda e, lhs=lhs, kc=kc, pg=pg: e.matmul(ps[:, pg, 0:8], lhsT=lhs, rhs=wv_g[:, kc, :], start=(kc == 0), stop=(kc == 7)),
                         R=[T_hT[tb], RING_T[s_g]], W=[PSB[pg]], inc=(kc == 7))
                B.op("dve", lambda e, pg=pg, tt=tt: e.tensor_tensor(out=gts[:, tt, :], in0=ps[:, pg, 0:8], in1=bif[:, tt, :], op=ALU.add),
                     R=[PSB[pg], T_bif], W=[T_gts])

            g3 = lambda i: gmath[:, i, :].rearrange("p (n h) -> p n h", n=16)
            B.op("act", lambda e: e.activation(out=g3(0), in_=gts[:, :, 4:8], func=AF.Exp, scale=-1.0), R=[T_gts], W=[T_gm])
            B.op("act", lambda e: e.activation(out=g3(1), in_=g3(0), func=AF.Ln, bias=onec, scale=1.0), R=[T_gm, T_small], W=[T_gm])
            B.op("act", lambda e: e.activation(out=g3(4), in_=gts[:, :, 0:4], func=AF.Exp), R=[T_gts], W=[T_gm])

            def gate_math_2():
                pbw, pbt = nextbank(), nextbank()
                B.op("pe", lambda e: e.matmul(ps[:, pbw, 0:64], lhsT=triu_f, rhs=gmath[:, 1, :], start=True, stop=True), R=[T_gm, T_cst], W=[PSB[pbw]])
                B.op("pe", lambda e: e.matmul(ps[:, pbt, 0:64], lhsT=ones_f, rhs=gmath[:, 1, :], start=True, stop=True), R=[T_gm, T_cst], W=[PSB[pbt]])
                B.op("dve", lambda e: e.tensor_copy(out=gmath[:, 6, :], in_=ps[:, pbt, 0:64]), R=[PSB[pbt]], W=[T_gm])
                B.op("dve", lambda e: e.tensor_tensor(out=gmath[:, 7, :], in0=ps[:, pbw, 0:64], in1=gmath[:, 6, :], op=ALU.subtract), R=[PSB[pbw], T_gm], W=[T_gm])
                B.op("act", lambda e: e.activation(out=gmath[:, 2, :], in_=gmath[:, 7, :], func=AF.Exp), R=[T_gm], W=[T_gm])
                B.op("act", lambda e: e.activation(out=gmath[:, 3, :], in_=gmath[:, 6, :], func=AF.Exp, scale=-1.0), R=[T_gm], W=[T_gm])
                B.op("dve", lambda e: e.scalar_tensor_tensor(out=gmath[:, 5, :], in0=gmath[:, 4, :], scalar=128.0 ** -0.5, in1=gmath[:, 2, :], op0=ALU.mult, op1=ALU.mult),
                     R=[T_gm], W=[T_gm])

            s_v = wget()
            tokmajor_v(s_v, gate_math_2)
            s_o = wget()
            wv_o = slot_view(s_o, 8, 512)
            for tt in range(NT):
                tb = tt // 4
                po = nextbank()
                for kc in range(8):
                    lhs = hT[:, kc, tt * 128:(tt + 1) * 128]
                    B.op("pe", lambda e, lhs=lhs, kc=kc, po=po: e.matmul(ps[:, po, :], lhsT=lhs, rhs=wv_o[:, kc, :], start=(kc == 0), stop=(kc == 7)),
                         R=[T_hT[tb], RING_T[s_o]], W=[PSB[po]], inc=(kc == 7))
                B.op("act", lambda e, po=po, tt=tt: e.activation(out=og[:, tt, :], in_=ps[:, po, :], func=AF.Sigmoid),
                     R=[PSB[po]], W=[T_og[tt], T_xin, T_xinC])

            def group_norm(src, Tsrc, gBs, dsts, Tdst, gate=None, Tgate=()):
                sq = stg32[:, 0, :].rearrange("p (h e) -> p h e", h=4)
                B.op("act", lambda e: e.activation(out=sq, in_=src, func=AF.Square), R=[Tsrc], W=[T_s32[0]])
                B.op("dve", lambda e: e.tensor_reduce(out=finc[:, 16:20], in_=sq, axis=AX.X, op=ALU.add), R=[T_s32[0]], W=[T_finc])
                B.op("act", lambda e: e.activation(out=finc[:, 20:24], in_=finc[:, 16:20], func=AF.Ln, bias=epsc, scale=1.0 / 128.0),
                     R=[T_finc, T_small], W=[T_finc])
                B.op("act", lambda e: e.activation(out=finc[:, 24:28], in_=finc[:, 20:24], func=AF.Exp, scale=-0.5), R=[T_finc], W=[T_finc])
                for i in range(4):
                    if gate is None:
                        B.op("dve", lambda e, i=i: e.scalar_tensor_tensor(
                            out=dsts[i], in0=src[:, i, :], scalar=finc[:, 24 + i:25 + i], in1=gBs[i], op0=ALU.mult, op1=ALU.mult),
                            R=[Tsrc, T_finc, T_small], W=Tdst)
                    else:
                        B.op("dve", lambda e, i=i: e.scalar_tensor_tensor(
                            out=stg32[:, 1, i * 128:(i + 1) * 128], in0=src[:, i, :], scalar=finc[:, 24 + i:25 + i], in1=gBs[i],
                            op0=ALU.mult, op1=ALU.mult), R=[Tsrc, T_finc, T_small], W=[T_s32[1]])
                if gate is not None:
                    B.op("dve", lambda e: e.tensor_tensor(out=dsts, in0=stg32[:, 1, :], in1=gate, op=ALU.mult),
                         R=[T_s32[1]] + list(Tgate), W=Tdst)

            for h in range(4):
                B.op("dve", lambda e, h=h: e.memset(cst32[:, h, :], 0.0), W=[T_C32[h]])
            s_qd = wget()
            s_kd = wget(la=2)
            s_vd = wget(la=1)
            wv_qd = [slot_view(s_qd, 8, 512), slot_view(s_kd, 8, 512)]
            wv_vd = slot_view(s_vd, 8, 512)
            ucnt = {"i": 0}

            def p2b_proj(cc, tb):
                g, cj = cc // 4, cc % 4
                s, wv = (s_qd, s_kd)[g], wv_qd[g]
                sl = slice(tb * 512, (tb + 1) * 512)
                pb = nextbank(0, 6)
                for kc in range(8):
                    B.op("pe", lambda e, kc=kc, pb=pb, sl=sl, wv=wv, cj=cj: e.matmul(
                        ps[:, pb, :], lhsT=wv[:, kc, cj * 128:(cj + 1) * 128], rhs=hT[:, kc, sl],
                        start=(kc == 0), stop=(kc == 7)), R=[T_hT[tb], RING_T[s]], W=[PSB[pb]], inc=(kc == 7))
                k = (ucnt["i"] % 2) * 2
                ucnt["i"] += 1
                B.op("dve", lambda e, k=k, pb=pb, sl=sl: e.tensor_tensor(out=stg[:, k, 0:512], in0=ps[:, pb, :], in1=cs[:, 1, sl], op=ALU.mult),
                     R=[PSB[pb], T_cs], W=[T_stg[k]])
                B.op("dve", lambda e, k=k, pb=pb, sl=sl: e.tensor_tensor(out=stg[:, k + 1, 0:512], in0=ps[:, pb, :], in1=cs[:, 0, sl], op=ALU.mult),
                     R=[PSB[pb], T_cs], W=[T_stg[k + 1]])
                return (cc, tb, k)

            def p2b_fin(cc, tb, k):
                sl = slice(tb * 512, (tb + 1) * 512)
                pb2 = nextbank(0, 6)
                B.op("pe", lambda e, k=k, pb2=pb2: e.matmul(ps[:, pb2, :], lhsT=perm_b, rhs=stg[:, k, 0:512], start=True, stop=False),
                     R=[T_stg[k], T_cst], W=[PSB[pb2]], inc=False)
                B.op("pe", lambda e, k=k, pb2=pb2: e.matmul(ps[:, pb2, :], lhsT=ident_b, rhs=stg[:, k + 1, 0:512], start=False, stop=True),
                     R=[T_stg[k + 1], T_cst], W=[PSB[pb2]], inc=True)
                B.op("act", lambda e, pb2=pb2, cc=cc, sl=sl: e.activation(out=qkT[:, cc, sl], in_=ps[:, pb2, :], func=AF.Copy),
                     R=[PSB[pb2]], W=[T_qk[cc][tb]])

            def vd_tile(tt):
                tb = tt // 4
                pv = nextbank(0, 6)
                for kc in range(8):
                    B.op("pe", lambda e, kc=kc, pv=pv, tt=tt: e.matmul(ps[:, pv, :], lhsT=hT[:, kc, tt * 128:(tt + 1) * 128], rhs=wv_vd[:, kc, :],
                                                                     start=(kc == 0), stop=(kc == 7)),
                         R=[T_hT[tb], RING_T[s_vd]], W=[PSB[pv]], inc=(kc == 7))
                B.op("act", lambda e, pv=pv, tt=tt: e.activation(out=vaug[:, tt, :, 0:128], in_=ps[:, pv, :].rearrange("p (h e) -> p h e", h=4), func=AF.Copy),
                     R=[PSB[pv]], W=[T_vaug[tt]])

            pending = []
            ETq = et[:].rearrange("p a (b c) -> p (a b) c", c=128)
            T_etq = [T("etq%d" % i) for i in range(8)]

            def mmain(tt):
                tb = tt // 4
                tsl = slice(tt * 128, (tt + 1) * 128)
                ki = s = tt % 2
                pbk = nextbank(0, 6)
                psk = ps[:, pbk, :].bitcast(BF16)
                for h in range(4):
                    B.op("pe", lambda e, h=h: e.transpose(psk[:, h * 128:(h + 1) * 128], qkT[:, 4 + h, tsl], ident_b),
                         R=[T_qk[4 + h][tb], T_cst], W=[PSB[pbk]], inc=(h == 3))
                for h in range(4):
                    B.op("act", lambda e, h=h: e.activation(
                        out=kwt[:, ki, h * 128:(h + 1) * 128], in_=psk[:, h * 128:(h + 1) * 128], func=AF.Identity,
                        scale=gmath[:, 5, tt * 4 + h:tt * 4 + h + 1]),
                        R=[PSB[pbk], T_gm], W=[T_kwt[ki]])
                for h in range(4):
                    dc = gmath[:, 3, tt * 4 + h:tt * 4 + h + 1]
                    B.op("act", lambda e, h=h, dc=dc: e.activation(out=cdb[:, h, 0:129], in_=cst32[:, h, 0:129], func=AF.Identity, scale=dc),
                         R=[T_C32[h], T_gm], W=[T_cdb[h]])
                pst = nextbank(0, 6)
                for h in range(4):
                    B.op("pe", lambda e, h=h: e.matmul(ps[:, pst, h * 128:(h + 1) * 128], lhsT=qkT[:, 4 + h, tsl], rhs=qkT[:, h, tsl], start=True, stop=True),
                         R=[T_qk[4 + h][tb], T_qk[h][tb]], W=[PSB[pst]], inc=(h == 3))
                for h in range(4):
                    wc = gmath[:, 5, tt * 4 + h:tt * 4 + h + 1]
                    B.op("dve", lambda e, h=h, wc=wc: e.scalar_tensor_tensor(
                        out=ETq[:, s * 4 + h, :], in0=ps[:, pst, h * 128:(h + 1) * 128], scalar=wc, in1=triu_f, op0=ALU.mult, op1=ALU.mult),
                        R=[PSB[pst], T_gm, T_cst], W=[T_etq[s * 4 + h]])
                return (tt, tb, tsl, ki, s)

            def mmain2(tt, tb, tsl, ki, s):
                pu = [nextbank(0, 6), nextbank(0, 6)]
                for h in range(4):
                    acc = ps[:, 6 + h // 2, (h % 2) * 130:(h % 2) * 130 + 129]
                    B.op("pe", lambda e, h=h, acc=acc: e.matmul(acc, lhsT=ETq[:, s * 4 + h, :], rhs=vaug[:, tt, h, :], start=True, stop=False),
                         R=[T_etq[s * 4 + h], T_vaug[tt]], W=[PSB[6 + h // 2]], inc=False)
                    B.op("pe", lambda e, h=h, acc=acc: e.matmul(acc, lhsT=qkT[:, h, tsl], rhs=cdb[:, h, 0:129], start=False, stop=True),
                         R=[T_qk[h][tb], T_cdb[h]], W=[PSB[6 + h // 2]], inc=True)
                    B.op("pe", lambda e, h=h: e.matmul(ps[:, pu[h // 2], (h % 2) * 130:(h % 2) * 130 + 129], lhsT=kwt[:, ki, h * 128:(h + 1) * 128],
                                                       rhs=vaug[:, tt, h, :], start=True, stop=True),
                         R=[T_kwt[ki], T_vaug[tt]], W=[PSB[pu[h // 2]]])
                for h in range(4):
                    dc = gmath[:, 3, tt * 4 + h:tt * 4 + h + 1]
                    B.op("dve", lambda e, h=h, dc=dc: e.scalar_tensor_tensor(
                        out=cst32[:, h, 0:129], in0=cst32[:, h, 0:129], scalar=dc, in1=ps[:, pu[h // 2], (h % 2) * 130:(h % 2) * 130 + 129],
                        op0=ALU.mult, op1=ALU.add), R=[T_C32[h], PSB[pu[h // 2]], T_gm], W=[T_C32[h]])
                for hp in range(2):
                    pv2 = ps[:, 6 + hp, 0:260].rearrange("p (a b) -> p a b", a=2)
                    B.op("act", lambda e, hp=hp, pv2=pv2: e.activation(
                        out=stg32b[:, s, hp * 256:(hp + 1) * 256].rearrange("p (a b) -> p a b", a=2), in_=pv2[:, :, 0:128], func=AF.Copy),
                        R=[PSB[6 + hp]], W=[T_s32b[s]])
                    B.op("act", lambda e, hp=hp, pv2=pv2: e.activation(
                        out=finc[:, 56 + 4 * s + 2 * hp:58 + 4 * s + 2 * hp], in_=pv2[:, :, 128], func=AF.Abs),
                        R=[PSB[6 + hp]], W=[T_finc])

            def mfin(tt):
                s = tt % 2
                nums = stg32b[:, s, :].rearrange("p (h e) -> p h e", h=4)
                for h in range(4):
                    B.op("act", lambda e, h=h: e.activation(out=stg32[:, 0, h * 128:(h + 1) * 128], in_=nums[:, h, :], func=AF.Square,
                                                            accum_out=finc[:, 16 + h:17 + h]), R=[T_s32b[s]], W=[T_s32[0], T_ss])
                B.op("dve", lambda e: e.tensor_tensor(out=finc[:, 4:8], in0=finc[:, 56 + 4 * s:60 + 4 * s], in1=gmath[:, 2, tt * 4:tt * 4 + 4], op=ALU.max),
                     R=[T_finc, T_gm], W=[T_finc])
                B.op("dve", lambda e: e.reciprocal(out=finc[:, 8:12], in_=finc[:, 4:8]), R=[T_finc], W=[T_finc])
                B.op("dve", lambda e: e.tensor_tensor(out=finc[:, 12:16], in0=finc[:, 8:12], in1=finc[:, 8:12], op=ALU.mult), R=[T_finc], W=[T_finc])
                B.op("dve", lambda e: e.tensor_tensor(out=finc[:, 20:24], in0=finc[:, 12:16], in1=finc[:, 16:20], op=ALU.mult), R=[T_finc, T_ss], W=[T_finc])
                B.op("act", lambda e: e.activation(out=finc[:, 24:28], in_=finc[:, 20:24], func=AF.Ln, bias=epsc, scale=1.0 / 128.0),
                     R=[T_finc, T_small], W=[T_finc])
                B.op("act", lambda e: e.activation(out=finc[:, 28:32], in_=finc[:, 24:28], func=AF.Exp, scale=-0.5), R=[T_finc], W=[T_finc])
                B.op("dve", lambda e: e.tensor_tensor(out=finc[:, 12:16], in0=finc[:, 28:32], in1=finc[:, 8:12], op=ALU.mult), R=[T_finc], W=[T_finc])
                for h in range(4):
                    B.op("pool", lambda e, h=h: e.tensor_scalar(
                        out=stg32[:, 1, h * 128:(h + 1) * 128], in0=nums[:, h, :], scalar1=finc[:, 12 + h:13 + h], scalar2=1.0,
                        op0=ALU.mult, op1=ALU.mult), R=[T_s32b[s], T_finc], W=[T_s32[1]])
                B.op("pool", lambda e: e.tensor_tensor(out=stg32[:, 1, :], in0=stg32[:, 1, :], in1=gmB[:], op=ALU.mult),
                     R=[T_s32[1], T_small], W=[T_s32[1]])
                B.op("pool", lambda e: e.tensor_tensor(out=mix[:, tt, 0:512], in0=stg32[:, 1, :], in1=og[:, tt, :], op=ALU.mult),
                     R=[T_s32[1], T_og[tt]], W=[T_mix[tt], T_xin, T_xinC])

            def next_units(n):
                out = []
                for _ in range(n):
                    if pending:
                        out.append(pending.pop(0))
                return out

            st = mmain(0)
            mmain2(*st)
            for tt in range(1, NT + 1):
                units = next_units(2)
                st = mmain(tt) if tt < NT else None
                hs = [p2b_proj(*u) for u in units]
                if st is not None:
                    mmain2(*st)
                vd_tile(tt - 1)
                for hnd in hs:
                    p2b_fin(*hnd)
                mfin(tt - 1)
                if (tt - 1) % 4 == 3:
                    pending.extend((cc, (tt - 1) // 4) for cc in range(8))
            while pending:
                units = next_units(2)
                hs = [p2b_proj(*u) for u in units]
                for hnd in hs:
                    p2b_fin(*hnd)
            if DEBUG == "hm":
                dbg_dump(mix, T_mix)
            if DEBUG == "qkd":
                dbg_dump(qkT, [t for row in T_qk for t in row])

            numsb = stg32b[:].rearrange("p a (q e) -> p (a q) e", q=4)
            nums_q = numsb.rearrange("p (c q) e -> p q c e", c=2)
            dens_q = finc[:, 32:40].rearrange("p (c q) -> p q c", c=2)
            blk = {"i": 0}
            gn_pending = []
            fin2 = stg32[:, 1, :].rearrange("p (q e) -> p q e", q=4)
            T_fin2 = T_s32[1]
            for h in range(4):
                for qb in range(NB):
                    nkt = 4 * qb + 4
                    if blk["i"] < 8:
                        adaln_piece_late(4 + blk["i"], 3)
                    blk["i"] += 1

                    def scores(kt, h=h, qb=qb):
                        c0 = max(0, kt * 128 - qb * 512)
                        diagk = kt >= 4 * qb
                        for c in range(2):
                            rows = slice(c * 64, (c + 1) * 64)
                            bank = (kt % 2) * 2 + c
                            B.op("pe", lambda e, bank=bank, c0=c0, rows=rows, kt=kt: e.matmul(
                                ps[:, bank, c0:512], lhsT=qkT[rows, 4 + h, kt * 128:(kt + 1) * 128], rhs=qkT[rows, h, qb * 512 + c0:(qb + 1) * 512],
                                start=True, stop=not diagk),
                                R=[T_qk[4 + h][kt // 4], T_qk[h][qb]], W=[PSB[bank]], inc=not diagk)
                        if diagk:
                            for c in range(2):
                                rows = slice(c * 64, (c + 1) * 64)
                                bank = (kt % 2) * 2 + c
                                for hh in range(2):
                                    B.op("pe", lambda e, bank=bank, c0=c0, rows=rows, hh=hh: e.matmul(
                                        ps[:, bank, c0:c0 + 128], lhsT=cmask[rows, hh, :], rhs=cmask[rows, 2 + hh, :],
                                        start=False, stop=(hh == 1)),
                                        R=[T_cst], W=[PSB[bank]], inc=(hh == 1))

                    def exps(kt, h=h, qb=qb):
                        c0 = max(0, kt * 128 - qb * 512)
                        for c in range(2):
                            bank = (kt % 2) * 2 + c
                            B.op("act", lambda e, bank=bank, c0=c0: e.activation(out=stg[:, bank, c0:512], in_=ps[:, bank, c0:512], func=AF.Exp, scale=0.125),
                                 R=[PSB[bank]], W=[T_stg[bank]])

                    def pv(kt, h=h, qb=qb):
                        for c in range(2):
                            bank = (kt % 2) * 2 + c
                            for qi in range(4):
                                qt = 4 * qb + qi
                                if qt < kt:
                                    continue
                                B.op("pe", lambda e, c=c, qi=qi, bank=bank, kt=kt: e.matmul(
                                    ps[:, 4 + qi, c * 256:c * 256 + 129], lhsT=stg[:, bank, qi * 128:(qi + 1) * 128], rhs=vaug[:, kt, h, :],
                                    start=(kt == 0 and c == 0), stop=(kt == 4 * qb + qi), skip_group_check=True),
                                    R=[T_stg[bank], T_vaug[kt]], W=[PSB[4 + qi]], inc=True)
                        if kt >= 4 * qb:
                            qi = kt - 4 * qb
                            pview = ps[:, 4 + qi, :].rearrange("p (c x) -> p c x", c=2)
                            B.op("dve", lambda e, qi=qi, pview=pview: e.tensor_copy(out=nums_q[:, qi], in_=pview[:, :, 0:128]),
                                 R=[PSB[4 + qi]], W=T_s32b)
                            B.op("dve", lambda e, qi=qi, pview=pview: e.tensor_copy(out=dens_q[:, qi], in_=pview[:, :, 128]),
                                 R=[PSB[4 + qi]], W=[T_finc])

                    scores(0)
                    for kt in range(nkt):
                        if kt + 1 < nkt:
                            scores(kt + 1)
                        exps(kt)
                        if kt == 1 and gn_pending:
                            gn_pending[0][0]()
                        if kt == 3 and gn_pending:
                            gn_pending.pop(0)[1]()
                        pv(kt)
                    B.op("dve", lambda e: e.reciprocal(out=finc[:, 40:48], in_=finc[:, 32:40]), R=[T_finc], W=[T_finc])
                    B.op("dve", lambda e: e.tensor_scalar(out=finc[:, 48:52], in0=finc[:, 44:48], scalar1=nlamc, scalar2=None, op0=ALU.mult),
                         R=[T_finc, T_small], W=[T_finc])
                    for qi in range(4):
                        B.op("dve", lambda e, qi=qi: e.tensor_scalar(out=fin[:, qi, :], in0=numsb[:, qi, :], scalar1=finc[:, 40 + qi:41 + qi],
                                                                   scalar2=None, op0=ALU.mult), R=T_s32b + [T_finc], W=[T_fin])
                    for qi in range(4):
                        B.op("dve", lambda e, qi=qi: e.scalar_tensor_tensor(out=fin[:, qi, :], in0=numsb[:, 4 + qi, :], scalar=finc[:, 48 + qi:49 + qi],
                                                                          in1=fin[:, qi, :], op0=ALU.mult, op1=ALU.add),
                             R=T_s32b + [T_finc, T_fin], W=[T_fin])
                    def _gnA():
                        sq = stg32[:, 0, :].rearrange("p (h e) -> p h e", h=4)
                        B.op("dve", lambda e: e.tensor_tensor(out=sq, in0=fin[:], in1=fin[:], op=ALU.mult), R=[T_fin], W=[T_s32[0]])
                        B.op("dve", lambda e: e.tensor_reduce(out=finc[:, 16:20], in_=sq, axis=AX.X, op=ALU.add), R=[T_s32[0]], W=[T_finc])

                    def _gnB(h=h, qb=qb):
                        gB = gdB[:, h * 128:(h + 1) * 128]
                        B.op("act", lambda e: e.activation(out=finc[:, 20:24], in_=finc[:, 16:20], func=AF.Ln, bias=epsc, scale=1.0 / 128.0),
                             R=[T_finc, T_small], W=[T_finc])
                        B.op("act", lambda e: e.activation(out=finc[:, 24:28], in_=finc[:, 20:24], func=AF.Exp, scale=-0.5), R=[T_finc], W=[T_finc])
                        for qi in range(4):
                            B.op("dve", lambda e, qi=qi: e.scalar_tensor_tensor(
                                out=mix[:, 4 * qb + qi, 512 + h * 128:512 + (h + 1) * 128], in0=fin[:, qi, :], scalar=finc[:, 24 + qi:25 + qi], in1=gB,
                                op0=ALU.mult, op1=ALU.mult), R=[T_fin, T_finc, T_small],
                                W=[T_mix[4 * qb + qi], T_og[4 * qb + qi], T_xin, T_xinC])
                    gn_pending.append((_gnA, _gnB))
            while gn_pending:
                a_, b_ = gn_pending.pop(0)
                a_()
                b_()
            B.op("dve", lambda e: e.scalar_tensor_tensor(out=A2, in0=sc_f, scalar=1.0, in1=gcols[:, 8:16], op0=ALU.add, op1=ALU.mult),
                 R=[T_small, T_mod2], W=[T_mod2])
            if DEBUG == "mix":
                dbg_dump(mix, T_mix)

            for tt in range(NT):
                tb = tt // 4
                pbk = nextbank()
                psk = ps[:, pbk, :].bitcast(BF16)
                for fc in range(8):
                    B.op("pe", lambda e, fc=fc, psk=psk, tt=tt: e.transpose(psk[:, fc * 128:(fc + 1) * 128], mix[:, tt, fc * 128:(fc + 1) * 128], ident_b),
                         R=[T_mix[tt], T_cst], W=[PSB[pbk]], inc=(fc == 7))
                if tt % 2 == 0:
                    B.op("act", lambda e, psk=psk, tt=tt: e.activation(out=mixT[:, :, tt * 128:(tt + 1) * 128], in_=psk.rearrange("p (c t) -> p c t", c=8), func=AF.Copy),
                         R=[PSB[pbk]], W=[T_mixT[tb]] + T_hT)
                else:
                    B.op("dve", lambda e, psk=psk, tt=tt: e.tensor_copy(out=mixT[:, :, tt * 128:(tt + 1) * 128], in_=psk.rearrange("p (c t) -> p c t", c=8)),
                         R=[PSB[pbk]], W=[T_mixT[tb]] + T_hT)
            s_w0 = wget()
            s_w1 = wget(la=2)
            first_x1 = True
            early_mod = []
            sq8 = [(stg[:, i, 0:512], T_stg[i]) for i in range(4)] + [(et[:, i, :], T_et[i]) for i in range(3)] + [(kwt[:, 0, :], T_kwt[0])]
            for tb in range(NB):
                sl = slice(tb * 512, (tb + 1) * 512)
                xb, Tx = (xin, T_xin) if tb % 2 == 0 else (xinC, T_xinC)
                half = slice(0, 8) if tb % 2 == 0 else slice(8, 16)
                B.dma("sp", xb, xT_d[:, :, sl], W=T_mix[half] + T_og[half] + [Tx])
                if tb in (1, 2):
                    k0 = tb - 1
                    for fc in range(8):
                        qa, qT = sq8[fc]
                        srcc = x1T[:, fc, k0 * 512:(k0 + 1) * 512]
                        if fc % 2 == 0:
                            B.op("act", lambda e, qa=qa, srcc=srcc: e.activation(out=qa, in_=srcc, func=AF.Square), R=[T_x1[fc][k0]], W=[qT])
                        else:
                            B.op("dve", lambda e, qa=qa, srcc=srcc: e.tensor_tensor(out=qa, in0=srcc, in1=srcc, op=ALU.mult), R=[T_x1[fc][k0]], W=[qT])
                for fo in range(8):
                    s = s_w0 if fo < 4 else s_w1
                    wv = slot_view(s, 8, 512)
                    pb = nextbank()
                    for kc in range(8):
                        B.op("pe", lambda e, kc=kc, pb=pb, wv=wv, fo=fo, sl=sl: e.matmul(
                            ps[:, pb, :], lhsT=wv[:, kc, (fo % 4) * 128:(fo % 4 + 1) * 128], rhs=mixT[:, kc, sl], start=(kc == 0), stop=(kc == 7)),
                            R=[T_mixT[tb], RING_T[s]], W=[PSB[pb]], inc=(kc == 7))
                    Wl = [T_x1[fo][tb]] + (ALL_RC if first_x1 else [])
                    first_x1 = False
                    B.op("dve", lambda e, pb=pb, fo=fo, sl=sl, xb=xb: e.scalar_tensor_tensor(
                        out=x1T[:, fo, sl], in0=ps[:, pb, :], scalar=gt_a[:, fo:fo + 1], in1=xb[:, fo, :], op0=ALU.mult, op1=ALU.add),
                        R=[PSB[pb], T_mod2, Tx], W=Wl)
                if tb in (1, 2):
                    k0 = tb - 1
                    early_mod.append(norm_block(x1T[:, :, k0 * 512:(k0 + 1) * 512], [T_x1[c][k0] for c in range(8)],
                                                h2T[:, :, k0 * 512:(k0 + 1) * 512], [T_h2[k0]] + T_mixT + T_hT, A2, sh_f, 512, k0,
                                                T_small=T_mod2, defer=True, presq=sq8))
            if DEBUG == "x1":
                dbg_dump(x1T, [t for row in T_x1 for t in row])

            def norm2(p, sbk):
                tb = p * 2 + sbk
                sl = slice(tb * 512, (tb + 1) * 512)
                norm_block(x1T[:, :, sl], [T_x1[c][tb] for c in range(8)], h2T[:, :, sbk * 512:(sbk + 1) * 512],
                           [T_h2[sbk]] + T_mixT + T_hT, A2, sh_f, 512, sbk, T_small=T_mod2)

            ostg = RA[:, 14336:16384].bitcast(F32).rearrange("p (a t) -> p a t", a=2)
            T_ostg = [T("ostg0"), T("ostg1")]
            sqb = [et[:, 0, :], et[:, 1, :], et[:, 2, :], kwt[:, 0, :]]
            T_sqb = [T_et[0], T_et[1], T_et[2], T_kwt[0]]
            SIDE_BANK = 7

            def staged_norm(tb, mode, sbk=None):
                sl = slice(tb * 512, (tb + 1) * 512)
                srcb = x1T[:, :, sl]
                Tsrc = [T_x1[c][tb] for c in range(8)]
                rs = stg32b[:, 0, :]

                def squares(f0):
                    for fc in range(f0, f0 + 4):
                        j = fc % 4
                        if fc % 2 == 0:
                            B.op("act", lambda e, fc=fc, j=j: e.activation(out=sqb[j], in_=srcb[:, fc, :], func=AF.Square), R=Tsrc, W=[T_sqb[j]])
                        else:
                            B.op("dve", lambda e, fc=fc, j=j: e.tensor_tensor(out=sqb[j], in0=srcb[:, fc, :], in1=srcb[:, fc, :], op=ALU.mult), R=Tsrc, W=[T_sqb[j]])

                def mms(f0):
                    for fc in range(f0, f0 + 4):
                        j = fc % 4
                        B.op("pe", lambda e, fc=fc, j=j: e.matmul(ps[:, SIDE_BANK, :], lhsT=ones_b, rhs=sqb[j], start=(fc == 0), stop=(fc == 7)),
                             R=[T_sqb[j], T_cst], W=[PSB[SIDE_BANK]], inc=True)

                squares(0)
                yield
                mms(0)
                squares(4)
                yield
                mms(4)
                B.op("act", lambda e: e.activation(out=rs, in_=ps[:, SIDE_BANK, :], func=AF.Ln, bias=epsc, scale=1.0 / D), R=[PSB[SIDE_BANK], T_small], W=[T_s32b[0]])
                B.op("act", lambda e: e.activation(out=rs, in_=rs, func=AF.Exp, scale=-0.5), R=[T_s32b[0]], W=[T_s32b[0]])
                yield
                for fc in range(8):
                    if mode == "out":
                        oj = fc % 2
                        B.op("dve", lambda e, fc=fc, oj=oj: e.scalar_tensor_tensor(
                            out=ostg[:, oj, :], in0=srcb[:, fc, :], scalar=g_fin[:, fc:fc + 1], in1=rs, op0=ALU.mult, op1=ALU.mult),
                            R=Tsrc + [T_s32b[0], T_small], W=[T_ostg[oj]])
                        B.dma("sp", out_d[:, fc, sl], ostg[:, oj, :], R=[T_ostg[oj]], is_out=True)
                        if fc < 7:
                            yield
                    else:
                        tmp = stg32b[:, 1, :]
                        B.op("dve", lambda e, fc=fc: e.scalar_tensor_tensor(
                            out=tmp, in0=srcb[:, fc, :], scalar=A2[:, fc:fc + 1], in1=rs, op0=ALU.mult, op1=ALU.mult),
                            R=Tsrc + [T_s32b[0], T_mod2], W=[T_s32b[1]])
                        B.op("act", lambda e, fc=fc: e.activation(out=h2T[:, fc, sbk * 512:(sbk + 1) * 512], in_=tmp, func=AF.Identity,
                                                                 bias=sh_f[:, fc:fc + 1], scale=1.0),
                             R=[T_s32b[1], T_mod2], W=[T_h2[sbk]])
                yield

            side = {"gens": []}

            def side_step():
                while side["gens"]:
                    try:
                        next(side["gens"][0])
                        return
                    except StopIteration:
                        side["gens"].pop(0)

            def final_big(tb):
                sl = slice(tb * 512, (tb + 1) * 512)
                ob = tb % 2
                T_oc = [T("oc%d" % fc) for fc in range(8)]
                modf = norm_block(x1T[:, :, sl], [T_x1[c][tb] for c in range(8)], outb[ob], [T_outb[ob]] + [t for row in T_act[0:16] for t in row],
                                  g_fin, None, 512, tb % 2, defer=True)
                rs_ = stg32[:, tb % 2, 0:512]
                first = True
                for fc in range(8):
                    B.op("dve", lambda e, fc=fc: e.scalar_tensor_tensor(
                        out=outb[ob][:, fc, :], in0=x1T[:, fc, sl], scalar=g_fin[:, fc:fc + 1], in1=rs_, op0=ALU.mult, op1=ALU.mult),
                        R=[T_x1[fc][tb], T_s32[tb % 2], T_small], W=[T_oc[fc]] + ([T_outb[ob]] + [t for row in T_act[0:16] for t in row] if first else []))
                    first = False
                    B.dma("sp", out_d[:, fc, sl], outb[ob][:, fc, :], R=[T_oc[fc]], is_out=True)

            T_outb = [T_xin, T_xinC]
            outb = [RBm[:, 0:8192].bitcast(F32).rearrange("p (c t) -> p c t", c=8),
                    RBm[:, 8192:16384].bitcast(F32).rearrange("p (c t) -> p c t", c=8)]

            for m_ in early_mod:
                m_()
            for p in range(2):
                for grp in range(11):
                    if p == 1 and grp == 1:
                        side["gens"] += [staged_norm(0, "out"), staged_norm(1, "out")]
                    s = wget()
                    wv = slot_view(s, 8, 512)
                    for pr in range(2):
                        i = grp * 2 + pr
                        if p == 1 and grp >= 1:
                            side_step()
                        chunks = (i, 22 + i)
                        dsets = [build_diag(cffn[:, ch, 0:3], 3, T_small) for ch in chunks]
                        for wi, ch in enumerate(chunks):
                            bi = (i % 2) * 2 + wi
                            B.op("dve", lambda e, bi=bi, ch=ch: e.tensor_copy(out=stg[:, bi, 0:2], in_=halo[:, ch, :]), R=[T_halo], W=[T_stg[bi]])
                            for sbk in range(2):
                                pb = nextbank(0, 7)
                                for kc in range(8):
                                    B.op("pe", lambda e, kc=kc, pb=pb, wv=wv, wi=wi, pr=pr, sbk=sbk: e.matmul(
                                        ps[:, pb, :], lhsT=wv[:, kc, wi * 256 + pr * 128:wi * 256 + (pr + 1) * 128],
                                        rhs=h2T[:, kc, sbk * 512:(sbk + 1) * 512], start=(kc == 0), stop=(kc == 7)),
                                        R=[T_h2[sbk], RING_T[s]], W=[PSB[pb]], inc=(kc == 7))
                                B.op("act", lambda e, bi=bi, pb=pb, sbk=sbk: e.activation(out=stg[:, bi, 2 + sbk * 512:2 + (sbk + 1) * 512], in_=ps[:, pb, :], func=AF.Copy),
                                     R=[PSB[pb]], W=[T_stg[bi]])
                            B.op("dve", lambda e, bi=bi, ch=ch: e.tensor_copy(out=halo[:, ch, :], in_=stg[:, bi, 1024:1026]), R=[T_stg[bi]], W=[T_halo])
                        ba, bg = (i % 2) * 2, (i % 2) * 2 + 1
                        pa2s, pg2s = [], []
                        for sbk in range(2):
                            pa2 = nextbank(0, 7)
                            pa2s.append(pa2)
                            for j in range(3):
                                B.op("pe", lambda e, j=j, pa2=pa2, sbk=sbk, ba=ba, d=dsets[0]: e.matmul(
                                    ps[:, pa2, :], lhsT=diag[:, d * 4 + j, :], rhs=stg[:, ba, sbk * 512 + j:sbk * 512 + j + 512], start=(j == 0), stop=(j == 2)),
                                    R=[T_stg[ba], T_diag[dsets[0]]], W=[PSB[pa2]], inc=(j == 2))
                        for sbk in range(2):
                            pg2 = nextbank(0, 7)
                            pg2s.append(pg2)
                            for j in range(3):
                                B.op("pe", lambda e, j=j, pg2=pg2, sbk=sbk, bg=bg, d=dsets[1]: e.matmul(
                                    ps[:, pg2, :], lhsT=diag[:, d * 4 + j, :], rhs=stg[:, bg, sbk * 512 + j:sbk * 512 + j + 512], start=(j == 0), stop=(j == 2)),
                                    R=[T_stg[bg], T_diag[dsets[1]]], W=[PSB[pg2]], inc=(j == 2))
                        for sbk in range(2):
                            pa2, pg2, si = pa2s[sbk], pg2s[sbk], sbk
                            B.op("act", lambda e, pg2=pg2, si=si, i=i: e.activation(out=stg32[:, si, :], in_=ps[:, pg2, :], func=AF.Silu, bias=cffn[:, 22 + i, 3:4], scale=1.0),
                                 R=[PSB[pg2], T_small], W=[T_s32[si]])
                            B.op("dve", lambda e, pa2=pa2, si=si, i=i, sbk=sbk: e.scalar_tensor_tensor(
                                out=actT(i)[:, sbk * 512:(sbk + 1) * 512], in0=ps[:, pa2, :], scalar=cffn[:, i, 3:4], in1=stg32[:, si, :], op0=ALU.add, op1=ALU.mult),
                                R=[PSB[pa2], T_s32[si], T_small], W=[T_act[i][sbk]] + (ALL_MIX if i < 16 else T_mixT + T_hT))
                def down(fo, sbks, s, hi=7):
                    wd = ring[:, s, 0:2816].rearrange("p (k n) -> p k n", k=22)
                    for sbk in sbks:
                        tb = p * 2 + sbk
                        sl = slice(tb * 512, (tb + 1) * 512)
                        pb = nextbank(0, hi)
                        for kc in range(22):
                            B.op("pe", lambda e, kc=kc, pb=pb, wd=wd, sbk=sbk: e.matmul(
                                ps[:, pb, :], lhsT=wd[:, kc, :], rhs=actT(kc)[:, sbk * 512:(sbk + 1) * 512], start=(kc == 0), stop=(kc == 21)),
                                R=[T_act[kc][sbk], RING_T[s]], W=[PSB[pb]], inc=(kc == 21))
                        B.op("dve", lambda e, pb=pb, fo=fo, sl=sl, tb=tb: e.scalar_tensor_tensor(
                            out=x1T[:, fo, sl], in0=ps[:, pb, :], scalar=gt_f[:, fo:fo + 1], in1=x1T[:, fo, sl], op0=ALU.mult, op1=ALU.add),
                            R=[PSB[pb], T_mod2, T_x1[fo][tb]], W=[T_x1[fo][tb]])

                if p == 0:
                    side["gens"] += [staged_norm(2, "ffn", 0), staged_norm(3, "ffn", 1)]
                    side_step()
                    for fo in range(8):
                        if fo < 7:
                            side_step()
                        down(fo, (0, 1), wget())
                else:
                    sl3 = slice(3 * 512, 4 * 512)

                    def last_sq(fo):
                        j = fo % 4
                        if fo % 2 == 0:
                            B.op("act", lambda e, fo=fo, j=j: e.activation(out=stg[:, j, 0:512], in_=x1T[:, fo, sl3], func=AF.Square), R=[T_x1[fo][3]], W=[T_stg[j]])
                        else:
                            B.op("dve", lambda e, fo=fo, j=j: e.tensor_tensor(out=stg[:, j, 0:512], in0=x1T[:, fo, sl3], in1=x1T[:, fo, sl3], op=ALU.mult), R=[T_x1[fo][3]], W=[T_stg[j]])

                    def last_mm(fo):
                        j = fo % 4
                        B.op("pe", lambda e, fo=fo, j=j: e.matmul(ps[:, 6, :], lhsT=ones_b, rhs=stg[:, j, 0:512], start=(fo == 0), stop=(fo == 7)),
                             R=[T_stg[j], T_cst], W=[PSB[6]], inc=True)

                    for sbk in range(2):
                        for fo in range(8):
                            down(fo, (sbk,), wget(), hi=(6 if sbk == 1 else 7))
                            if sbk == 1:
                                if fo == 0:
                                    side["gens"] += [staged_norm(2, "out")]
                                side_step()
                                if fo >= 1:
                                    last_mm(fo - 1)
                                last_sq(fo)
                    last_mm(7)
            if DEBUG == "x2":
                dbg_dump(x1T, [t for row in T_x1 for t in row])

            while side["gens"]:
                side_step()
            rs3 = stg32[:, 1, :]
            B.op("act", lambda e: e.activation(out=rs3, in_=ps[:, 6, :], func=AF.Ln, bias=epsc, scale=1.0 / D), R=[PSB[6], T_small], W=[T_s32[1]])
            B.op("act", lambda e: e.activation(out=rs3, in_=rs3, func=AF.Exp, scale=-0.5), R=[T_s32[1]], W=[T_s32[1]])
            T_oc = [T("oc%d" % fc) for fc in range(8)]
            for fc in range(8):
                B.op("dve", lambda e, fc=fc: e.scalar_tensor_tensor(
                    out=outb[1][:, fc, :], in0=x1T[:, fc, 3 * 512:4 * 512], scalar=g_fin[:, fc:fc + 1], in1=rs3, op0=ALU.mult, op1=ALU.mult),
                    R=[T_x1[fc][3], T_s32[1], T_small], W=[T_oc[fc]] + ([T_outb[1]] + [t for row in T_act[0:16] for t in row] if fc == 0 else []))
                B.dma("sp", out_d[:, fc, 3 * 512:4 * 512], outb[1][:, fc, :], R=[T_oc[fc]], is_out=True)

        try:
            body()
        except _Stop:
            pass
        B.finish()
        block = es.enter_context(nc.Block())
        B.emit(block)
    return nc


def _chunk_rows(w, kc):
    n = w.shape[1]
    return np.ascontiguousarray(w.reshape(kc, 128, n).transpose(1, 0, 2))


def _consts():
    cst = np.zeros((128, 5, 128), np.float32)
    cst[:, 0, :] = np.eye(128, dtype=np.float32)
    cst[:, 1, :] = np.triu(np.ones((128, 128), np.float32))
    cst[:, 2, :] = 1.0
    p = np.arange(128)
    perm = np.zeros((128, 128), np.float32)
    perm[p ^ 32, p] = 1.0
    cst[:, 3, :] = perm
    half = 32
    inv_freq = (np.float32(10000.0) ** (-np.arange(half, dtype=np.float32) / np.float32(half))).astype(np.float32)
    cst[:, 4, 0] = inv_freq[p % 32]
    cst[:, 4, 1] = np.where((p % 64) < 32, 1.0, -1.0)
    return cst


_NC_CACHE = {}


def kernel(x, c, positions, w_ada, b_ada, g_mix, w_in, conv_qk_w, conv_qk_b, b_if, g_mlstm,
           lam_q1, lam_k1, lam_q2, lam_k2, g_diff, w_out, g_ffn, w_up, conv_ffn_w, conv_ffn_b,
           w_down, g_final):
    f32 = lambda a: np.asarray(a, dtype=np.float32)
    x, c = f32(x), f32(c)
    positions = np.asarray(positions, dtype=np.int32)
    w_ada, b_ada, w_in, w_out, w_up, w_down = f32(w_ada)[0], f32(b_ada)[0], f32(w_in)[0], f32(w_out)[0], f32(w_up)[0], f32(w_down)[0]
    nb = x.shape[0]

    def pieces(w, cols_list):
        out = []
        for cols in cols_list:
            out.append(_chunk_rows(w[:, cols], 8).reshape(128, 4096))
        return np.ascontiguousarray(np.stack(out))

    wada_p = pieces(w_ada, [np.arange(i * 512, (i + 1) * 512) for i in range(12)])
    win_cols = [np.arange(0, 512), np.arange(512, 1024), np.arange(1024, 1536), np.arange(1536, 2048),
                np.arange(2056, 2568), np.arange(2568, 3080), np.arange(3080, 3592)]
    win_p = pieces(w_in, win_cols)
    wgate_p = np.ascontiguousarray(_chunk_rows(w_in[:, 2048:2056], 8).reshape(128, 64))
    wout_p = pieces(w_out, [np.arange(0, 512), np.arange(512, 1024)])
    up_cols = []
    for g in range(11):
        a = np.arange(g * 256, (g + 1) * 256)
        up_cols.append(np.concatenate([a, 2816 + a]))
    wup_p = pieces(w_up, up_cols)
    wdown_p = np.ascontiguousarray(np.stack([_chunk_rows(w_down[:, f * 128:(f + 1) * 128], 22).reshape(128, 2816) for f in range(8)]))
    col8 = lambda v: np.ascontiguousarray(f32(v).reshape(-1, 128).T)
    bada_p = col8(b_ada)
    gcols = np.ascontiguousarray(np.concatenate([col8(f32(g_mix)[0]), col8(f32(g_ffn)[0]), col8(f32(g_final))], axis=1))
    cqk = np.ascontiguousarray(np.concatenate([f32(conv_qk_w)[0].reshape(4, 8, 128).transpose(2, 1, 0),
                                               f32(conv_qk_b)[0].reshape(8, 128).T[:, :, None]], axis=2))
    cffn = np.ascontiguousarray(np.concatenate([f32(conv_ffn_w)[0].reshape(3, 44, 128).transpose(2, 1, 0),
                                                f32(conv_ffn_b)[0].reshape(44, 128).T[:, :, None]], axis=2))
    bif = np.ascontiguousarray(np.broadcast_to(f32(b_if)[0][None, None, :], (128, 16, 8)))
    gmB = np.ascontiguousarray(np.broadcast_to(f32(g_mlstm)[0][None, :], (128, 512)))
    gdB = np.ascontiguousarray(np.broadcast_to(f32(g_diff)[0][None, :], (128, 512)))
    lam = np.ascontiguousarray(np.broadcast_to(np.stack([f32(lam_q1)[0], f32(lam_k1)[0], f32(lam_q2)[0], f32(lam_k2)[0]])[None], (128, 4, 64)))
    cst = _consts()
    pp = np.arange(128)
    cmk = np.zeros((128, 4, 128), np.float32)
    for hh in range(2):
        cmk[pp, hh, (pp % 64) + 64 * hh] = 1.0
        cmk[:, 2 + hh, :] = np.where(((pp % 64) + 64 * hh)[:, None] > np.arange(128)[None, :], -30000.0, 0.0)

    in_maps = []
    for b in range(nb):
        xT = np.ascontiguousarray(x[b].T.reshape(8, 128, S).transpose(1, 0, 2))
        in_maps.append({
            "xT": xT, "c": np.ascontiguousarray(c[b].reshape(8, 128).T),
            "pos": np.ascontiguousarray(np.broadcast_to(positions[b][None, :], (128, S))),
            "w_ada": wada_p, "b_ada": bada_p, "gcols": gcols, "w_in": win_p, "w_gate": wgate_p, "cqk": cqk, "bif": bif,
            "gmB": gmB, "gdB": gdB, "lam": lam, "w_out": wout_p, "w_up": wup_p, "cffn": cffn, "w_down": wdown_p, "consts": cst, "cmask": cmk,
        })
    if "nc" not in _NC_CACHE:
        _NC_CACHE["nc"] = build_program()
    nc = _NC_CACHE["nc"]
    res = run_bass_kernel_spmd(nc, in_maps, core_ids=list(range(nb)))
    outs = []
    for b in range(nb):
        oT = np.asarray(res.results[b]["outT"], dtype=np.float32)
        outs.append(oT.transpose(1, 0, 2).reshape(D, S).T)
    out = np.ascontiguousarray(np.stack(outs)).astype(np.float32)
    if DEBUG:
        kernel.dbg = [np.asarray(res.results[b]["dbg"]) for b in range(nb)]
    return out
```

```python
import os
import math
import numpy as np
from contextlib import ExitStack
import concourse.bass as bass
import concourse.mybir as mybir
from concourse.bass_utils import run_bass_kernel_spmd

F32 = mybir.dt.float32
BF16 = mybir.dt.bfloat16
I32 = mybir.dt.int32
AF = mybir.ActivationFunctionType
ALU = mybir.AluOpType
AX = mybir.AxisListType

S = 2048
D = 1024
NT = 16
NB = 4
EPS = 1e-6
LAM_INIT = 0.8 - 0.6 * math.exp(-0.3 * 0)
PI = math.pi
TWO_PI = 2.0 * math.pi
CW1 = 6.28125
CW2 = TWO_PI - 6.28125
NSLOT = 4
LOOKAHEAD = 3
DEBUG = os.environ.get("MK_DEBUG", "")


class T:
    __slots__ = ("name", "w", "r", "excl", "wl")

    def __init__(self, name="", excl=False):
        self.name = name
        self.w = None
        self.r = {}
        self.wl = []
        self.excl = excl


class Builder:
    ENG = ("pe", "dve", "act", "pool", "sp")

    def __init__(self, nc, es):
        self.nc = nc
        self.sem = {k: es.enter_context(nc.semaphore("sem_" + k)) for k in self.ENG}
        self.cnt = {k: 0 for k in self.ENG}
        self.ops = {k: [] for k in self.ENG}
        self.known = {k: {} for k in self.ENG}
        self.NRING = 24
        self.ring = [es.enter_context(nc.semaphore("dma%d" % i)) for i in range(self.NRING)]
        self.ring_cnt = [0] * self.NRING
        self.ring_i = 0
        self.out_tokens = []
        self.swsem = []
        self._es = es

    def _semobj(self, key):
        if isinstance(key, tuple):
            return self.swsem[key[1]]
        return self.sem[key] if isinstance(key, str) else self.ring[key]

    def _need(self, eng, tok, waits, raw):
        if tok is None:
            return
        key, val = tok
        if key == eng and not raw and eng != "pool":
            return
        if self.known[eng].get(key, 0) >= val:
            return
        self.known[eng][key] = val
        waits.append((key, val))

    def _deps(self, eng, R, W, soft=False):
        waits = []
        for t in R:
            self._need(eng, t.w, waits, True)
            for tk in t.wl:
                self._need(eng, tk, waits, True)
            if t.excl:
                for k, v in t.r.items():
                    self._need(eng, (k, v), waits, False)
        for t in W:
            if not soft:
                self._need(eng, t.w, waits, False)
                for tk in t.wl:
                    self._need(eng, tk, waits, False)
            for k, v in t.r.items():
                self._need(eng, (k, v), waits, False)
        return waits

    def _commit(self, tok, R, W, soft=False):
        for t in W:
            if soft:
                t.wl.append(tok)
            else:
                t.w = tok
                t.wl = []
            t.r = {}
        k, v = tok
        for t in R:
            if t.r.get(k, 0) < v:
                t.r[k] = v

    def op(self, eng, fn, R=(), W=(), inc=True, soft=False):
        waits = self._deps(eng, R, W, soft)
        tok = (eng, self.cnt[eng] + 1)
        if inc:
            self.cnt[eng] += 1
        self._commit(tok, R, W, soft)
        self.ops[eng].append((waits, fn, (eng, 1) if inc else None))
        return tok

    def dma(self, q, out, in_, R=(), W=(), is_out=False, soft=False, **kw):
        waits = self._deps(q, R, W, soft)
        if q == "pool":
            n = len(self.swsem)
            self.swsem.append(self._es.enter_context(self.nc.semaphore("swdma%d" % n)))
            tok = (("sw", n), 16)
            self._commit(tok, R, W, soft)
            self.ops[q].append((waits, lambda e, out=out, in_=in_, kw=kw: e.dma_start(out=out, in_=in_, **kw), (tok[0], 16)))
            if is_out:
                self.out_tokens.append(tok)
            return tok
        i = self.ring_i
        self.ring_i = (self.ring_i + 1) % self.NRING
        if self.ring_cnt[i] > 0:
            self._need(q, (i, self.ring_cnt[i]), waits, True)
        self.ring_cnt[i] += 16
        tok = (i, self.ring_cnt[i])
        self._commit(tok, R, W, soft)
        self.ops[q].append((waits, lambda e, out=out, in_=in_, kw=kw: e.dma_start(out=out, in_=in_, **kw), (i, 16)))
        if is_out:
            self.out_tokens.append(tok)
        return tok

    def finish(self):
        waits = []
        for tok in self.out_tokens:
            self._need("sp", tok, waits, True)
        self.ops["sp"].append((waits, None, None))

    def emit(self, block):
        def run(eng_key):
            def body(e):
                for waits, fn, inc in self.ops[eng_key]:
                    for key, val in waits:
                        e.wait_ge(self._semobj(key), val)
                    if fn is None:
                        continue
                    ins = fn(e)
                    if inc is not None:
                        ins.then_inc(self._semobj(inc[0]), inc[1])
            return body
        block.tensor(run("pe"))
        block.vector(run("dve"))
        block.scalar(run("act"))
        block.gpsimd(run("pool"))
        block.sync(run("sp"))


def build_program(stop_after=None):
    nc = bass.Bass("TRN2", target_bir_lowering=False)
    dt_in = lambda name, shape, dt=F32: nc.dram_tensor(name, list(shape), dt, kind="ExternalInput").ap()
    xT_d = dt_in("xT", [128, 8, S])
    c_d = dt_in("c", [128, 8])
    pos_d = dt_in("pos", [128, S], I32)
    wada_d = dt_in("w_ada", [12, 128, 4096])
    bada_d = dt_in("b_ada", [128, 48])
    gcols_d = dt_in("gcols", [128, 24])
    win_d = dt_in("w_in", [7, 128, 4096])
    wgate_d = dt_in("w_gate", [128, 64])
    cqk_d = dt_in("cqk", [128, 8, 5])
    bif_d = dt_in("bif", [128, 16, 8])
    gmB_d = dt_in("gmB", [128, 512])
    gdB_d = dt_in("gdB", [128, 512])
    lam_d = dt_in("lam", [128, 4, 64])
    wout_d = dt_in("w_out", [2, 128, 4096])
    wup_d = dt_in("w_up", [11, 128, 4096])
    cffn_d = dt_in("cffn", [128, 44, 4])
    wdown_d = dt_in("w_down", [8, 128, 2816])
    cmask_d = dt_in("cmask", [128, 4, 128])
    consts_d = dt_in("consts", [128, 5, 128])
    out_d = nc.dram_tensor("outT", [128, 8, S], F32, kind="ExternalOutput").ap()
    dbg_d = nc.dram_tensor("dbg", [128, 8, S], F32, kind="ExternalOutput").ap() if DEBUG else None

    with ExitStack() as es:
        B = Builder(nc, es)
        sb = lambda name, shape, dt=F32: es.enter_context(nc.sbuf_tensor("s_" + name, list(shape), dt))

        ps = es.enter_context(nc.psum_tensor("ps", [128, 8, 512], F32))
        PSB = [T("ps%d" % i, excl=True) for i in range(8)]
        ring = sb("ring", [128, NSLOT, 4096], BF16)
        RING_T = [T("slot%d" % i) for i in range(NSLOT)]
        cst_f = sb("cst_f", [128, 5, 128], F32)
        cst_b = sb("cst_b", [128, 4, 128], BF16)
        smalls = sb("smalls", [128, 256], F32)
        RA = sb("RA", [128, 8 * S], BF16)
        RBm = sb("RBm", [128, 16 * 1024], BF16)
        RC = sb("RC", [128, 16512], F32)
        gts = sb("gts", [128, 16, 8], F32)
        gmath = sb("gmath", [128, 8, 64], F32)
        gmB = sb("gmB", [128, 512], F32)
        gdB = sb("gdB", [128, 512], F32)
        lamt = sb("lamt", [128, 4, 64], F32)
        cqk = sb("cqk", [128, 8, 5], F32)
        cffn = sb("cffn", [128, 44, 4], F32)
        bif = sb("bif", [128, 16, 8], F32)
        stg = sb("stg", [128, 4, 1040], BF16)
        stg32 = sb("stg32", [128, 2, 512], F32)
        stg32b = sb("stg32b", [128, 2, 512], F32)
        diag = sb("diag", [128, 16, 128], BF16)
        et = sb("et", [128, 3, 512], BF16)
        cst32 = sb("cst32", [128, 4, 130], F32)
        cdb = sb("cdb", [128, 4, 130], BF16)
        kwt = sb("kwt", [128, 2, 512], BF16)
        fin = sb("fin", [128, 4, 128], F32)
        finc = sb("finc", [128, 64], F32)
        halo = sb("halo", [128, 44, 2], BF16)
        cact_b = sb("cact_b", [128, 8], BF16)
        haloqk = sb("haloqk", [128, 8, 4], BF16)
        cmask = sb("cmask", [128, 4, 128], BF16)

        hT = RA[:].rearrange("p (c t) -> p c t", c=8)
        mixT = hT
        xin = RBm[:, 0:8192].bitcast(F32).rearrange("p (c t) -> p c t", c=8)
        mix = RBm[:].rearrange("p (n f) -> p n f", n=16)
        xinB = RC[:, 8192:12288].rearrange("p (c t) -> p c t", c=8)
        xinC = RBm[:, 8192:16384].bitcast(F32).rearrange("p (c t) -> p c t", c=8)
        xinD = RC[:, 4096:8192].rearrange("p (c t) -> p c t", c=8)
        og = mix[:, :, 512:1024]
        qkT = RC[:, 0:8192].bitcast(BF16).rearrange("p (c t) -> p c t", c=8)
        vaug = RC[:, 8192:8192 + 4128].bitcast(BF16).rearrange("p (n h e) -> p n h e", n=16, h=4)
        cs = RC[:, 12320:12320 + 4096].rearrange("p (k t) -> p k t", k=2)
        x1T = RC[:, 0:16384].rearrange("p (c t) -> p c t", c=8)
        h2T = RA[:, 0:8192].rearrange("p (c t) -> p c t", c=8)
        actA = RBm[:].rearrange("p (c t) -> p c t", c=16)
        actB = RA[:, 8192:8192 + 6144].rearrange("p (c t) -> p c t", c=6)

        def actT(i):
            return actA[:, i, :] if i < 16 else actB[:, i - 16, :]

        T_hT = [T("hT%d" % i) for i in range(NB)]
        T_xin = T("xin")
        T_xinB = T("xinB")
        T_xinC = T("xinC")
        T_xinD = T("xinD")
        T_mod2 = T("mod2")
        T_rs = T("ropescr")
        T_ss = T("ss")
        T_mix = [T("mix%d" % i) for i in range(NT)]
        T_og = [T("og%d" % i) for i in range(NT)]
        T_qk = [[T("qk%d_%d" % (c, b)) for b in range(NB)] for c in range(8)]
        T_vaug = [T("vaug%d" % i) for i in range(NT)]
        T_cs = T("cs")
        T_x1 = [[T("x1_%d_%d" % (c, b)) for b in range(NB)] for c in range(8)]
        T_cst = T("cst")
        T_small = T("small")
        T_bif = T("bif")
        T_gts = T("gts")
        T_gm = T("gmath")
        T_stg = [T("stg%d" % i) for i in range(4)]
        T_s32 = [T("s32_0"), T("s32_1")]
        T_s32b = [T("s32b_0"), T("s32b_1")]
        T_diag = [T("diag%d" % i) for i in range(4)]
        T_et = [T("et0"), T("et1"), T("et2")]
        T_C32 = [T("C32_%d" % h) for h in range(4)]
        T_cdb = [T("cdb_%d" % h) for h in range(4)]
        T_kwt = [T("kwt0"), T("kwt1")]
        T_fin = T("fin")
        T_finc = T("finc")
        T_halo = T("halo")
        T_hqk = [T("hqk%d" % i) for i in range(8)]
        T_mixT = [T("mixT%d" % i) for i in range(NB)]
        T_h2 = [[T("h2_%d_%d" % (s_, c_)) for c_ in range(8)] for s_ in range(2)]
        T_act = [[T("act%d_%d" % (i, s)) for s in range(2)] for i in range(22)]
        ALL_RC = [t for row in T_qk for t in row] + T_vaug + [T_cs]
        ALL_MIX = T_mix + T_og + [T_xin, T_xinC]

        ident_b = cst_b[:, 0, :]
        triu_b = cst_b[:, 1, :]
        ones_b = cst_b[:, 2, :]
        perm_b = cst_b[:, 3, :]
        ident_f = cst_f[:, 0, :]
        triu_f = cst_f[:, 1, :]
        ones_f = cst_f[:, 2, :]
        invf = cst_f[:, 4, 0:1]
        sgn = cst_f[:, 4, 1:2]

        c_raw = smalls[:, 0:8]
        mod = smalls[:, 16:64]
        A1 = smalls[:, 64:72]
        A2 = smalls[:, 72:80]
        gcols = smalls[:, 80:104]
        bada = smalls[:, 104:152]
        epsc = smalls[:, 152:153]
        lamc = smalls[:, 153:154]
        nlamc = smalls[:, 154:155]
        onec = smalls[:, 155:156]
        ltmp = smalls[:, 156:160]
        sh_a, sc_a, gt_a = mod[:, 0:8], mod[:, 8:16], mod[:, 16:24]
        sh_f, sc_f, gt_f = mod[:, 24:32], mod[:, 32:40], mod[:, 40:48]
        g_fin = gcols[:, 16:24]

        rot = {"i": 0}

        def nextbank(lo=0, hi=8):
            b = lo + rot["i"] % (hi - lo)
            rot["i"] += 1
            return b

        WSEQ = ([(wada_d[i], 4096) for i in range(4)] +
                [(win_d[0], 4096), (win_d[1], 4096), (wgate_d, 64), (win_d[2], 4096), (win_d[3], 4096),
                 (win_d[4], 4096), (win_d[5], 4096), (win_d[6], 4096)] +
                [(wada_d[i], 4096) for i in range(4, 12)] +
                [(wout_d[0], 4096), (wout_d[1], 4096)])
        for _p in range(2):
            WSEQ += [(wup_d[g], 4096) for g in range(11)] + [(wdown_d[f], 2816) for f in range(8)] * (1 + _p)
        ws = {"issued": 0, "next": 0}

        def _issue_to(n):
            while ws["issued"] < min(n, len(WSEQ)):
                k = ws["issued"]
                src, nel = WSEQ[k]
                s = k % NSLOT
                a = 4 if nel == 4096 else (2 if nel == 2816 else 1)
                B.dma("pool", ring[:, s, 0:nel].rearrange("p (a n) -> p a n", a=a), src.rearrange("p (a n) -> p a n", a=a), W=[RING_T[s]],
                      R=([T_small, T_cst, T_rs, T_bif] if k == 0 else []))
                ws["issued"] += 1

        def wget(la=LOOKAHEAD):
            k = ws["next"]
            ws["next"] += 1
            _issue_to(k + 1 + la)
            return k % NSLOT

        def slot_view(s, kc, n):
            return ring[:, s, 0:kc * n].rearrange("p (k n) -> p k n", k=kc)

        class _Stop(Exception):
            pass

        def dbg_dump(ap, Ts):
            dd = dbg_d
            if tuple(ap.shape) == (128, 16, 1024):
                dd = dbg_d.rearrange("p c (a t) -> p (c a) t", a=2)
            B.dma("pool", dd, ap, R=Ts, is_out=True)
            raise _Stop()

        def body():
            B.dma("sp", RC[:, 0:2048].bitcast(I32), pos_d, W=[T_rs])
            B.dma("sp", cst_f[:], consts_d, W=[T_cst], soft=True)
            B.dma("pool", cmask[:], cmask_d, W=[T_cst], soft=True)
            B.dma("sp", c_raw, c_d, W=[T_small], soft=True)
            B.dma("sp", bada, bada_d, W=[T_small], soft=True)
            B.dma("sp", gcols, gcols_d, W=[T_small], soft=True)
            B.dma("sp", cqk[:], cqk_d, W=[T_small], soft=True)
            B.dma("sp", cffn[:], cffn_d, W=[T_small], soft=True)
            B.dma("sp", gmB[:], gmB_d, W=[T_small], soft=True)
            B.dma("sp", gdB[:], gdB_d, W=[T_small], soft=True)
            B.dma("sp", lamt[:], lam_d, W=[T_small], soft=True)
            B.dma("sp", bif[:], bif_d, W=[T_bif])
            B.op("dve", lambda e: e.tensor_copy(out=cst_b[:], in_=cst_f[:, 0:4, :]), R=[T_cst], W=[T_cst])
            B.op("dve", lambda e: e.memset(epsc, EPS), W=[T_small], soft=True)
            B.op("dve", lambda e: e.memset(onec, 1.0), W=[T_small], soft=True)
            B.op("dve", lambda e: e.memset(halo[:], 0.0), W=[T_halo])
            B.op("act", lambda e: e.activation(out=cact_b[:], in_=c_raw, func=AF.Silu), R=[T_small], W=[T_small])
            _issue_to(LOOKAHEAD)

            def rope_tables():
                sA = RC[:, 0:2048]
                sB = RC[:, 2048:4096]
                sC = RC[:, 4096:6144]
                sAi = sA.bitcast(I32)
                op = lambda fn, **kw: B.op("dve", fn, R=[T_rs, T_cst], W=[T_rs])
                op(lambda e: e.tensor_copy(out=sB, in_=sAi))
                op(lambda e: e.tensor_scalar(out=sB, in0=sB, scalar1=invf, scalar2=None, op0=ALU.mult))
                op(lambda e: e.tensor_scalar(out=sC, in0=sB, scalar1=1.0 / TWO_PI, scalar2=None, op0=ALU.mult))
                op(lambda e: e.tensor_copy(out=sAi, in_=sC))
                op(lambda e: e.tensor_copy(out=sC, in_=sAi))
                op(lambda e: e.scalar_tensor_tensor(out=sB, in0=sC, scalar=-TWO_PI, in1=sB, op0=ALU.mult, op1=ALU.add))
                op(lambda e: e.tensor_scalar(out=sC, in0=sB, scalar1=PI, scalar2=-TWO_PI, op0=ALU.is_gt, op1=ALU.mult))
                op(lambda e: e.tensor_tensor(out=sB, in0=sB, in1=sC, op=ALU.add))
                op(lambda e: e.tensor_scalar(out=sB, in0=sB, scalar1=-PI, scalar2=TWO_PI, op0=ALU.is_lt, op1=ALU.mult) if False else
                   e.tensor_scalar(out=sB, in0=sB, scalar1=PI, scalar2=-PI, op0=ALU.min, op1=ALU.max))
                B.op("act", lambda e: e.activation(out=cs[:, 1, :], in_=sB, func=AF.Sin, scale=sgn), R=[T_rs, T_cst], W=[T_cs])
                op(lambda e: e.tensor_scalar(out=sC, in0=sB, scalar1=PI / 2, scalar2=None, op0=ALU.add))
                sAf = sA
                op(lambda e: e.tensor_scalar(out=sAf, in0=sC, scalar1=PI, scalar2=-TWO_PI, op0=ALU.is_gt, op1=ALU.mult))
                op(lambda e: e.tensor_tensor(out=sC, in0=sC, in1=sAf, op=ALU.add))
                op(lambda e: e.tensor_scalar(out=sC, in0=sC, scalar1=PI, scalar2=-PI, op0=ALU.min, op1=ALU.max))
                B.op("act", lambda e: e.activation(out=cs[:, 0, :], in_=sC, func=AF.Sin), R=[T_rs], W=[T_cs])

            rope_tables()

            def adaln_cols(pieces, pb, col0=None):
                for piece in pieces:
                    s = wget()
                    wv = slot_view(s, 8, 512)
                    for jj in range(4):
                        j = (piece * 4 + jj) if col0 is None else (col0 + jj)
                        for kc in range(8):
                            B.op("pe", lambda e, wv=wv, jj=jj, kc=kc, j=j: e.matmul(
                                ps[:, pb, j:j + 1], lhsT=wv[:, kc, jj * 128:(jj + 1) * 128], rhs=cact_b[:, kc:kc + 1],
                                start=(kc == 0), stop=(kc == 7)),
                                R=[RING_T[s], T_small], W=[PSB[pb]], inc=(kc == 7))

            def adaln_piece_late(piece, pb):
                adaln_cols([piece], pb, col0=0)
                j0 = piece * 4
                B.op("act", lambda e: e.activation(out=mod[:, j0:j0 + 4], in_=ps[:, pb, 0:4], func=AF.Copy), R=[PSB[pb]], W=[T_mod2])
                B.op("dve", lambda e: e.tensor_tensor(out=mod[:, j0:j0 + 4], in0=mod[:, j0:j0 + 4], in1=bada[:, j0:j0 + 4], op=ALU.add),
                     R=[T_mod2, T_small], W=[T_mod2])

            pb_mod = nextbank()
            adaln_cols(range(4), pb_mod)
            B.op("dve", lambda e: e.tensor_tensor(out=mod[:, 0:16], in0=ps[:, pb_mod, 0:16], in1=bada[:, 0:16], op=ALU.add),
                 R=[PSB[pb_mod], T_small], W=[T_small])
            B.op("dve", lambda e: e.scalar_tensor_tensor(out=A1, in0=sc_a, scalar=1.0, in1=gcols[:, 0:8], op0=ALU.add, op1=ALU.mult),
                 R=[T_small], W=[T_small])
            B.op("dve", lambda e: e.tensor_tensor(out=stg32[:, 0, 0:64], in0=lamt[:, 0, :], in1=lamt[:, 1, :], op=ALU.mult), R=[T_small], W=[T_s32[0]])
            B.op("dve", lambda e: e.tensor_tensor(out=stg32[:, 0, 64:128], in0=lamt[:, 2, :], in1=lamt[:, 3, :], op=ALU.mult), R=[T_small], W=[T_s32[0]])
            B.op("dve", lambda e: e.tensor_reduce(out=ltmp[:, 0:2], in_=stg32[:, 0, 0:128].rearrange("p (a b) -> p a b", a=2), axis=AX.X, op=ALU.add),
                 R=[T_s32[0]], W=[T_small])
            B.op("act", lambda e: e.activation(out=ltmp[:, 2:4], in_=ltmp[:, 0:2], func=AF.Exp), R=[T_small], W=[T_small])
            B.op("dve", lambda e: e.scalar_tensor_tensor(out=lamc, in0=ltmp[:, 2:3], scalar=LAM_INIT, in1=ltmp[:, 3:4], op0=ALU.add, op1=ALU.subtract),
                 R=[T_small], W=[T_small])
            B.op("dve", lambda e: e.tensor_scalar(out=nlamc, in0=lamc, scalar1=-1.0, scalar2=None, op0=ALU.mult), R=[T_small], W=[T_small])
            B.op("dve", lambda e: e.tensor_scalar(out=gdB[:], in0=gdB[:], scalar1=(1.0 - LAM_INIT), scalar2=None, op0=ALU.mult), R=[T_small], W=[T_small])

            def norm_block(src, T_src, dst, T_dst, Acol, shcol, ncols, si, T_small=T_small, defer=False, presq=None):
                pb = nextbank()
                for fc in range(8):
                    if presq is not None:
                        qa, qT = presq[fc]
                        B.op("pe", lambda e, fc=fc, qa=qa: e.matmul(ps[:, pb, 0:ncols], lhsT=ones_b, rhs=qa, start=(fc == 0), stop=(fc == 7)),
                             R=[qT, T_cst], W=[PSB[pb]], inc=True)
                        continue
                    sj = 2 * si + fc % 2
                    if fc % 2 == 0:
                        B.op("act", lambda e, fc=fc, sj=sj: e.activation(out=stg[:, sj, 0:ncols], in_=src[:, fc, :], func=AF.Square),
                             R=T_src, W=[T_stg[sj]])
                    else:
                        B.op("dve", lambda e, fc=fc, sj=sj: e.tensor_tensor(out=stg[:, sj, 0:ncols], in0=src[:, fc, :], in1=src[:, fc, :], op=ALU.mult),
                             R=T_src, W=[T_stg[sj]])
                    B.op("pe", lambda e, fc=fc, sj=sj: e.matmul(ps[:, pb, 0:ncols], lhsT=ones_b, rhs=stg[:, sj, 0:ncols],
                                                              start=(fc == 0), stop=(fc == 7)),
                         R=[T_stg[sj], T_cst], W=[PSB[pb]], inc=True)
                rs = stg32[:, si, 0:ncols]
                B.op("act", lambda e: e.activation(out=rs, in_=ps[:, pb, 0:ncols], func=AF.Ln, bias=epsc, scale=1.0 / D),
                     R=[PSB[pb], T_small], W=[T_s32[si]])
                B.op("act", lambda e: e.activation(out=rs, in_=rs, func=AF.Exp, scale=-0.5), R=[T_s32[si]], W=[T_s32[si]])
                if defer:
                    return lambda: norm_mod(src, T_src, dst, T_dst, Acol, shcol, ncols, si, T_small)
                norm_mod(src, T_src, dst, T_dst, Acol, shcol, ncols, si, T_small)

            def norm_mod(src, T_src, dst, T_dst, Acol, shcol, ncols, si, T_small):
                T_fc = T_dst.pop("per_fc") if isinstance(T_dst, dict) else None
                T_dst = T_dst["common"] if isinstance(T_dst, dict) else T_dst
                rs = stg32[:, si, 0:ncols]
                for fc in range(8):
                    if shcol is not None:
                        tmp = stg32b[:, fc % 2, 0:ncols]
                        B.op("dve", lambda e, fc=fc, tmp=tmp: e.scalar_tensor_tensor(
                            out=tmp, in0=src[:, fc, :], scalar=Acol[:, fc:fc + 1], in1=rs, op0=ALU.mult, op1=ALU.mult),
                            R=list(T_src) + [T_s32[si], T_small], W=[T_s32b[fc % 2]])
                        B.op("act", lambda e, fc=fc, tmp=tmp: e.activation(out=dst[:, fc, :], in_=tmp, func=AF.Identity,
                                                                         bias=shcol[:, fc:fc + 1], scale=1.0),
                             R=[T_s32b[fc % 2], T_small], W=(list(T_dst) + [T_fc[fc]]) if T_fc is not None else T_dst)
                    else:
                        B.op("dve", lambda e, fc=fc: e.scalar_tensor_tensor(
                            out=dst[:, fc, :], in0=src[:, fc, :], scalar=Acol[:, fc:fc + 1], in1=rs, op0=ALU.mult, op1=ALU.mult),
                            R=list(T_src) + [T_s32[si], T_small], W=T_dst)

            dg = {"i": 0}

            def build_diag(wcols, ntap, Tw):
                d = dg["i"] % 4
                dg["i"] += 1
                for j in range(ntap):
                    B.op("dve", lambda e, j=j, d=d: e.tensor_scalar(out=diag[:, d * 4 + j, :], in0=ident_f,
                                                                 scalar1=wcols[:, j:j + 1], scalar2=None, op0=ALU.mult),
                         R=[T_cst, Tw], W=[T_diag[d]])
                return d

            p2a = {"prev": None, "n": 0, "slots": None}

            def p2a_proj(cc, tb):
                if p2a["slots"] is None:
                    sq_ = wget()
                    sk_ = wget(la=2)
                    p2a["slots"] = (sq_, sk_)
                s = p2a["slots"][cc // 4]
                wv = slot_view(s, 8, 512)
                cj = cc % 4
                d = build_diag(cqk[:, cc, 0:4], 4, T_small)
                pb = nextbank()
                for kc in range(8):
                    B.op("pe", lambda e, kc=kc, pb=pb, tb=tb, wv=wv, cj=cj: e.matmul(
                        ps[:, pb, :], lhsT=wv[:, kc, cj * 128:(cj + 1) * 128], rhs=hT[:, kc, tb * 512:(tb + 1) * 512],
                        start=(kc == 0), stop=(kc == 7)), R=[T_hT[tb], RING_T[s]], W=[PSB[pb]], inc=(kc == 7))
                si = p2a["n"] % 2
                p2a["n"] += 1
                if tb == 0:
                    B.op("dve", lambda e, si=si: e.memset(stg[:, si, 0:3], 0.0), W=[T_stg[si]])
                else:
                    B.op("dve", lambda e, si=si, cc=cc: e.tensor_copy(out=stg[:, si, 0:3], in_=haloqk[:, cc, 0:3]), R=[T_hqk[cc]], W=[T_stg[si]])
                B.op("act", lambda e, si=si, pb=pb: e.activation(out=stg[:, si, 3:515], in_=ps[:, pb, :], func=AF.Copy),
                     R=[PSB[pb]], W=[T_stg[si]])
                B.op("dve", lambda e, si=si, cc=cc: e.tensor_copy(out=haloqk[:, cc, 0:3], in_=stg[:, si, 512:515]), R=[T_stg[si]], W=[T_hqk[cc]])
                return (cc, tb, si, d)

            def p2a_conv(cc, tb, si, d):
                pb2 = nextbank()
                for j in range(4):
                    B.op("pe", lambda e, j=j, si=si, pb2=pb2, d=d: e.matmul(
                        ps[:, pb2, :], lhsT=diag[:, d * 4 + j, :], rhs=stg[:, si, j:j + 512], start=(j == 0), stop=(j == 3)),
                        R=[T_stg[si], T_diag[d]], W=[PSB[pb2]], inc=(j == 3))
                B.op("act", lambda e, pb2=pb2, cc=cc, tb=tb: e.activation(
                    out=qkT[:, cc, tb * 512:(tb + 1) * 512], in_=ps[:, pb2, :], func=AF.Silu, bias=cqk[:, cc, 4:5], scale=1.0),
                    R=[PSB[pb2], T_small], W=[T_qk[cc][tb]] + ([T_rs] if cc < 6 else []))

            def p2a_push(cc, tb):
                h_ = p2a_proj(cc, tb)
                if p2a["prev"] is not None:
                    p2a_conv(*p2a["prev"])
                p2a["prev"] = h_

            def p2a_flush():
                if p2a["prev"] is not None:
                    p2a_conv(*p2a["prev"])
                    p2a["prev"] = None

            prev_mod = None
            for tb in range(NB):
                xb, Tx = [(xin, T_xin), (xinB, T_xinB), (xinC, T_xinC), (xin, T_xin)][tb]
                B.dma("sp", xb, xT_d[:, :, tb * 512:(tb + 1) * 512], W=[Tx], R=([RING_T[3]] if tb == 2 else []))
                m = norm_block(xb, [Tx], hT[:, :, tb * 512:(tb + 1) * 512], [T_hT[tb]], A1, sh_a, 512, tb % 2, defer=True)
                if tb >= 2:
                    for cc in range(4 * (tb - 2), 4 * (tb - 2) + 4):
                        p2a_push(cc, 0)
                if prev_mod is not None:
                    prev_mod()
                prev_mod = m
            for cc in range(4):
                p2a_push(cc, 1)
            prev_mod()
            for cc in range(4, 8):
                p2a_push(cc, 1)
            for tb in (2, 3):
                for cc in range(8):
                    p2a_push(cc, tb)
            p2a_flush()
            if DEBUG == "hT":
                dbg_dump(hT, T_hT)

            if DEBUG == "qkm":
                dbg_dump(qkT, [t for row in T_qk for t in row])

            def tokmajor_v(s_v, hook):
                wv_v = slot_view(s_v, 8, 512)
                for tt in range(NT):
                    if tt == 8 and hook is not None:
                        hook()
                    tb = tt // 4
                    pv = nextbank()
                    for kc in range(8):
                        B.op("pe", lambda e, kc=kc, pv=pv, tt=tt: e.matmul(ps[:, pv, :], lhsT=hT[:, kc, tt * 128:(tt + 1) * 128], rhs=wv_v[:, kc, :],
                                                                         start=(kc == 0), stop=(kc == 7)),
                             R=[T_hT[tb], RING_T[s_v]], W=[PSB[pv]], inc=(kc == 7))
                    B.op("dve", lambda e, pv=pv, tt=tt: e.tensor_copy(out=vaug[:, tt, :, 0:128], in_=ps[:, pv, :].rearrange("p (h e) -> p h e", h=4)),
                         R=[PSB[pv]], W=[T_vaug[tt]])

            B.op("dve", lambda e: e.memset(vaug[:, :, :, 128:129], 1.0), W=T_vaug + [T_xinB])
            s_g = wget()
            wv_g = slot_view(s_g, 8, 8)
            for tt in range(NT):
                tb = tt // 4
                pg = nextbank()
                for kc in range(8):
                    lhs = hT[:, kc, tt * 128:(tt + 1) * 128]
                    B.op("pe", lambda e, lhs=lhs, kc=kc, pg=pg: e.matmul(ps[:, pg, 0:8], lhsT=lhs, rhs=wv_g[:, kc, :], start=(kc == 0), stop=(kc == 7)),
                         R=[T_hT[tb], RING_T[s_g]], W=[PSB[pg]], inc=(kc == 7))
                B.op("dve", lambda e, pg=pg, tt=tt: e.tensor_tensor(out=gts[:, tt, :], in0=ps[:, pg, 0:8], in1=bif[:, tt, :], op=ALU.add),
                     R=[PSB[pg], T_bif], W=[T_gts])

            g3 = lambda i: gmath[:, i, :].rearrange("p (n h) -> p n h", n=16)
            B.op("act", lambda e: e.activation(out=g3(0), in_=gts[:, :, 4:8], func=AF.Exp, scale=-1.0), R=[T_gts], W=[T_gm])
            B.op("act", lambda e: e.activation(out=g3(1), in_=g3(0), func=AF.Ln, bias=onec, scale=1.0), R=[T_gm, T_small], W=[T_gm])
            B.op("act", lambda e: e.activation(out=g3(4), in_=gts[:, :, 0:4], func=AF.Exp), R=[T_gts], W=[T_gm])

            def gate_math_2():
                pbw, pbt = nextbank(), nextbank()
                B.op("pe", lambda e: e.matmul(ps[:, pbw, 0:64], lhsT=triu_f, rhs=gmath[:, 1, :], start=True, stop=True), R=[T_gm, T_cst], W=[PSB[pbw]])
                B.op("pe", lambda e: e.matmul(ps[:, pbt, 0:64], lhsT=ones_f, rhs=gmath[:, 1, :], start=True, stop=True), R=[T_gm, T_cst], W=[PSB[pbt]])
                B.op("dve", lambda e: e.tensor_copy(out=gmath[:, 6, :], in_=ps[:, pbt, 0:64]), R=[PSB[pbt]], W=[T_gm])
                B.op("dve", lambda e: e.tensor_tensor(out=gmath[:, 7, :], in0=ps[:, pbw, 0:64], in1=gmath[:, 6, :], op=ALU.subtract), R=[PSB[pbw], T_gm], W=[T_gm])
                B.op("act", lambda e: e.activation(out=gmath[:, 2, :], in_=gmath[:, 7, :], func=AF.Exp), R=[T_gm], W=[T_gm])
                B.op("act", lambda e: e.activation(out=gmath[:, 3, :], in_=gmath[:, 6, :], func=AF.Exp, scale=-1.0), R=[T_gm], W=[T_gm])
                B.op("dve", lambda e: e.scalar_tensor_tensor(out=gmath[:, 5, :], in0=gmath[:, 4, :], scalar=128.0 ** -0.5, in1=gmath[:, 2, :], op0=ALU.mult, op1=ALU.mult),
                     R=[T_gm], W=[T_gm])

            s_v = wget()
            tokmajor_v(s_v, gate_math_2)
            s_o = wget()
            wv_o = slot_view(s_o, 8, 512)
            for tt in range(NT):
                tb = tt // 4
                po = nextbank()
                for kc in range(8):
                    lhs = hT[:, kc, tt * 128:(tt + 1) * 128]
                    B.op("pe", lambda e, lhs=lhs, kc=kc, po=po: e.matmul(ps[:, po, :], lhsT=lhs, rhs=wv_o[:, kc, :], start=(kc == 0), stop=(kc == 7)),
                         R=[T_hT[tb], RING_T[s_o]], W=[PSB[po]], inc=(kc == 7))
                B.op("act", lambda e, po=po, tt=tt: e.activation(out=og[:, tt, :], in_=ps[:, po, :], func=AF.Sigmoid),
                     R=[PSB[po]], W=[T_og[tt], T_xin, T_xinC])

            def group_norm(src, Tsrc, gBs, dsts, Tdst, gate=None, Tgate=()):
                sq = stg32[:, 0, :].rearrange("p (h e) -> p h e", h=4)
                B.op("act", lambda e: e.activation(out=sq, in_=src, func=AF.Square), R=[Tsrc], W=[T_s32[0]])
                B.op("dve", lambda e: e.tensor_reduce(out=finc[:, 16:20], in_=sq, axis=AX.X, op=ALU.add), R=[T_s32[0]], W=[T_finc])
                B.op("act", lambda e: e.activation(out=finc[:, 20:24], in_=finc[:, 16:20], func=AF.Ln, bias=epsc, scale=1.0 / 128.0),
                     R=[T_finc, T_small], W=[T_finc])
                B.op("act", lambda e: e.activation(out=finc[:, 24:28], in_=finc[:, 20:24], func=AF.Exp, scale=-0.5), R=[T_finc], W=[T_finc])
                for i in range(4):
                    if gate is None:
                        B.op("dve", lambda e, i=i: e.scalar_tensor_tensor(
                            out=dsts[i], in0=src[:, i, :], scalar=finc[:, 24 + i:25 + i], in1=gBs[i], op0=ALU.mult, op1=ALU.mult),
                            R=[Tsrc, T_finc, T_small], W=Tdst)
                    else:
                        B.op("dve", lambda e, i=i: e.scalar_tensor_tensor(
                            out=stg32[:, 1, i * 128:(i + 1) * 128], in0=src[:, i, :], scalar=finc[:, 24 + i:25 + i], in1=gBs[i],
                            op0=ALU.mult, op1=ALU.mult), R=[Tsrc, T_finc, T_small], W=[T_s32[1]])
                if gate is not None:
                    B.op("dve", lambda e: e.tensor_tensor(out=dsts, in0=stg32[:, 1, :], in1=gate, op=ALU.mult),
                         R=[T_s32[1]] + list(Tgate), W=Tdst)

            for h in range(4):
                B.op("dve", lambda e, h=h: e.memset(cst32[:, h, :], 0.0), W=[T_C32[h]])
            s_qd = wget()
            s_kd = wget(la=2)
            s_vd = wget(la=1)
            wv_qd = [slot_view(s_qd, 8, 512), slot_view(s_kd, 8, 512)]
            wv_vd = slot_view(s_vd, 8, 512)
            ucnt = {"i": 0}

            def p2b_proj(cc, tb):
                g, cj = cc // 4, cc % 4
                s, wv = (s_qd, s_kd)[g], wv_qd[g]
                sl = slice(tb * 512, (tb + 1) * 512)
                pb = nextbank(0, 6)
                for kc in range(8):
                    B.op("pe", lambda e, kc=kc, pb=pb, sl=sl, wv=wv, cj=cj: e.matmul(
                        ps[:, pb, :], lhsT=wv[:, kc, cj * 128:(cj + 1) * 128], rhs=hT[:, kc, sl],
                        start=(kc == 0), stop=(kc == 7)), R=[T_hT[tb], RING_T[s]], W=[PSB[pb]], inc=(kc == 7))
                k = (ucnt["i"] % 2) * 2
                ucnt["i"] += 1
                B.op("dve", lambda e, k=k, pb=pb, sl=sl: e.tensor_tensor(out=stg[:, k, 0:512], in0=ps[:, pb, :], in1=cs[:, 1, sl], op=ALU.mult),
                     R=[PSB[pb], T_cs], W=[T_stg[k]])
                B.op("dve", lambda e, k=k, pb=pb, sl=sl: e.tensor_tensor(out=stg[:, k + 1, 0:512], in0=ps[:, pb, :], in1=cs[:, 0, sl], op=ALU.mult),
                     R=[PSB[pb], T_cs], W=[T_stg[k + 1]])
                return (cc, tb, k)

            def p2b_fin(cc, tb, k):
                sl = slice(tb * 512, (tb + 1) * 512)
                pb2 = nextbank(0, 6)
                B.op("pe", lambda e, k=k, pb2=pb2: e.matmul(ps[:, pb2, :], lhsT=perm_b, rhs=stg[:, k, 0:512], start=True, stop=False),
                     R=[T_stg[k], T_cst], W=[PSB[pb2]], inc=False)
                B.op("pe", lambda e, k=k, pb2=pb2: e.matmul(ps[:, pb2, :], lhsT=ident_b, rhs=stg[:, k + 1, 0:512], start=False, stop=True),
                     R=[T_stg[k + 1], T_cst], W=[PSB[pb2]], inc=True)
                B.op("act", lambda e, pb2=pb2, cc=cc, sl=sl: e.activation(out=qkT[:, cc, sl], in_=ps[:, pb2, :], func=AF.Copy),
                     R=[PSB[pb2]], W=[T_qk[cc][tb]])

            def vd_tile(tt):
                tb = tt // 4
                pv = nextbank(0, 6)
                for kc in range(8):
                    B.op("pe", lambda e, kc=kc, pv=pv, tt=tt: e.matmul(ps[:, pv, :], lhsT=hT[:, kc, tt * 128:(tt + 1) * 128], rhs=wv_vd[:, kc, :],
                                                                     start=(kc == 0), stop=(kc == 7)),
                         R=[T_hT[tb], RING_T[s_vd]], W=[PSB[pv]], inc=(kc == 7))
                B.op("act", lambda e, pv=pv, tt=tt: e.activation(out=vaug[:, tt, :, 0:128], in_=ps[:, pv, :].rearrange("p (h e) -> p h e", h=4), func=AF.Copy),
                     R=[PSB[pv]], W=[T_vaug[tt]])

            pending = []
            ETq = et[:].rearrange("p a (b c) -> p (a b) c", c=128)
            T_etq = [T("etq%d" % i) for i in range(8)]

            def mmain(tt):
                tb = tt // 4
                tsl = slice(tt * 128, (tt + 1) * 128)
                ki = s = tt % 2
                pbk = nextbank(0, 6)
                psk = ps[:, pbk, :].bitcast(BF16)
                for h in range(4):
                    B.op("pe", lambda e, h=h: e.transpose(psk[:, h * 128:(h + 1) * 128], qkT[:, 4 + h, tsl], ident_b),
                         R=[T_qk[4 + h][tb], T_cst], W=[PSB[pbk]], inc=(h == 3))
                for h in range(4):
                    B.op("act", lambda e, h=h: e.activation(
                        out=kwt[:, ki, h * 128:(h + 1) * 128], in_=psk[:, h * 128:(h + 1) * 128], func=AF.Identity,
                        scale=gmath[:, 5, tt * 4 + h:tt * 4 + h + 1]),
                        R=[PSB[pbk], T_gm], W=[T_kwt[ki]])
                for h in range(4):
                    dc = gmath[:, 3, tt * 4 + h:tt * 4 + h + 1]
                    B.op("act", lambda e, h=h, dc=dc: e.activation(out=cdb[:, h, 0:129], in_=cst32[:, h, 0:129], func=AF.Identity, scale=dc),
                         R=[T_C32[h], T_gm], W=[T_cdb[h]])
                pst = nextbank(0, 6)
                for h in range(4):
                    B.op("pe", lambda e, h=h: e.matmul(ps[:, pst, h * 128:(h + 1) * 128], lhsT=qkT[:, 4 + h, tsl], rhs=qkT[:, h, tsl], start=True, stop=True),
                         R=[T_qk[4 + h][tb], T_qk[h][tb]], W=[PSB[pst]], inc=(h == 3))
                for h in range(4):
                    wc = gmath[:, 5, tt * 4 + h:tt * 4 + h + 1]
                    B.op("dve", lambda e, h=h, wc=wc: e.scalar_tensor_tensor(
                        out=ETq[:, s * 4 + h, :], in0=ps[:, pst, h * 128:(h + 1) * 128], scalar=wc, in1=triu_f, op0=ALU.mult, op1=ALU.mult),
                        R=[PSB[pst], T_gm, T_cst], W=[T_etq[s * 4 + h]])
                return (tt, tb, tsl, ki, s)

            def mmain2(tt, tb, tsl, ki, s):
                pu = [nextbank(0, 6), nextbank(0, 6)]
                for h in range(4):
                    acc = ps[:, 6 + h // 2, (h % 2) * 130:(h % 2) * 130 + 129]
                    B.op("pe", lambda e, h=h, acc=acc: e.matmul(acc, lhsT=ETq[:, s * 4 + h, :], rhs=vaug[:, tt, h, :], start=True, stop=False),
                         R=[T_etq[s * 4 + h], T_vaug[tt]], W=[PSB[6 + h // 2]], inc=False)
                    B.op("pe", lambda e, h=h, acc=acc: e.matmul(acc, lhsT=qkT[:, h, tsl], rhs=cdb[:, h, 0:129], start=False, stop=True),
                         R=[T_qk[h][tb], T_cdb[h]], W=[PSB[6 + h // 2]], inc=True)
                    B.op("pe", lambda e, h=h: e.matmul(ps[:, pu[h // 2], (h % 2) * 130:(h % 2) * 130 + 129], lhsT=kwt[:, ki, h * 128:(h + 1) * 128],
                                                       rhs=vaug[:, tt, h, :], start=True, stop=True),
                         R=[T_kwt[ki], T_vaug[tt]], W=[PSB[pu[h // 2]]])
                for h in range(4):
                    dc = gmath[:, 3, tt * 4 + h:tt * 4 + h + 1]
                    B.op("dve", lambda e, h=h, dc=dc: e.scalar_tensor_tensor(
                        out=cst32[:, h, 0:129], in0=cst32[:, h, 0:129], scalar=dc, in1=ps[:, pu[h // 2], (h % 2) * 130:(h % 2) * 130 + 129],
                        op0=ALU.mult, op1=ALU.add), R=[T_C32[h], PSB[pu[h // 2]], T_gm], W=[T_C32[h]])
                for hp in range(2):
                    pv2 = ps[:, 6 + hp, 0:260].rearrange("p (a b) -> p a b", a=2)
                    B.op("act", lambda e, hp=hp, pv2=pv2: e.activation(
                        out=stg32b[:, s, hp * 256:(hp + 1) * 256].rearrange("p (a b) -> p a b", a=2), in_=pv2[:, :, 0:128], func=AF.Copy),
                        R=[PSB[6 + hp]], W=[T_s32b[s]])
                    B.op("act", lambda e, hp=hp, pv2=pv2: e.activation(
                        out=finc[:, 56 + 4 * s + 2 * hp:58 + 4 * s + 2 * hp], in_=pv2[:, :, 128], func=AF.Abs),
                        R=[PSB[6 + hp]], W=[T_finc])

            def mfin(tt):
                s = tt % 2
                nums = stg32b[:, s, :].rearrange("p (h e) -> p h e", h=4)
                for h in range(4):
                    B.op("act", lambda e, h=h: e.activation(out=stg32[:, 0, h * 128:(h + 1) * 128], in_=nums[:, h, :], func=AF.Square,
                                                            accum_out=finc[:, 16 + h:17 + h]), R=[T_s32b[s]], W=[T_s32[0], T_ss])
                B.op("dve", lambda e: e.tensor_tensor(out=finc[:, 4:8], in0=finc[:, 56 + 4 * s:60 + 4 * s], in1=gmath[:, 2, tt * 4:tt * 4 + 4], op=ALU.max),
                     R=[T_finc, T_gm], W=[T_finc])
                B.op("dve", lambda e: e.reciprocal(out=finc[:, 8:12], in_=finc[:, 4:8]), R=[T_finc], W=[T_finc])
                B.op("dve", lambda e: e.tensor_tensor(out=finc[:, 12:16], in0=finc[:, 8:12], in1=finc[:, 8:12], op=ALU.mult), R=[T_finc], W=[T_finc])
                B.op("dve", lambda e: e.tensor_tensor(out=finc[:, 20:24], in0=finc[:, 12:16], in1=finc[:, 16:20], op=ALU.mult), R=[T_finc, T_ss], W=[T_finc])
                B.op("act", lambda e: e.activation(out=finc[:, 24:28], in_=finc[:, 20:24], func=AF.Ln, bias=epsc, scale=1.0 / 128.0),
                     R=[T_finc, T_small], W=[T_finc])
                B.op("act", lambda e: e.activation(out=finc[:, 28:32], in_=finc[:, 24:28], func=AF.Exp, scale=-0.5), R=[T_finc], W=[T_finc])
                B.op("dve", lambda e: e.tensor_tensor(out=finc[:, 12:16], in0=finc[:, 28:32], in1=finc[:, 8:12], op=ALU.mult), R=[T_finc], W=[T_finc])
                for h in range(4):
                    B.op("pool", lambda e, h=h: e.tensor_scalar(
                        out=stg32[:, 1, h * 128:(h + 1) * 128], in0=nums[:, h, :], scalar1=finc[:, 12 + h:13 + h], scalar2=1.0,
                        op0=ALU.mult, op1=ALU.mult), R=[T_s32b[s], T_finc], W=[T_s32[1]])
                B.op("pool", lambda e: e.tensor_tensor(out=stg32[:, 1, :], in0=stg32[:, 1, :], in1=gmB[:], op=ALU.mult),
                     R=[T_s32[1], T_small], W=[T_s32[1]])
                B.op("pool", lambda e: e.tensor_tensor(out=mix[:, tt, 0:512], in0=stg32[:, 1, :], in1=og[:, tt, :], op=ALU.mult),
                     R=[T_s32[1], T_og[tt]], W=[T_mix[tt], T_xin, T_xinC])

            def next_units(n):
                out = []
                for _ in range(n):
                    if pending:
                        out.append(pending.pop(0))
                return out

            st = mmain(0)
            mmain2(*st)
            for tt in range(1, NT + 1):
                units = next_units(2)
                st = mmain(tt) if tt < NT else None
                hs = [p2b_proj(*u) for u in units]
                if st is not None:
                    mmain2(*st)
                vd_tile(tt - 1)
                for hnd in hs:
                    p2b_fin(*hnd)
                mfin(tt - 1)
                if (tt - 1) % 4 == 3:
                    pending.extend((cc, (tt - 1) // 4) for cc in range(8))
            while pending:
                units = next_units(2)
                hs = [p2b_proj(*u) for u in units]
                for hnd in hs:
                    p2b_fin(*hnd)
            if DEBUG == "hm":
                dbg_dump(mix, T_mix)
            if DEBUG == "qkd":
                dbg_dump(qkT, [t for row in T_qk for t in row])

            numsb = stg32b[:].rearrange("p a (q e) -> p (a q) e", q=4)
            nums_q = numsb.rearrange("p (c q) e -> p q c e", c=2)
            dens_q = finc[:, 32:40].rearrange("p (c q) -> p q c", c=2)
            blk = {"i": 0}
            gn_pending = []
            fin2 = stg32[:, 1, :].rearrange("p (q e) -> p q e", q=4)
            T_fin2 = T_s32[1]
            for h in range(4):
                for qb in range(NB):
                    nkt = 4 * qb + 4
                    if blk["i"] < 8:
                        adaln_piece_late(4 + blk["i"], 3)
                    blk["i"] += 1

                    def scores(kt, h=h, qb=qb):
                        c0 = max(0, kt * 128 - qb * 512)
                        diagk = kt >= 4 * qb
                        for c in range(2):
                            rows = slice(c * 64, (c + 1) * 64)
                            bank = (kt % 2) * 2 + c
                            B.op("pe", lambda e, bank=bank, c0=c0, rows=rows, kt=kt: e.matmul(
                                ps[:, bank, c0:512], lhsT=qkT[rows, 4 + h, kt * 128:(kt + 1) * 128], rhs=qkT[rows, h, qb * 512 + c0:(qb + 1) * 512],
                                start=True, stop=not diagk),
                                R=[T_qk[4 + h][kt // 4], T_qk[h][qb]], W=[PSB[bank]], inc=not diagk)
                        if diagk:
                            for c in range(2):
                                rows = slice(c * 64, (c + 1) * 64)
                                bank = (kt % 2) * 2 + c
                                for hh in range(2):
                                    B.op("pe", lambda e, bank=bank, c0=c0, rows=rows, hh=hh: e.matmul(
                                        ps[:, bank, c0:c0 + 128], lhsT=cmask[rows, hh, :], rhs=cmask[rows, 2 + hh, :],
                                        start=False, stop=(hh == 1)),
                                        R=[T_cst], W=[PSB[bank]], inc=(hh == 1))

                    def exps(kt, h=h, qb=qb):
                        c0 = max(0, kt * 128 - qb * 512)
                        for c in range(2):
                            bank = (kt % 2) * 2 + c
                            B.op("act", lambda e, bank=bank, c0=c0: e.activation(out=stg[:, bank, c0:512], in_=ps[:, bank, c0:512], func=AF.Exp, scale=0.125),
                                 R=[PSB[bank]], W=[T_stg[bank]])

                    def pv(kt, h=h, qb=qb):
                        for c in range(2):
                            bank = (kt % 2) * 2 + c
                            for qi in range(4):
                                qt = 4 * qb + qi
                                if qt < kt:
                                    continue
                                B.op("pe", lambda e, c=c, qi=qi, bank=bank, kt=kt: e.matmul(
                                    ps[:, 4 + qi, c * 256:c * 256 + 129], lhsT=stg[:, bank, qi * 128:(qi + 1) * 128], rhs=vaug[:, kt, h, :],
                                    start=(kt == 0 and c == 0), stop=(kt == 4 * qb + qi), skip_group_check=True),
                                    R=[T_stg[bank], T_vaug[kt]], W=[PSB[4 + qi]], inc=True)
                        if kt >= 4 * qb:
                            qi = kt - 4 * qb
                            pview = ps[:, 4 + qi, :].rearrange("p (c x) -> p c x", c=2)
                            B.op("dve", lambda e, qi=qi, pview=pview: e.tensor_copy(out=nums_q[:, qi], in_=pview[:, :, 0:128]),
                                 R=[PSB[4 + qi]], W=T_s32b)
                            B.op("dve", lambda e, qi=qi, pview=pview: e.tensor_copy(out=dens_q[:, qi], in_=pview[:, :, 128]),
                                 R=[PSB[4 + qi]], W=[T_finc])

                    scores(0)
                    for kt in range(nkt):
                        if kt + 1 < nkt:
                            scores(kt + 1)
                        exps(kt)
                        if kt == 1 and gn_pending:
                            gn_pending[0][0]()
                        if kt == 3 and gn_pending:
                            gn_pending.pop(0)[1]()
                        pv(kt)
                    B.op("dve", lambda e: e.reciprocal(out=finc[:, 40:48], in_=finc[:, 32:40]), R=[T_finc], W=[T_finc])
                    B.op("dve", lambda e: e.tensor_scalar(out=finc[:, 48:52], in0=finc[:, 44:48], scalar1=nlamc, scalar2=None, op0=ALU.mult),
                         R=[T_finc, T_small], W=[T_finc])
                    for qi in range(4):
                        B.op("dve", lambda e, qi=qi: e.tensor_scalar(out=fin[:, qi, :], in0=numsb[:, qi, :], scalar1=finc[:, 40 + qi:41 + qi],
                                                                   scalar2=None, op0=ALU.mult), R=T_s32b + [T_finc], W=[T_fin])
                    for qi in range(4):
                        B.op("dve", lambda e, qi=qi: e.scalar_tensor_tensor(out=fin[:, qi, :], in0=numsb[:, 4 + qi, :], scalar=finc[:, 48 + qi:49 + qi],
                                                                          in1=fin[:, qi, :], op0=ALU.mult, op1=ALU.add),
                             R=T_s32b + [T_finc, T_fin], W=[T_fin])
                    def _gnA():
                        sq = stg32[:, 0, :].rearrange("p (h e) -> p h e", h=4)
                        B.op("dve", lambda e: e.tensor_tensor(out=sq, in0=fin[:], in1=fin[:], op=ALU.mult), R=[T_fin], W=[T_s32[0]])
                        B.op("dve", lambda e: e.tensor_reduce(out=finc[:, 16:20], in_=sq, axis=AX.X, op=ALU.add), R=[T_s32[0]], W=[T_finc])

                    def _gnB(h=h, qb=qb):
                        gB = gdB[:, h * 128:(h + 1) * 128]
                        B.op("act", lambda e: e.activation(out=finc[:, 20:24], in_=finc[:, 16:20], func=AF.Ln, bias=epsc, scale=1.0 / 128.0),
                             R=[T_finc, T_small], W=[T_finc])
                        B.op("act", lambda e: e.activation(out=finc[:, 24:28], in_=finc[:, 20:24], func=AF.Exp, scale=-0.5), R=[T_finc], W=[T_finc])
                        for qi in range(4):
                            B.op("dve", lambda e, qi=qi: e.scalar_tensor_tensor(
                                out=mix[:, 4 * qb + qi, 512 + h * 128:512 + (h + 1) * 128], in0=fin[:, qi, :], scalar=finc[:, 24 + qi:25 + qi], in1=gB,
                                op0=ALU.mult, op1=ALU.mult), R=[T_fin, T_finc, T_small],
                                W=[T_mix[4 * qb + qi], T_og[4 * qb + qi], T_xin, T_xinC])
                    gn_pending.append((_gnA, _gnB))
            while gn_pending:
                a_, b_ = gn_pending.pop(0)
                a_()
                b_()
            B.op("dve", lambda e: e.scalar_tensor_tensor(out=A2, in0=sc_f, scalar=1.0, in1=gcols[:, 8:16], op0=ALU.add, op1=ALU.mult),
                 R=[T_small, T_mod2], W=[T_mod2])
            if DEBUG == "mix":
                dbg_dump(mix, T_mix)

            for tt in range(NT):
                tb = tt // 4
                pbk = nextbank()
                psk = ps[:, pbk, :].bitcast(BF16)
                for fc in range(8):
                    B.op("pe", lambda e, fc=fc, psk=psk, tt=tt: e.transpose(psk[:, fc * 128:(fc + 1) * 128], mix[:, tt, fc * 128:(fc + 1) * 128], ident_b),
                         R=[T_mix[tt], T_cst], W=[PSB[pbk]], inc=(fc == 7))
                if tt % 2 == 0:
                    B.op("act", lambda e, psk=psk, tt=tt: e.activation(out=mixT[:, :, tt * 128:(tt + 1) * 128], in_=psk.rearrange("p (c t) -> p c t", c=8), func=AF.Copy),
                         R=[PSB[pbk]], W=[T_mixT[tb]] + T_hT)
                else:
                    B.op("dve", lambda e, psk=psk, tt=tt: e.tensor_copy(out=mixT[:, :, tt * 128:(tt + 1) * 128], in_=psk.rearrange("p (c t) -> p c t", c=8)),
                         R=[PSB[pbk]], W=[T_mixT[tb]] + T_hT)
            s_w0 = wget()
            s_w1 = wget(la=2)
            first_x1 = True
            early_mod = []
            sq8 = [(stg[:, i, 0:512], T_stg[i]) for i in range(4)] + [(et[:, i, :], T_et[i]) for i in range(3)] + [(kwt[:, 0, :], T_kwt[0])]
            for tb in range(NB):
                sl = slice(tb * 512, (tb + 1) * 512)
                xb, Tx = (xin, T_xin) if tb % 2 == 0 else (xinC, T_xinC)
                B.dma("sp", xb, xT_d[:, :, sl], W=ALL_MIX)
                if tb in (1, 2):
                    k0 = tb - 1
                    for fc in range(8):
                        qa, qT = sq8[fc]
                        srcc = x1T[:, fc, k0 * 512:(k0 + 1) * 512]
                        if fc % 2 == 0:
                            B.op("act", lambda e, qa=qa, srcc=srcc: e.activation(out=qa, in_=srcc, func=AF.Square), R=[T_x1[fc][k0]], W=[qT])
                        else:
                            B.op("dve", lambda e, qa=qa, srcc=srcc: e.tensor_tensor(out=qa, in0=srcc, in1=srcc, op=ALU.mult), R=[T_x1[fc][k0]], W=[qT])
                for fo in range(8):
                    s = s_w0 if fo < 4 else s_w1
                    wv = slot_view(s, 8, 512)
                    pb = nextbank()
                    for kc in range(8):
                        B.op("pe", lambda e, kc=kc, pb=pb, wv=wv, fo=fo, sl=sl: e.matmul(
                            ps[:, pb, :], lhsT=wv[:, kc, (fo % 4) * 128:(fo % 4 + 1) * 128], rhs=mixT[:, kc, sl], start=(kc == 0), stop=(kc == 7)),
                            R=[T_mixT[tb], RING_T[s]], W=[PSB[pb]], inc=(kc == 7))
                    Wl = [T_x1[fo][tb]] + (ALL_RC if first_x1 else [])
                    first_x1 = False
                    B.op("dve", lambda e, pb=pb, fo=fo, sl=sl, xb=xb: e.scalar_tensor_tensor(
                        out=x1T[:, fo, sl], in0=ps[:, pb, :], scalar=gt_a[:, fo:fo + 1], in1=xb[:, fo, :], op0=ALU.mult, op1=ALU.add),
                        R=[PSB[pb], T_mod2, Tx], W=Wl)
                if tb in (1, 2):
                    k0 = tb - 1
                    early_mod.append(norm_block(x1T[:, :, k0 * 512:(k0 + 1) * 512], [T_x1[c][k0] for c in range(8)],
                                                h2T[:, :, k0 * 512:(k0 + 1) * 512], {"common": T_mixT + T_hT, "per_fc": T_h2[k0]}, A2, sh_f, 512, k0,
                                                T_small=T_mod2, defer=True, presq=sq8))
            if DEBUG == "x1":
                dbg_dump(x1T, [t for row in T_x1 for t in row])

            def norm2(p, sbk):
                tb = p * 2 + sbk
                sl = slice(tb * 512, (tb + 1) * 512)
                norm_block(x1T[:, :, sl], [T_x1[c][tb] for c in range(8)], h2T[:, :, sbk * 512:(sbk + 1) * 512],
                           {"common": T_mixT + T_hT, "per_fc": T_h2[sbk]}, A2, sh_f, 512, sbk, T_small=T_mod2)

            ostg = RA[:, 14336:16384].bitcast(F32).rearrange("p (a t) -> p a t", a=2)
            T_ostg = [T("ostg0"), T("ostg1")]
            sqb = [et[:, 0, :], et[:, 1, :], et[:, 2, :], kwt[:, 0, :]]
            T_sqb = [T_et[0], T_et[1], T_et[2], T_kwt[0]]
            SIDE_BANK = 7

            def staged_norm(tb, mode, sbk=None):
                sl = slice(tb * 512, (tb + 1) * 512)
                srcb = x1T[:, :, sl]
                Tsrc = [T_x1[c][tb] for c in range(8)]
                rs = stg32b[:, 0, :]

                def squares(f0):
                    for fc in range(f0, f0 + 4):
                        j = fc % 4
                        if fc % 2 == 0:
                            B.op("act", lambda e, fc=fc, j=j: e.activation(out=sqb[j], in_=srcb[:, fc, :], func=AF.Square), R=Tsrc, W=[T_sqb[j]])
                        else:
                            B.op("dve", lambda e, fc=fc, j=j: e.tensor_tensor(out=sqb[j], in0=srcb[:, fc, :], in1=srcb[:, fc, :], op=ALU.mult), R=Tsrc, W=[T_sqb[j]])

                def mms(f0):
                    for fc in range(f0, f0 + 4):
                        j = fc % 4
                        B.op("pe", lambda e, fc=fc, j=j: e.matmul(ps[:, SIDE_BANK, :], lhsT=ones_b, rhs=sqb[j], start=(fc == 0), stop=(fc == 7)),
                             R=[T_sqb[j], T_cst], W=[PSB[SIDE_BANK]], inc=True)

                squares(0)
                yield
                mms(0)
                squares(4)
                yield
                mms(4)
                B.op("act", lambda e: e.activation(out=rs, in_=ps[:, SIDE_BANK, :], func=AF.Ln, bias=epsc, scale=1.0 / D), R=[PSB[SIDE_BANK], T_small], W=[T_s32b[0]])
                B.op("act", lambda e: e.activation(out=rs, in_=rs, func=AF.Exp, scale=-0.5), R=[T_s32b[0]], W=[T_s32b[0]])
                yield
                for fc in range(8):
                    if mode == "out":
                        oj = fc % 2
                        B.op("dve", lambda e, fc=fc, oj=oj: e.scalar_tensor_tensor(
                            out=ostg[:, oj, :], in0=srcb[:, fc, :], scalar=g_fin[:, fc:fc + 1], in1=rs, op0=ALU.mult, op1=ALU.mult),
                            R=Tsrc + [T_s32b[0], T_small], W=[T_ostg[oj]])
                        B.dma("sp", out_d[:, fc, sl], ostg[:, oj, :], R=[T_ostg[oj]], is_out=True)
                        if fc < 7:
                            yield
                    else:
                        tmp = stg32b[:, 1, :]
                        B.op("dve", lambda e, fc=fc: e.scalar_tensor_tensor(
                            out=tmp, in0=srcb[:, fc, :], scalar=A2[:, fc:fc + 1], in1=rs, op0=ALU.mult, op1=ALU.mult),
                            R=Tsrc + [T_s32b[0], T_mod2], W=[T_s32b[1]])
                        B.op("act", lambda e, fc=fc: e.activation(out=h2T[:, fc, sbk * 512:(sbk + 1) * 512], in_=tmp, func=AF.Identity,
                                                                 bias=sh_f[:, fc:fc + 1], scale=1.0),
                             R=[T_s32b[1], T_mod2], W=[T_h2[sbk][fc]])
                yield

            side = {"gens": []}

            def side_step():
                while side["gens"]:
                    try:
                        next(side["gens"][0])
                        return
                    except StopIteration:
                        side["gens"].pop(0)

            def final_big(tb):
                sl = slice(tb * 512, (tb + 1) * 512)
                ob = tb % 2
                T_oc = [T("oc%d" % fc) for fc in range(8)]
                modf = norm_block(x1T[:, :, sl], [T_x1[c][tb] for c in range(8)], outb[ob], [T_outb[ob]] + [t for row in T_act[0:16] for t in row],
                                  g_fin, None, 512, tb % 2, defer=True)
                rs_ = stg32[:, tb % 2, 0:512]
                first = True
                for fc in range(8):
                    B.op("dve", lambda e, fc=fc: e.scalar_tensor_tensor(
                        out=outb[ob][:, fc, :], in0=x1T[:, fc, sl], scalar=g_fin[:, fc:fc + 1], in1=rs_, op0=ALU.mult, op1=ALU.mult),
                        R=[T_x1[fc][tb], T_s32[tb % 2], T_small], W=[T_oc[fc]] + ([T_outb[ob]] + [t for row in T_act[0:16] for t in row] if first else []))
                    first = False
                    B.dma("sp", out_d[:, fc, sl], outb[ob][:, fc, :], R=[T_oc[fc]], is_out=True)

            T_outb = [T_xin, T_xinC]
            outb = [RBm[:, 0:8192].bitcast(F32).rearrange("p (c t) -> p c t", c=8),
                    RBm[:, 8192:16384].bitcast(F32).rearrange("p (c t) -> p c t", c=8)]

            for m_ in early_mod:
                m_()
            for p in range(2):
                for grp in range(11):
                    if p == 1 and grp == 1:
                        side["gens"] += [staged_norm(0, "out"), staged_norm(1, "out")]
                    s = wget()
                    wv = slot_view(s, 8, 512)
                    for pr in range(2):
                        i = grp * 2 + pr
                        if p == 1 and grp >= 1:
                            side_step()
                        chunks = (i, 22 + i)
                        dsets = [build_diag(cffn[:, ch, 0:3], 3, T_small) for ch in chunks]
                        for wi, ch in enumerate(chunks):
                            bi = (i % 2) * 2 + wi
                            B.op("dve", lambda e, bi=bi, ch=ch: e.tensor_copy(out=stg[:, bi, 0:2], in_=halo[:, ch, :]), R=[T_halo], W=[T_stg[bi]])
                            for sbk in range(2):
                                pb = nextbank(0, 7)
                                for kc in range(8):
                                    B.op("pe", lambda e, kc=kc, pb=pb, wv=wv, wi=wi, pr=pr, sbk=sbk: e.matmul(
                                        ps[:, pb, :], lhsT=wv[:, kc, wi * 256 + pr * 128:wi * 256 + (pr + 1) * 128],
                                        rhs=h2T[:, kc, sbk * 512:(sbk + 1) * 512], start=(kc == 0), stop=(kc == 7)),
                                        R=[T_h2[sbk][kc], RING_T[s]], W=[PSB[pb]], inc=(kc == 7))
                                B.op("act", lambda e, bi=bi, pb=pb, sbk=sbk: e.activation(out=stg[:, bi, 2 + sbk * 512:2 + (sbk + 1) * 512], in_=ps[:, pb, :], func=AF.Copy),
                                     R=[PSB[pb]], W=[T_stg[bi]])
                            B.op("dve", lambda e, bi=bi, ch=ch: e.tensor_copy(out=halo[:, ch, :], in_=stg[:, bi, 1024:1026]), R=[T_stg[bi]], W=[T_halo])
                        ba, bg = (i % 2) * 2, (i % 2) * 2 + 1
                        pa2s, pg2s = [], []
                        for sbk in range(2):
                            pa2 = nextbank(0, 7)
                            pa2s.append(pa2)
                            for j in range(3):
                                B.op("pe", lambda e, j=j, pa2=pa2, sbk=sbk, ba=ba, d=dsets[0]: e.matmul(
                                    ps[:, pa2, :], lhsT=diag[:, d * 4 + j, :], rhs=stg[:, ba, sbk * 512 + j:sbk * 512 + j + 512], start=(j == 0), stop=(j == 2)),
                                    R=[T_stg[ba], T_diag[dsets[0]]], W=[PSB[pa2]], inc=(j == 2))
                        for sbk in range(2):
                            pg2 = nextbank(0, 7)
                            pg2s.append(pg2)
                            for j in range(3):
                                B.op("pe", lambda e, j=j, pg2=pg2, sbk=sbk, bg=bg, d=dsets[1]: e.matmul(
                                    ps[:, pg2, :], lhsT=diag[:, d * 4 + j, :], rhs=stg[:, bg, sbk * 512 + j:sbk * 512 + j + 512], start=(j == 0), stop=(j == 2)),
                                    R=[T_stg[bg], T_diag[dsets[1]]], W=[PSB[pg2]], inc=(j == 2))
                        for sbk in range(2):
                            pa2, pg2, si = pa2s[sbk], pg2s[sbk], sbk
                            B.op("act", lambda e, pg2=pg2, si=si, i=i: e.activation(out=stg32[:, si, :], in_=ps[:, pg2, :], func=AF.Silu, bias=cffn[:, 22 + i, 3:4], scale=1.0),
                                 R=[PSB[pg2], T_small], W=[T_s32[si]])
                            B.op("dve", lambda e, pa2=pa2, si=si, i=i, sbk=sbk: e.scalar_tensor_tensor(
                                out=actT(i)[:, sbk * 512:(sbk + 1) * 512], in0=ps[:, pa2, :], scalar=cffn[:, i, 3:4], in1=stg32[:, si, :], op0=ALU.add, op1=ALU.mult),
                                R=[PSB[pa2], T_s32[si], T_small], W=[T_act[i][sbk]] + (ALL_MIX if i < 16 else T_mixT + T_hT))
                def down(fo, sbks, s, hi=7):
                    wd = ring[:, s, 0:2816].rearrange("p (k n) -> p k n", k=22)
                    for sbk in sbks:
                        tb = p * 2 + sbk
                        sl = slice(tb * 512, (tb + 1) * 512)
                        pb = nextbank(0, hi)
                        for kc in range(22):
                            B.op("pe", lambda e, kc=kc, pb=pb, wd=wd, sbk=sbk: e.matmul(
                                ps[:, pb, :], lhsT=wd[:, kc, :], rhs=actT(kc)[:, sbk * 512:(sbk + 1) * 512], start=(kc == 0), stop=(kc == 21)),
                                R=[T_act[kc][sbk], RING_T[s]], W=[PSB[pb]], inc=(kc == 21))
                        B.op("dve", lambda e, pb=pb, fo=fo, sl=sl, tb=tb: e.scalar_tensor_tensor(
                            out=x1T[:, fo, sl], in0=ps[:, pb, :], scalar=gt_f[:, fo:fo + 1], in1=x1T[:, fo, sl], op0=ALU.mult, op1=ALU.add),
                            R=[PSB[pb], T_mod2, T_x1[fo][tb]], W=[T_x1[fo][tb]])

                if p == 0:
                    side["gens"] += [staged_norm(2, "ffn", 0), staged_norm(3, "ffn", 1)]
                    side_step()
                    for fo in range(8):
                        if fo < 7:
                            side_step()
                        down(fo, (0, 1), wget())
                else:
                    sl3 = slice(3 * 512, 4 * 512)

                    def last_sq(fo):
                        j = fo % 4
                        if fo % 2 == 0:
                            B.op("act", lambda e, fo=fo, j=j: e.activation(out=stg[:, j, 0:512], in_=x1T[:, fo, sl3], func=AF.Square), R=[T_x1[fo][3]], W=[T_stg[j]])
                        else:
                            B.op("dve", lambda e, fo=fo, j=j: e.tensor_tensor(out=stg[:, j, 0:512], in0=x1T[:, fo, sl3], in1=x1T[:, fo, sl3], op=ALU.mult), R=[T_x1[fo][3]], W=[T_stg[j]])

                    def last_mm(fo):
                        j = fo % 4
                        B.op("pe", lambda e, fo=fo, j=j: e.matmul(ps[:, 6, :], lhsT=ones_b, rhs=stg[:, j, 0:512], start=(fo == 0), stop=(fo == 7)),
                             R=[T_stg[j], T_cst], W=[PSB[6]], inc=True)

                    for sbk in range(2):
                        for fo in range(8):
                            down(fo, (sbk,), wget(), hi=(6 if sbk == 1 else 7))
                            if sbk == 1:
                                if fo == 0:
                                    side["gens"] += [staged_norm(2, "out")]
                                side_step()
                                if fo >= 1:
                                    last_mm(fo - 1)
                                last_sq(fo)
                    last_mm(7)
            if DEBUG == "x2":
                dbg_dump(x1T, [t for row in T_x1 for t in row])

            while side["gens"]:
                side_step()
            rs3 = stg32[:, 1, :]
            B.op("act", lambda e: e.activation(out=rs3, in_=ps[:, 6, :], func=AF.Ln, bias=epsc, scale=1.0 / D), R=[PSB[6], T_small], W=[T_s32[1]])
            B.op("act", lambda e: e.activation(out=rs3, in_=rs3, func=AF.Exp, scale=-0.5), R=[T_s32[1]], W=[T_s32[1]])
            T_oc = [T("oc%d" % fc) for fc in range(8)]
            for fc in range(8):
                B.op("dve", lambda e, fc=fc: e.scalar_tensor_tensor(
                    out=outb[1][:, fc, :], in0=x1T[:, fc, 3 * 512:4 * 512], scalar=g_fin[:, fc:fc + 1], in1=rs3, op0=ALU.mult, op1=ALU.mult),
                    R=[T_x1[fc][3], T_s32[1], T_small], W=[T_oc[fc]] + ([T_outb[1]] + [t for row in T_act[0:16] for t in row] if fc == 0 else []))
                B.dma("sp", out_d[:, fc, 3 * 512:4 * 512], outb[1][:, fc, :], R=[T_oc[fc]], is_out=True)

        try:
            body()
        except _Stop:
            pass
        B.finish()
        block = es.enter_context(nc.Block())
        B.emit(block)
    return nc


def _chunk_rows(w, kc):
    n = w.shape[1]
    return np.ascontiguousarray(w.reshape(kc, 128, n).transpose(1, 0, 2))


def _consts():
    cst = np.zeros((128, 5, 128), np.float32)
    cst[:, 0, :] = np.eye(128, dtype=np.float32)
    cst[:, 1, :] = np.triu(np.ones((128, 128), np.float32))
    cst[:, 2, :] = 1.0
    p = np.arange(128)
    perm = np.zeros((128, 128), np.float32)
    perm[p ^ 32, p] = 1.0
    cst[:, 3, :] = perm
    half = 32
    inv_freq = (np.float32(10000.0) ** (-np.arange(half, dtype=np.float32) / np.float32(half))).astype(np.float32)
    cst[:, 4, 0] = inv_freq[p % 32]
    cst[:, 4, 1] = np.where((p % 64) < 32, 1.0, -1.0)
    return cst


_NC_CACHE = {}


def kernel(x, c, positions, w_ada, b_ada, g_mix, w_in, conv_qk_w, conv_qk_b, b_if, g_mlstm,
           lam_q1, lam_k1, lam_q2, lam_k2, g_diff, w_out, g_ffn, w_up, conv_ffn_w, conv_ffn_b,
           w_down, g_final):
    f32 = lambda a: np.asarray(a, dtype=np.float32)
    x, c = f32(x), f32(c)
    positions = np.asarray(positions, dtype=np.int32)
    w_ada, b_ada, w_in, w_out, w_up, w_down = f32(w_ada)[0], f32(b_ada)[0], f32(w_in)[0], f32(w_out)[0], f32(w_up)[0], f32(w_down)[0]
    nb = x.shape[0]

    def pieces(w, cols_list):
        out = []
        for cols in cols_list:
            out.append(_chunk_rows(w[:, cols], 8).reshape(128, 4096))
        return np.ascontiguousarray(np.stack(out))

    wada_p = pieces(w_ada, [np.arange(i * 512, (i + 1) * 512) for i in range(12)])
    win_cols = [np.arange(0, 512), np.arange(512, 1024), np.arange(1024, 1536), np.arange(1536, 2048),
                np.arange(2056, 2568), np.arange(2568, 3080), np.arange(3080, 3592)]
    win_p = pieces(w_in, win_cols)
    wgate_p = np.ascontiguousarray(_chunk_rows(w_in[:, 2048:2056], 8).reshape(128, 64))
    wout_p = pieces(w_out, [np.arange(0, 512), np.arange(512, 1024)])
    up_cols = []
    for g in range(11):
        a = np.arange(g * 256, (g + 1) * 256)
        up_cols.append(np.concatenate([a, 2816 + a]))
    wup_p = pieces(w_up, up_cols)
    wdown_p = np.ascontiguousarray(np.stack([_chunk_rows(w_down[:, f * 128:(f + 1) * 128], 22).reshape(128, 2816) for f in range(8)]))
    col8 = lambda v: np.ascontiguousarray(f32(v).reshape(-1, 128).T)
    bada_p = col8(b_ada)
    gcols = np.ascontiguousarray(np.concatenate([col8(f32(g_mix)[0]), col8(f32(g_ffn)[0]), col8(f32(g_final))], axis=1))
    cqk = np.ascontiguousarray(np.concatenate([f32(conv_qk_w)[0].reshape(4, 8, 128).transpose(2, 1, 0),
                                               f32(conv_qk_b)[0].reshape(8, 128).T[:, :, None]], axis=2))
    cffn = np.ascontiguousarray(np.concatenate([f32(conv_ffn_w)[0].reshape(3, 44, 128).transpose(2, 1, 0),
                                                f32(conv_ffn_b)[0].reshape(44, 128).T[:, :, None]], axis=2))
    bif = np.ascontiguousarray(np.broadcast_to(f32(b_if)[0][None, None, :], (128, 16, 8)))
    gmB = np.ascontiguousarray(np.broadcast_to(f32(g_mlstm)[0][None, :], (128, 512)))
    gdB = np.ascontiguousarray(np.broadcast_to(f32(g_diff)[0][None, :], (128, 512)))
    lam = np.ascontiguousarray(np.broadcast_to(np.stack([f32(lam_q1)[0], f32(lam_k1)[0], f32(lam_q2)[0], f32(lam_k2)[0]])[None], (128, 4, 64)))
    cst = _consts()
    pp = np.arange(128)
    cmk = np.zeros((128, 4, 128), np.float32)
    for hh in range(2):
        cmk[pp, hh, (pp % 64) + 64 * hh] = 1.0
        cmk[:, 2 + hh, :] = np.where(((pp % 64) + 64 * hh)[:, None] > np.arange(128)[None, :], -30000.0, 0.0)

    in_maps = []
    for b in range(nb):
        xT = np.ascontiguousarray(x[b].T.reshape(8, 128, S).transpose(1, 0, 2))
        in_maps.append({
            "xT": xT, "c": np.ascontiguousarray(c[b].reshape(8, 128).T),
            "pos": np.ascontiguousarray(np.broadcast_to(positions[b][None, :], (128, S))),
            "w_ada": wada_p, "b_ada": bada_p, "gcols": gcols, "w_in": win_p, "w_gate": wgate_p, "cqk": cqk, "bif": bif,
            "gmB": gmB, "gdB": gdB, "lam": lam, "w_out": wout_p, "w_up": wup_p, "cffn": cffn, "w_down": wdown_p, "consts": cst, "cmask": cmk,
        })
    if "nc" not in _NC_CACHE:
        _NC_CACHE["nc"] = build_program()
    nc = _NC_CACHE["nc"]
    res = run_bass_kernel_spmd(nc, in_maps, core_ids=list(range(nb)))
    outs = []
    for b in range(nb):
        oT = np.asarray(res.results[b]["outT"], dtype=np.float32)
        outs.append(oT.transpose(1, 0, 2).reshape(D, S).T)
    out = np.ascontiguousarray(np.stack(outs)).astype(np.float32)
    if DEBUG:
        kernel.dbg = [np.asarray(res.results[b]["dbg"]) for b in range(nb)]
    return out
```
